# Optimizing a Trainium2 kernel written in Bass

```python
import math
import jax
import jax.numpy as jnp
from jax import lax
import numpy as np

D_MODEL = 1024
BATCH = 8
SEQ = 4096
DEPTH = 1

GRID_W = 64
CTX_LEN = 256
DN_HEADS = 4
DN_HEAD_DIM = 128
DN_WIDTH = DN_HEADS * DN_HEAD_DIM
SHORT_CONV = 5
CHUNK = 64
POOL_GROUPS = 4
POOL_WINDOWS = (2, 4, 8, 16)
POOL_WIDTH = D_MODEL - DN_WIDTH
POOL_GROUP_DIM = POOL_WIDTH // POOL_GROUPS
MIX_WIDTH = DN_WIDTH + POOL_WIDTH
D_FF = 128 * ((8 * D_MODEL // 3 + 127) // 128)
FFN_CONV = 3
N_MOD = 6
NORM_EPS = 1e-6
Q_OFF = 0
K_OFF = DN_WIDTH
V_OFF = 2 * DN_WIDTH
Z_OFF = 3 * DN_WIDTH
GATE_OFF = 4 * DN_WIDTH
POOL_OFF = GATE_OFF + 4 * DN_HEADS
IN_COLS = POOL_OFF + POOL_WIDTH

kernel_name = 'hymba_gdn_pool_convffn_dit_block'


def rms_norm(x, g):
    xf = x.astype(jnp.float32)
    y = xf * lax.rsqrt(jnp.mean(xf * xf, axis=-1, keepdims=True) + NORM_EPS)
    return (y * g.astype(jnp.float32)).astype(x.dtype)


def l2_normalize(x):
    xf = x.astype(jnp.float32)
    return xf * lax.rsqrt(jnp.sum(xf * xf, axis=-1, keepdims=True) + NORM_EPS)


def modulation(cond, w_mod, b_mod):
    m = jax.nn.silu(cond) @ w_mod + b_mod
    return jnp.split(m[..., None, :], N_MOD, axis=-1)


def depthwise_conv(x, w):
    pad = w.shape[0] // 2
    return lax.conv_general_dilated(x, w[:, None, :], window_strides=(1,), padding=[(pad, pad)],
                                    dimension_numbers=('NWC', 'WIO', 'NWC'),
                                    feature_group_count=x.shape[-1])


def split_heads(a):
    return a.reshape(*a.shape[:-1], -1, DN_HEAD_DIM)


def gated_delta_rule(q, k, v, g, beta, s0):
    bsz, t, nh, dk = k.shape
    n = t // CHUNK

    def to_chunks(a):
        a = a.astype(jnp.float32).reshape(bsz, n, CHUNK, nh, *a.shape[3:])
        return jnp.moveaxis(a, 3, 1)

    k_c, v_c = to_chunks(k), to_chunks(v)
    gcum = jnp.cumsum(to_chunks(g), axis=-1)
    b_c = to_chunks(beta)[..., None]
    incl = jnp.tril(jnp.ones((CHUNK, CHUNK), dtype=bool))
    strict = jnp.tril(jnp.ones((CHUNK, CHUNK), dtype=bool), k=-1)
    decay = jnp.exp(jnp.where(incl, gcum[..., :, None] - gcum[..., None, :], -jnp.inf))
    kb = k_c * b_c
    lower = jnp.where(strict, jnp.einsum('bhnid,bhnjd->bhnij', kb, k_c) * decay, 0.0)
    eye = jnp.eye(CHUNK, dtype=jnp.float32)
    t_inv = lax.linalg.triangular_solve(lower + eye, jnp.broadcast_to(eye, lower.shape),
                                        left_side=True, lower=True, unit_diagonal=True)
    u_c = jnp.einsum('bhnij,bhnjv->bhniv', t_inv, v_c * b_c)
    w_c = jnp.einsum('bhnij,bhnjk->bhnik', t_inv, kb * jnp.exp(gcum)[..., None])
    g_last = gcum[..., -1]
    k_dec = k_c * jnp.exp(g_last[..., None] - gcum)[..., None]

    def seq_first(a):
        return jnp.moveaxis(a, 2, 0)

    xs = [seq_first(a) for a in (u_c, w_c, k_dec, g_last)]
    with_out = q is not None
    if with_out:
        q_c = to_chunks(q) * dk ** -0.5
        attn = jnp.where(incl, jnp.einsum('bhnik,bhnjk->bhnij', q_c, k_c) * decay, 0.0)
        xs += [seq_first(q_c * jnp.exp(gcum)[..., None]), seq_first(attn)]

    def step(s, xs_n):
        u_n, w_n, kd_n, gl_n = xs_n[:4]
        v_new = u_n - jnp.einsum('bhck,bhkv->bhcv', w_n, s)
        s_new = s * jnp.exp(gl_n)[..., None, None] + jnp.einsum('bhck,bhcv->bhkv', kd_n, v_new)
        if not with_out:
            return s_new, None
        qd_n, a_n = xs_n[4:]
        o_n = (jnp.einsum('bhck,bhkv->bhcv', qd_n, s)
               + jnp.einsum('bhij,bhjv->bhiv', a_n, v_new))
        return s_new, o_n

    s_fin, o = lax.scan(step, s0, tuple(xs))
    if not with_out:
        return None, s_fin
    o = jnp.moveaxis(jnp.moveaxis(o, 0, 2), 1, 3).reshape(bsz, t, nh, -1)
    return o, s_fin


def dn_gates(p_gate, a_log, dt_bias):
    pf = p_gate.astype(jnp.float32)
    b_f, b_b, a_f, a_b = jnp.split(pf, 4, axis=-1)

    def log_decay(a, d):
        return -jnp.exp(a_log[d].astype(jnp.float32)) * jax.nn.softplus(a + dt_bias[d].astype(jnp.float32))

    return (log_decay(a_f, 0), jax.nn.sigmoid(b_f)), (log_decay(a_b, 1), jax.nn.sigmoid(b_b))


def bidirectional_delta(q, k, v, gates, s0_f, s0_b):
    (g_f, beta_f), (g_b, beta_b) = gates

    def flip(a):
        return None if a is None else jnp.flip(a, axis=1)

    o_f, s_f = gated_delta_rule(q, k, v, g_f, beta_f, s0_f)
    o_b, s_b = gated_delta_rule(flip(q), flip(k), flip(v), flip(g_b), flip(beta_b), s0_b)
    o = None if q is None else o_f + flip(o_b)
    return o, s_f, s_b


def box_mean(u, w, axis):
    n = u.shape[axis]
    left = w // 2
    right = w - 1 - left
    cs = jnp.cumsum(u, axis=axis)
    cs = jnp.concatenate([jnp.zeros_like(lax.slice_in_dim(cs, 0, 1, axis=axis)), cs], axis=axis)
    idx = jnp.arange(n)
    lo = jnp.clip(idx - left, 0, n)
    hi = jnp.clip(idx + right + 1, 0, n)
    total = jnp.take(cs, hi, axis=axis) - jnp.take(cs, lo, axis=axis)
    shape = [1] * u.ndim
    shape[axis] = n
    return total / (hi - lo).astype(u.dtype).reshape(shape)


def pool_mixer(u, pool_w, pool_scale, rows):
    bsz, t, _ = u.shape
    if rows is None:
        spatial, axes = (t,), (1,)
    else:
        spatial, axes = (rows, GRID_W), (1, 2)
    uf = u.astype(jnp.float32).reshape(bsz, *spatial, POOL_GROUPS, POOL_GROUP_DIM)
    diffs = []
    for gi, w in enumerate(POOL_WINDOWS):
        ug = uf[..., gi, :]
        m = ug
        for ax in axes:
            m = box_mean(m, w, ax)
        diffs.append(m - ug)
    d = jnp.stack(diffs, axis=-2).reshape(bsz, t, POOL_GROUPS, POOL_GROUP_DIM).astype(u.dtype)
    y = jnp.einsum('btgc,gcd->btgd', d, pool_w).reshape(bsz, t, POOL_WIDTH)
    return y * pool_scale


def token_mixers(h, rows, s0_f, s0_b, w_in, conv_qkv, a_log, dt_bias, o_norm, pool_w, pool_scale, w_out):
    p = h @ w_in
    qkv = jax.nn.silu(depthwise_conv(p[..., Q_OFF:Z_OFF], conv_qkv))
    q, k, v = (split_heads(a) for a in jnp.split(qkv, 3, axis=-1))
    gates = dn_gates(p[..., GATE_OFF:POOL_OFF], a_log, dt_bias)
    o, s_f, s_b = bidirectional_delta(l2_normalize(q), l2_normalize(k), v, gates, s0_f, s0_b)
    z = split_heads(p[..., Z_OFF:GATE_OFF]).astype(jnp.float32)
    o = rms_norm(o, o_norm) * jax.nn.silu(z)
    dn_out = o.reshape(*h.shape[:-1], DN_WIDTH).astype(h.dtype)
    pool_out = pool_mixer(p[..., POOL_OFF:], pool_w, pool_scale, rows)
    y = jnp.concatenate([dn_out, pool_out], axis=-1) @ w_out
    return y, s_f, s_b


def context_states(hc, w_in, conv_qkv, a_log, dt_bias):
    kv = jax.nn.silu(depthwise_conv(hc @ w_in[:, K_OFF:Z_OFF], conv_qkv[:, K_OFF:Z_OFF]))
    k, v = (split_heads(a) for a in jnp.split(kv, 2, axis=-1))
    gates = dn_gates(hc @ w_in[:, GATE_OFF:POOL_OFF], a_log, dt_bias)
    zeros = jnp.zeros((hc.shape[0], DN_HEADS, DN_HEAD_DIM, DN_HEAD_DIM), jnp.float32)
    _, s_f, s_b = bidirectional_delta(None, l2_normalize(k), v, gates, zeros, zeros)
    return s_f, s_b


def conv_glu_ffn(h, w_up, conv_w, w_down):
    gate, up = jnp.split(h @ w_up, 2, axis=-1)
    return (jax.nn.silu(depthwise_conv(gate, conv_w)) * up) @ w_down


def layer(x, ctx, c, c_ctx, rows, update_ctx, w_mod, b_mod, g_pre_mix, g_post_mix, g_pre_ffn,
          g_post_ffn, w_in, conv_qkv, a_log, dt_bias, o_norm, pool_w, pool_scale, w_out,
          w_up, conv_ffn, w_down):
    sh1, sc1, gt1, sh2, sc2, gt2 = modulation(c, w_mod, b_mod)
    csh1, csc1, cgt1, csh2, csc2, cgt2 = modulation(c_ctx, w_mod, b_mod)
    mix_params = (w_in, conv_qkv, a_log, dt_bias, o_norm, pool_w, pool_scale, w_out)
    hx = rms_norm(x, g_pre_mix) * (1 + sc1) + sh1
    hc = rms_norm(ctx, g_pre_mix) * (1 + csc1) + csh1
    if update_ctx:
        zeros = jnp.zeros((ctx.shape[0], DN_HEADS, DN_HEAD_DIM, DN_HEAD_DIM), jnp.float32)
        yc, s_f, s_b = token_mixers(hc, None, zeros, zeros, *mix_params)
    else:
        s_f, s_b = context_states(hc, w_in, conv_qkv, a_log, dt_bias)
    yx, _, _ = token_mixers(hx, rows, s_f, s_b, *mix_params)
    x = x + gt1 * rms_norm(yx, g_post_mix)
    hx = rms_norm(x, g_pre_ffn) * (1 + sc2) + sh2
    x = x + gt2 * rms_norm(conv_glu_ffn(hx, w_up, conv_ffn, w_down), g_post_ffn)
    if update_ctx:
        ctx = ctx + cgt1 * rms_norm(yc, g_post_mix)
        hc = rms_norm(ctx, g_pre_ffn) * (1 + csc2) + csh2
        ctx = ctx + cgt2 * rms_norm(conv_glu_ffn(hc, w_up, conv_ffn, w_down), g_post_ffn)
    return x, ctx


def setup_inputs(seed: int = 0) -> dict:
    key = jax.random.key(seed)
    ks = jax.random.split(key, 21)
    f32 = jnp.float32

    def nrm(k, shape, scale):
        return jax.random.normal(k, shape, f32) * scale

    dt = jnp.exp(jax.random.uniform(ks[13], (DEPTH, 2, DN_HEADS), f32,
                                    minval=math.log(1e-3), maxval=math.log(1e-1)))
    return {
        'x': nrm(ks[0], (BATCH, SEQ, D_MODEL), 1.0),
        'c': nrm(ks[1], (BATCH, D_MODEL), 1.0),
        'ctx': nrm(ks[2], (BATCH, CTX_LEN, D_MODEL), 1.0),
        'c_ctx': nrm(ks[3], (D_MODEL,), 1.0),
        'w_mod': nrm(ks[4], (DEPTH, D_MODEL, N_MOD * D_MODEL), 0.5 * D_MODEL ** -0.5),
        'b_mod': nrm(ks[5], (DEPTH, N_MOD * D_MODEL), 0.02),
        'g_pre_mix': 1.0 + nrm(ks[6], (DEPTH, D_MODEL), 0.05),
        'g_post_mix': 1.0 + nrm(ks[7], (DEPTH, D_MODEL), 0.05),
        'g_pre_ffn': 1.0 + nrm(ks[8], (DEPTH, D_MODEL), 0.05),
        'g_post_ffn': 1.0 + nrm(ks[9], (DEPTH, D_MODEL), 0.05),
        'w_in': nrm(ks[10], (DEPTH, D_MODEL, IN_COLS), D_MODEL ** -0.5),
        'conv_qkv': nrm(ks[11], (DEPTH, SHORT_CONV, 3 * DN_WIDTH), SHORT_CONV ** -0.5),
        'a_log': jnp.log(jax.random.uniform(ks[12], (DEPTH, 2, DN_HEADS), f32, minval=1.0, maxval=16.0)),
        'dt_bias': jnp.log(jnp.expm1(dt)),
        'o_norm': 1.0 + nrm(ks[14], (DEPTH, DN_HEAD_DIM), 0.05),
        'pool_w': nrm(ks[15], (DEPTH, POOL_GROUPS, POOL_GROUP_DIM, POOL_GROUP_DIM), POOL_GROUP_DIM ** -0.5),
        'pool_scale': 1.0 + nrm(ks[16], (DEPTH, POOL_WIDTH), 0.05),
        'w_out': nrm(ks[17], (DEPTH, MIX_WIDTH, D_MODEL), MIX_WIDTH ** -0.5),
        'w_up': nrm(ks[18], (DEPTH, D_MODEL, 2 * D_FF), D_MODEL ** -0.5),
        'conv_ffn': nrm(ks[19], (DEPTH, FFN_CONV, D_FF), FFN_CONV ** -0.5),
        'w_down': nrm(ks[20], (DEPTH, D_FF, D_MODEL), D_FF ** -0.5),
    }


def reference(x, c, ctx, c_ctx, w_mod, b_mod, g_pre_mix, g_post_mix, g_pre_ffn, g_post_ffn,
              w_in, conv_qkv, a_log, dt_bias, o_norm, pool_w, pool_scale, w_out, w_up,
              conv_ffn, w_down):
    rows = x.shape[1] // GRID_W
    for i in range(DEPTH):
        x, ctx = layer(x, ctx, c, c_ctx, rows, i < DEPTH - 1,
                       w_mod[i], b_mod[i], g_pre_mix[i], g_post_mix[i], g_pre_ffn[i],
                       g_post_ffn[i], w_in[i], conv_qkv[i], a_log[i], dt_bias[i], o_norm[i],
                       pool_w[i], pool_scale[i], w_out[i], w_up[i], conv_ffn[i], w_down[i])
    return x
```

```python
import contextlib
import numpy as np
import concourse.bass as bass
import concourse.mybir as mybir
from concourse.bass_utils import run_bass_kernel_spmd

F32 = mybir.dt.float32
BF16 = mybir.dt.bfloat16
AF = mybir.ActivationFunctionType
ALU = mybir.AluOpType

D = 1024
T = 4096
TC = 256
INC = 2576
DFF = 2816
NFC = 22
CH = 128
EPS = 1e-6
DEBUG = False
NCH_DEBUG = 0


class Buf:
    __slots__ = ("name", "w", "r")

    def __init__(self, name=""):
        self.name = name
        self.w = None
        self.r = []


class TL:
    __slots__ = ("ap", "b")

    def __init__(self, ap, name=""):
        self.ap = ap
        self.b = Buf(name)


class Prog:
    ENGS = ("pe", "dve", "act", "pool", "sp")

    def __init__(self, nc, n_dma_sems=16):
        self.nc = nc
        self.stack = contextlib.ExitStack()
        self.ops = {e: [] for e in self.ENGS}
        self.sems = {}
        self.n_dma = n_dma_sems
        self.gen = -1
        self.n_inst = 0
        self._new_gen()

    def _new_gen(self):
        self.gen += 1
        g = self.gen
        self.count = {e: 0 for e in self.ENGS}
        old_seen = getattr(self, "seen", None)
        self.seen = {e: {} for e in self.ENGS}
        for e in ("pe", "dve", "act", "pool"):
            self.sems[(e, g)] = self.stack.enter_context(self.nc.semaphore("s_%s_%d" % (e, g)))
        if g == 0:
            self.dma_cnt = {}
            self.dma_rr = {}
            self.dma_n = {"sp": self.n_dma, "act": 8, "pool": 8}
            for q, nq in self.dma_n.items():
                self.dma_rr[q] = 0
                for k in range(nq):
                    key = ("d_%s%d" % (q, k), -1)
                    self.sems[key] = self.stack.enter_context(self.nc.semaphore("d_%s%d" % (q, k)))
                    self.dma_cnt[key] = 0
        else:
            for e in self.ENGS:
                for key, v in old_seen[e].items():
                    if key[1] == -1:
                        self.seen[e][key] = v

    def sbuf(self, name, shape, dtype):
        return self.stack.enter_context(self.nc.sbuf_tensor(name, list(shape), dtype))

    def psum(self, name, shape, dtype):
        return self.stack.enter_context(self.nc.psum_tensor(name, list(shape), dtype))

    def _deps(self, eng, reads, writes):
        deps = {}

        def add(tok):
            if tok is None:
                return
            k, v = tok
            if k[1] != self.gen and k[1] != -1:
                return
            if deps.get(k, 0) < v:
                deps[k] = v
        for b in reads:
            add(b.w)
        for b in writes:
            add(b.w)
            for t in b.r:
                add(t)
        waits = []
        seen = self.seen[eng]
        for k, v in deps.items():
            if seen.get(k, 0) < v:
                seen[k] = v
                waits.append((k, v))
        return waits

    def _commit(self, tok, reads, writes):
        for b in reads:
            b.r.append(tok)
            if len(b.r) > 64:
                best = {}
                for k, v in b.r:
                    if best.get(k, 0) < v:
                        best[k] = v
                b.r = list(best.items())
        for b in writes:
            b.w = tok
            b.r = []

    def op(self, eng, fn, reads=(), writes=()):
        reads = [t.b if isinstance(t, TL) else t for t in reads]
        writes = [t.b if isinstance(t, TL) else t for t in writes]
        waits = self._deps(eng, reads, writes)
        self.count[eng] += 1
        tok = ((eng, self.gen), self.count[eng])
        self.ops[eng].append((waits, fn, ((eng, self.gen), 1)))
        self._commit(tok, reads, writes)
        self.n_inst += 1
        return tok

    def dma(self, q, out, in_, reads=(), writes=(), **kw):
        reads = [t.b if isinstance(t, TL) else t for t in reads]
        writes = [t.b if isinstance(t, TL) else t for t in writes]
        k = self.dma_rr[q]
        self.dma_rr[q] = (k + 1) % self.dma_n[q]
        key = ("d_%s%d" % (q, k), -1)
        waits = self._deps(q, reads, writes)
        prev = self.dma_cnt[key]
        if prev > 0 and self.seen[q].get(key, 0) < prev:
            self.seen[q][key] = prev
            waits.append((key, prev))
        self.dma_cnt[key] = prev + 16
        tok = (key, prev + 16)

        def fn(e, out=out, in_=in_, kw=kw):
            return e.dma_start(out=out, in_=in_, **kw)
        self.ops[q].append((waits, fn, (key, 16)))
        self._commit(tok, reads, writes)
        self.n_inst += 1
        return tok

    def barrier(self):
        for e in self.ENGS:
            waits = []
            for k in ("pe", "dve", "act", "pool"):
                v = self.count[k]
                key = (k, self.gen)
                if k != e and v > 0 and self.seen[e].get(key, 0) < v:
                    self.seen[e][key] = v
                    waits.append((key, v))
            for key, v in self.dma_cnt.items():
                if v > 0 and self.seen[e].get(key, 0) < v:
                    self.seen[e][key] = v
                    waits.append((key, v))
            if waits:
                self.ops[e].append((waits, None, None))
        self._new_gen()

    def emit(self):
        nc = self.nc
        sems = self.sems

        def replay(eng_name, e):
            for waits, fn, inc in self.ops[eng_name]:
                for k, v in waits:
                    e.wait_ge(sems[k], v)
                if fn is not None:
                    ins = fn(e)
                    ins.then_inc(sems[inc[0]], inc[1])

        with nc.Block() as block:
            @block.tensor
            def _(e):
                replay("pe", e)

            @block.vector
            def _(e):
                replay("dve", e)

            @block.scalar
            def _(e):
                replay("act", e)

            @block.gpsimd
            def _(e):
                replay("pool", e)

            @block.sync
            def _(e):
                replay("sp", e)
        self.stack.close()


class Arena:
    def __init__(self, P, nbytes):
        self.cap = nbytes // 2
        self.t = P.sbuf("arena", [128, self.cap], BF16)
        self.off = 0

    def reset(self):
        self.off = 0

    def alloc(self, shape, dtype, name=""):
        n = 1
        for s in shape:
            n *= s
        w = n * (2 if dtype == F32 else 1)
        off = (self.off + 31) // 32 * 32
        assert off + w <= self.cap, ("arena overflow", name, off + w, self.cap)
        ap = self.t[:, off:off + w]
        self.off = off + w
        if dtype == F32:
            ap = ap.bitcast(F32)
        if len(shape) == 2:
            ap = ap.rearrange("p (a b) -> p a b", a=shape[0])
        elif len(shape) == 3:
            ap = ap.rearrange("p (a b c) -> p a b c", a=shape[0], b=shape[1])
        return TL(ap, name)


def build_program(stop_after=99):
    nc = bass.Bass("TRN2", target_bir_lowering=False)

    def din(name, shape):
        return nc.dram_tensor(name, list(shape), F32, kind="ExternalInput").ap()

    x = din("x", [T, D])
    ctx = din("ctx", [TC, D])
    cc = din("cc", [2, D])
    w_mod = din("w_mod", [D, 6 * D])
    b_mod = din("b_mod", [6 * D])
    g_pre_mix = din("g_pre_mix", [D])
    g_post_mix = din("g_post_mix", [D])
    g_pre_ffn = din("g_pre_ffn", [D])
    g_post_ffn = din("g_post_ffn", [D])
    w_in = din("w_in", [D, INC])
    conv_qkv = din("conv_qkv", [5, 1536])
    a_log = din("a_log", [8])
    dt_bias = din("dt_bias", [8])
    o_norm = din("o_norm", [128])
    pool_w = din("pool_w", [4, 128, 128])
    pool_scale = din("pool_scale", [512])
    w_out = din("w_out", [D, D])
    w_up = din("w_up", [D, 2 * DFF])
    conv_ffn = din("conv_ffn", [3, DFF])
    w_down = din("w_down", [DFF, D])
    out = nc.dram_tensor("out", [T, D], F32, kind="ExternalOutput").ap()

    skind = "ExternalOutput" if DEBUG else "Internal"

    def scr(name, shape, dt):
        return nc.dram_tensor(name, list(shape), dt, kind=skind).ap()

    qT_s = scr("qT_s", [4, 128, T], BF16)
    kT_s = scr("kT_s", [4, 128, T], BF16)
    vT_s = scr("vT_s", [4, 128, T], BF16)
    kT_c = scr("kT_c", [4, 128, TC], BF16)
    vT_c = scr("vT_c", [4, 128, TC], BF16)
    zT_s = scr("zT_s", [4, 128, T], F32)
    uT_s = scr("uT_s", [4, 128, T], F32)
    poolT_s = scr("poolT_s", [4, 128, T], BF16)
    of_s = scr("of_s", [T // CH, 128, 512], F32)
    ob_s = scr("ob_s", [T // CH, 128, 512], F32)
    x1_s = scr("x1_s", [T, D], F32)
    hx2T_s = scr("hx2T_s", [8, 128, T], BF16)
    wd_s = scr("wd_s", [NFC, 128, D], BF16)
    wout_s = scr("wout_s", [128, 8, D], BF16)
    wup_s = scr("wup_s", [128, 8, 2 * DFF], BF16)
    dbg_s = scr("dbg_s", [128, 4096], F32)
    dbg2_s = scr("dbg2_s", [128, 4096], F32)

    P = Prog(nc)
    op = P.op

    def pers(name, shape, dt):
        return TL(P.sbuf(name, [128] + list(shape), dt)[:], name)

    identf = pers("identf", [128], F32)
    identb = pers("identb", [128], BF16)
    onesf = pers("onesf", [128], F32)
    onesb = pers("onesb", [128], BF16)
    Lmask = pers("Lmask", [128], F32)
    Umask = pers("Umask", [128], F32)
    mbF = pers("mbF", [128], F32)
    mbB = pers("mbB", [128], F32)
    stF4 = pers("stF4", [4, 128], F32)
    stB4 = pers("stB4", [4, 128], F32)
    I4 = pers("I4", [4, 128], F32)
    eps1 = pers("eps1", [1], F32)
    epsq = pers("epsq", [1], F32)
    one1 = pers("one1", [1], F32)
    modfm = pers("modfm", [32, 2], F32)
    bmodfm = pers("bmodfm", [48], F32)
    gfm = pers("gfm", [16], F32)
    A1 = pers("A1", [8], F32)
    B1 = pers("B1", [8], F32)
    A1c = pers("A1c", [8], F32)
    B1c = pers("B1c", [8], F32)
    A2 = pers("A2", [8], F32)
    B2 = pers("B2", [8], F32)
    gt1g = pers("gt1g", [D], F32)
    gt2g = pers("gt2g", [D], F32)
    cw = pers("cw", [60], F32)
    cwf = pers("cwf", [66], F32)
    onorm = pers("onorm", [1], F32)
    pscale = pers("pscale", [4], F32)
    negA = pers("negA", [8], F32)
    dtb = pers("dtb", [8], F32)
    Gm = pers("Gm", [T // CH, 8], F32)
    Bt = pers("Bt", [T // CH, 8], F32)
    Gc = pers("Gc", [TC // CH, 8], F32)
    Bc = pers("Bc", [TC // CH, 8], F32)
    S = pers("S", [8, 128], F32)
    Sb = pers("Sb", [8, 128], BF16)
    halp = pers("halp", [T // 512, 8, 2], BF16)

    psf = [TL(P.psum("psf%d" % i, [128, 512], F32)[:], "psf%d" % i) for i in range(6)]
    psb = [TL(P.psum("psb%d" % i, [128, 1024], BF16)[:], "psb%d" % i) for i in range(2)]
    rr = {"f": 0, "b": 0}

    def nps():
        rr["f"] = rr["f"] % 5 + 1
        return psf[rr["f"]]

    def npb():
        rr["b"] = (rr["b"] + 1) % 2
        return psb[rr["b"]]

    ffn_banks = [psf[5], TL(psb[0].ap.bitcast(F32), "psb0f"), TL(psb[1].ap.bitcast(F32), "psb1f")]
    ffn_banks[1].b = psb[0].b
    ffn_banks[2].b = psb[1].b

    def nps_ffn():
        rr["b"] = (rr["b"] + 1) % 3
        return ffn_banks[rr["b"]]

    ar = Arena(P, 177152)
    if DEBUG:
        print("sbuf bytes remaining after arena:", nc.sbuf_bytes_remaining)
    ar.dbg = (dbg2_s, Gm, Bt, Gc, Bc, S)

    def mm(outT, out_ap, groups, reads):
        def fn(e, groups=groups, out_ap=out_ap):
            ins = None
            for oa, pairs in groups:
                n = len(pairs)
                for i, (l, r) in enumerate(pairs):
                    ins = e.matmul(oa if oa is not None else out_ap, lhsT=l, rhs=r, start=(i == 0), stop=(i == n - 1))
            return ins
        return op("pe", fn, reads=reads, writes=[outT])

    def sel(t, pattern, cm, cmp_op, fill=0.0):
        op("pool", lambda e: e.affine_select(out=t.ap, in_=t.ap, pattern=pattern, compare_op=cmp_op, fill=fill,
                                             base=0, channel_multiplier=cm), reads=[t], writes=[t])

    for t_ in (identf, onesf, Lmask, Umask):
        op("pool", lambda e, t_=t_: e.memset(t_.ap, 1.0), writes=[t_])
    sel(identf, [[-1, 128]], 1, ALU.is_equal)
    sel(Lmask, [[1, 128]], -1, ALU.is_ge)
    sel(Umask, [[-1, 128]], 1, ALU.is_ge)
    op("dve", lambda e: e.tensor_copy(out=identb.ap, in_=identf.ap), reads=[identf], writes=[identb])
    op("dve", lambda e: e.tensor_copy(out=onesb.ap, in_=onesf.ap), reads=[onesf], writes=[onesb])
    op("dve", lambda e: e.tensor_scalar(out=mbF.ap, in0=Lmask.ap, scalar1=-1.0, scalar2=1e5, op0=ALU.add, op1=ALU.mult),
       reads=[Lmask], writes=[mbF])
    op("dve", lambda e: e.tensor_scalar(out=mbB.ap, in0=Umask.ap, scalar1=-1.0, scalar2=1e5, op0=ALU.add, op1=ALU.mult),
       reads=[Umask], writes=[mbB])
    for h in range(4):
        op("dve", lambda e, h=h: e.tensor_tensor(out=stF4.ap[:, h, :], in0=Lmask.ap, in1=identf.ap, op=ALU.subtract),
           reads=[Lmask, identf], writes=[stF4])
        op("dve", lambda e, h=h: e.tensor_tensor(out=stB4.ap[:, h, :], in0=Umask.ap, in1=identf.ap, op=ALU.subtract),
           reads=[Umask, identf], writes=[stB4])
        op("dve", lambda e, h=h: e.tensor_copy(out=I4.ap[:, h, :], in_=identf.ap), reads=[identf], writes=[I4])
    blkm = {}
    for bsz in (16, 32, 64):
        E_ = ar.alloc([128], F32, "Eblk%d" % bsz)
        op("pool", lambda e, E_=E_: e.memset(E_.ap, 1.0), writes=[E_])
        op("pool", lambda e, E_=E_, bsz=bsz: e.affine_select(out=E_.ap, in_=E_.ap, pattern=[[1, 128]], compare_op=ALU.is_ge,
                                                             fill=0.0, base=0, channel_multiplier=-bsz), reads=[E_], writes=[E_])
        op("pool", lambda e, E_=E_, bsz=bsz: e.affine_select(out=E_.ap, in_=E_.ap, pattern=[[-1, 128]], compare_op=ALU.is_ge,
                                                             fill=0.0, base=bsz - 1, channel_multiplier=bsz), reads=[E_], writes=[E_])
        ps = nps()
        mm(ps, ps.ap[:, 0:128], [(None, [(E_.ap, E_.ap)])], [E_])
        blkm[bsz] = ar.alloc([128], F32, "blk%d" % bsz)
        op("dve", lambda e, ps=ps, bsz=bsz: e.tensor_copy(out=blkm[bsz].ap, in_=ps.ap[:, 0:128]), reads=[ps], writes=[blkm[bsz]])
    off16 = ar.alloc([128], F32, "off16")
    off32 = ar.alloc([128], F32, "off32")
    off64 = ar.alloc([128], F32, "off64")
    op("dve", lambda e: e.tensor_tensor(out=off16.ap, in0=blkm[32].ap, in1=blkm[16].ap, op=ALU.subtract), reads=[blkm[32], blkm[16]], writes=[off16])
    op("dve", lambda e: e.tensor_tensor(out=off32.ap, in0=blkm[64].ap, in1=blkm[32].ap, op=ALU.subtract), reads=[blkm[64], blkm[32]], writes=[off32])
    op("dve", lambda e: e.tensor_tensor(out=off64.ap, in0=onesf.ap, in1=blkm[64].ap, op=ALU.subtract), reads=[onesf, blkm[64]], writes=[off64])
    mk4 = {}
    for nm_, src_m in (("bm16", blkm[16]), ("off16", off16), ("off32", off32), ("off64", off64)):
        mk4[nm_] = pers(nm_ + "_4b", [4, 128], BF16)
        for h in range(4):
            op("dve", lambda e, nm_=nm_, src_m=src_m, h=h: e.tensor_copy(out=mk4[nm_].ap[:, h, :], in_=src_m.ap),
               reads=[src_m], writes=[mk4[nm_]])
    op("dve", lambda e: e.memset(eps1.ap, EPS), writes=[eps1])
    op("dve", lambda e: e.memset(epsq.ap, EPS * 128.0), writes=[epsq])
    op("dve", lambda e: e.memset(one1.ap, 1.0), writes=[one1])

    ldst = [ar.alloc([128], F32, "ldst%d" % i) for i in range(2)]
    ldi = [0]

    def load_fm(src2d, n, dst_ap, dstT):
        st = ldst[ldi[0] % 2]
        ldi[0] += 1
        P.dma("sp", st.ap[0:n, :], src2d, writes=[st])
        ps = nps()
        mm(ps, ps.ap[:, 0:n], [(None, [(st.ap[0:n, :], identf.ap[0:n, 0:n])])], [st, identf])
        op("dve", lambda e: e.tensor_copy(out=dst_ap, in_=ps.ap[:, 0:n]), reads=[ps], writes=[dstT])

    load_fm(b_mod.rearrange("(c p) -> c p", p=128), 48, bmodfm.ap, bmodfm)
    load_fm(g_pre_mix.rearrange("(c p) -> c p", p=128), 8, gfm.ap[:, 0:8], gfm)
    load_fm(g_pre_ffn.rearrange("(c p) -> c p", p=128), 8, gfm.ap[:, 8:16], gfm)
    load_fm(conv_qkv.rearrange("t (c p) -> (t c) p", p=128), 60, cw.ap, cw)
    load_fm(conv_ffn.rearrange("t (c p) -> (t c) p", p=128), 66, cwf.ap, cwf)
    load_fm(pool_scale.rearrange("(c p) -> c p", p=128), 4, pscale.ap, pscale)
    load_fm(o_norm.rearrange("(c p) -> c p", p=128), 1, onorm.ap, onorm)
    P.dma("sp", negA.ap, a_log.partition_broadcast(128), writes=[negA])
    P.dma("sp", dtb.ap, dt_bias.partition_broadcast(128), writes=[dtb])
    op("act", lambda e: e.activation(out=negA.ap, in_=negA.ap, func=AF.Exp), reads=[negA], writes=[negA])
    op("dve", lambda e: e.tensor_scalar(out=negA.ap, in0=negA.ap, scalar1=-1.0, scalar2=None, op0=ALU.mult),
       reads=[negA], writes=[negA])

    ccst = ar.alloc([128], F32, "ccst")
    sT = ar.alloc([8, 2], F32, "sT")
    sRep = ar.alloc([8, 128], F32, "sRep")
    wst = [ar.alloc([8, 512], F32, "wst%d" % i) for i in range(2)]
    browst = [ar.alloc([512], F32, "brow%d" % i) for i in range(2)]
    grow = [ar.alloc([512], F32, "grow%d" % i) for i in range(2)]
    P.dma("sp", ccst.ap[0:16, :], cc.rearrange("r (k p) -> (r k) p", p=128), writes=[ccst])
    ps = nps()
    mm(ps, ps.ap[:, 0:16], [(None, [(ccst.ap[0:16, :], identf.ap[0:16, 0:16])])], [ccst, identf])
    for r in range(2):
        op("act", lambda e, r=r, ps=ps: e.activation(out=sT.ap[:, :, r], in_=ps.ap[:, r * 8:(r + 1) * 8], func=AF.Silu),
           reads=[ps], writes=[sT])
    for kc in range(8):
        op("dve", lambda e, kc=kc: e.tensor_scalar(out=sRep.ap[:, kc, :], in0=onesf.ap, scalar1=sT.ap[:, kc, 0:1],
                                                   scalar2=None, op0=ALU.mult), reads=[onesf, sT], writes=[sRep])
    psm = psf[0]
    wm_v = w_mod.rearrange("(k p) n -> p k n", p=128)
    fmi = 0
    for hb in range(12):
        blk = hb // 2
        w = wst[hb % 2]
        P.dma("sp", w.ap, wm_v[:, :, hb * 512:(hb + 1) * 512], writes=[w])
        if blk in (2, 5):
            ps = nps()
            mm(ps, ps.ap, [(None, [(sRep.ap[:, kc, :], w.ap[:, kc, :]) for kc in range(8)])], [sRep, w])
            br = browst[hb % 2]
            gr = grow[hb % 2]
            dst = gt1g if blk == 2 else gt2g
            gsrc = g_post_mix if blk == 2 else g_post_ffn
            half = hb % 2
            P.dma("sp", br.ap, b_mod[hb * 512:(hb + 1) * 512].partition_broadcast(128), writes=[br])
            P.dma("sp", gr.ap, gsrc[half * 512:(half + 1) * 512].partition_broadcast(128), writes=[gr])
            op("dve", lambda e, ps=ps, br=br: e.tensor_tensor(out=br.ap, in0=ps.ap, in1=br.ap, op=ALU.add),
               reads=[ps, br], writes=[br])
            op("dve", lambda e, br=br, gr=gr, dst=dst, half=half: e.tensor_tensor(
                out=dst.ap[:, half * 512:(half + 1) * 512], in0=br.ap, in1=gr.ap, op=ALU.mult),
               reads=[br, gr], writes=[dst])
        else:
            groups = []
            for fc in range(4):
                groups.append((psm.ap[:, fmi * 2:fmi * 2 + 2],
                               [(w.ap[:, kc, fc * 128:(fc + 1) * 128], sT.ap[:, kc, :]) for kc in range(8)]))
                fmi += 1
            mm(psm, None, groups, [w, sT])
    op("dve", lambda e: e.tensor_copy(out=modfm.ap, in_=psm.ap[:, 0:64].rearrange("p (i r) -> p i r", r=2)),
       reads=[psm], writes=[modfm])
    bsel = {0: 0, 1: 8, 2: 24, 3: 32}

    def mk_ab(Adst, Bdst, r, sh_i, sc_i, g_off, bsh, bsc):
        op("dve", lambda e: e.tensor_tensor(out=Bdst.ap, in0=modfm.ap[:, sh_i:sh_i + 8, r], in1=bmodfm.ap[:, bsh:bsh + 8],
                                            op=ALU.add), reads=[modfm, bmodfm], writes=[Bdst])
        op("dve", lambda e: e.scalar_tensor_tensor(out=Adst.ap, in0=modfm.ap[:, sc_i:sc_i + 8, r], scalar=1.0,
                                                   in1=bmodfm.ap[:, bsc:bsc + 8], op0=ALU.add, op1=ALU.add),
           reads=[modfm, bmodfm], writes=[Adst])
        op("dve", lambda e: e.tensor_tensor(out=Adst.ap, in0=Adst.ap, in1=gfm.ap[:, g_off:g_off + 8], op=ALU.mult),
           reads=[Adst, gfm], writes=[Adst])

    mk_ab(A1, B1, 0, 0, 8, 0, 0, 8)
    mk_ab(A1c, B1c, 1, 0, 8, 0, 0, 8)
    mk_ab(A2, B2, 0, 16, 24, 8, 24, 32)

    if DEBUG:
        dbgt = ar.alloc([4096], F32, "dbgt")
        op("dve", lambda e: e.memset(dbgt.ap, 0.0), writes=[dbgt])
        for i, (t_, n) in enumerate(((A1, 8), (B1, 8), (A1c, 8), (B1c, 8), (A2, 8), (B2, 8), (cw, 60), (cwf, 66),
                                     (negA, 8), (dtb, 8), (pscale, 4), (onorm, 1))):
            op("dve", lambda e, t_=t_, n=n, i=i: e.tensor_copy(out=dbgt.ap[:, i * 128:i * 128 + n], in_=t_.ap),
               reads=[t_], writes=[dbgt])
        for i, t_ in enumerate((blkm[16], off16, off32, off64)):
            src_ap = t_.ap
            op("dve", lambda e, src_ap=src_ap, i=i: e.tensor_copy(out=dbgt.ap[:, 1536 + i * 128:1664 + i * 128], in_=src_ap),
               reads=[t_], writes=[dbgt])
        op("dve", lambda e: e.tensor_copy(out=dbgt.ap[:, 2048:3072], in_=gt1g.ap), reads=[gt1g], writes=[dbgt])
        op("dve", lambda e: e.tensor_copy(out=dbgt.ap[:, 3072:4096], in_=gt2g.ap), reads=[gt2g], writes=[dbgt])
        P.dma("sp", dbg_s, dbgt.ap, reads=[dbgt])

    if stop_after <= 0:
        return finish(P, nc, out, ar)

    P.barrier()
    ar.reset()
    w_in_bf = ar.alloc([8, INC], BF16, "w_in_bf")
    mark1 = ar.off
    wst2 = [ar.alloc([8, 512], F32, "wstA%d" % i) for i in range(2)]
    cast_i = [0]

    def load_cast(dstT, src_view, ncols, piece=512):
        for c0 in range(0, ncols, piece):
            n = min(piece, ncols - c0)
            i = cast_i[0]
            cast_i[0] += 1
            st = wst2[i % 2]
            P.dma("sp", st.ap[:, :, 0:n], src_view[:, :, c0:c0 + n], writes=[st])
            eng = ("dve", "act", "pool")[i % 3]
            if eng == "act":
                op("act", lambda e, st=st, c0=c0, n=n: e.copy(out=dstT.ap[:, :, c0:c0 + n], in_=st.ap[:, :, 0:n]),
                   reads=[st], writes=[dstT])
            else:
                op(eng, lambda e, st=st, c0=c0, n=n: e.tensor_copy(out=dstT.ap[:, :, c0:c0 + n], in_=st.ap[:, :, 0:n]),
                   reads=[st], writes=[dstT])

    load_cast(w_in_bf, w_in.rearrange("(k p) n -> p k n", p=128), INC)
    P.barrier()
    ar.off = mark1

    stg = [ar.alloc([516], F32, "stg%d" % j) for j in range(12)]
    xt = [ar.alloc([D], F32, "xt%d" % i) for i in range(2)]
    xs = [ar.alloc([D], BF16, "xs%d" % i) for i in range(2)]
    junk = ar.alloc([D], BF16, "junk")
    hxT = [ar.alloc([8, 512], BF16, "hxT%d" % i) for i in range(2)]
    stt = [ar.alloc([4], F32, "stt%d" % i) for i in range(4)]
    acc = [ar.alloc([512], F32, "acc%d" % i) for i in range(2)]
    qs = [[ar.alloc([512], F32, "qs%d_%d" % (p_, i)) for i in range(8)] for p_ in range(2)]
    sq = [ar.alloc([512], BF16, "sq%d" % i) for i in range(3)]
    rt = [ar.alloc([512], F32, "rt%d" % i) for i in range(3)]
    ot = [ar.alloc([512], BF16, "ot%d" % i) for i in range(4)]
    zo = [ar.alloc([512], F32, "zo%d" % i) for i in range(3)]
    gt1_ = ar.alloc([4, 8], F32, "gtmp1")
    gt2_ = ar.alloc([4, 8], F32, "gtmp2")
    cnt = {"x": 0, "acc": 0, "sq": 0, "ot": 0, "zo": 0, "st": 0}

    def rot(lst, key):
        cnt[key] += 1
        return lst[cnt[key] % len(lst)]

    def make_hxT(hx, src, r0, sub, A_, B_, xtile_in=None):
        cnt["x"] += 1
        xtile = xt[cnt["x"] % 2] if xtile_in is None else xtile_in
        xsb = xs[cnt["x"] % 2]
        st = rot(stt, "st")
        if xtile_in is None:
            P.dma("sp", xtile.ap, src[r0:r0 + 128, :], writes=[xtile])
        op("act", lambda e: e.memzero(st.ap), writes=[st])
        op("act", lambda e, jk=junk: e.activation(out=jk.ap, in_=xtile.ap, func=AF.Square, accum_out=st.ap[:, 0:1]),
           reads=[xtile, st], writes=[junk, st])
        op("act", lambda e: e.activation(out=st.ap[:, 1:2], in_=st.ap[:, 0:1], func=AF.Sqrt, scale=1.0 / D, bias=eps1.ap),
           reads=[st, eps1], writes=[st])
        op("dve", lambda e: e.reciprocal(out=st.ap[:, 2:3], in_=st.ap[:, 1:2]), reads=[st], writes=[st])
        op("dve", lambda e: e.tensor_scalar(out=xsb.ap, in0=xtile.ap, scalar1=st.ap[:, 2:3], scalar2=None, op0=ALU.mult),
           reads=[xtile, st], writes=[xsb])
        pb = npb()

        def tr(e):
            ins = None
            for kc in range(8):
                ins = e.transpose(out=pb.ap[:, kc * 128:(kc + 1) * 128], in_=xsb.ap[:, kc * 128:(kc + 1) * 128],
                                  identity=identb.ap)
            return ins
        op("pe", tr, reads=[xsb, identb], writes=[pb])
        for kc in range(8):
            dst = hx.ap[:, kc, sub * 128:(sub + 1) * 128]
            srcp = pb.ap[:, kc * 128:(kc + 1) * 128]
            if kc % 2 == 0:
                op("act", lambda e, dst=dst, srcp=srcp, kc=kc: e.activation(
                    out=dst, in_=srcp, func=AF.Identity, scale=A_.ap[:, kc:kc + 1], bias=B_.ap[:, kc:kc + 1]),
                   reads=[pb, A_, B_], writes=[hx])
            else:
                op("dve", lambda e, dst=dst, srcp=srcp, kc=kc: e.tensor_scalar(
                    out=dst, in0=srcp, scalar1=A_.ap[:, kc:kc + 1], scalar2=B_.ap[:, kc:kc + 1], op0=ALU.mult, op1=ALU.add),
                   reads=[pb, A_, B_], writes=[hx])

    def interleave(gens):
        gens = list(gens)
        while gens:
            for g_ in list(gens):
                try:
                    next(g_)
                except StopIteration:
                    gens.remove(g_)

    def project(src, Ttok, W, A_, B_, chunks, dq, dk, dv, Gd, Bd):
        ntile = Ttok // W
        nsub = W // 128
        for j in range(12):
            op("pool", lambda e, j=j: e.memset(stg[j].ap[:, 0:4], 0.0), writes=[stg[j]])

        def conv_silu(j, m0, n, tok0, qp):
            a = rot(acc, "acc")
            sj = stg[j]
            op("dve", lambda e, a=a, sj=sj, j=j: e.tensor_scalar(
                out=a.ap[:, 0:n], in0=sj.ap[:, m0:m0 + n], scalar1=cw.ap[:, j:j + 1], scalar2=None, op0=ALU.mult),
               reads=[sj, cw], writes=[a])
            for tap in range(1, 5):
                op("dve", lambda e, a=a, sj=sj, j=j, tap=tap: e.scalar_tensor_tensor(
                    out=a.ap[:, 0:n], in0=sj.ap[:, m0 + tap:m0 + tap + n], scalar=cw.ap[:, tap * 12 + j:tap * 12 + j + 1],
                    in1=a.ap[:, 0:n], op0=ALU.mult, op1=ALU.add), reads=[sj, cw, a], writes=[a])
            if j < 8:
                op("act", lambda e, a=a, j=j: e.activation(out=qs[qp][j].ap[:, 0:n], in_=a.ap[:, 0:n], func=AF.Silu),
                   reads=[a], writes=[qs[qp][j]])
            else:
                o = rot(ot, "ot")
                op("act", lambda e, a=a, o=o: e.activation(out=o.ap[:, 0:n], in_=a.ap[:, 0:n], func=AF.Silu),
                   reads=[a], writes=[o])
                P.dma("act", dv[j - 8][:, tok0:tok0 + n], o.ap[:, 0:n], reads=[o])

        def l2norm_front(j, n, qp):
            s_ = rot(sq, "sq")
            r_ = rt[cnt["sq"] % 3]
            q_ = qs[qp][j]
            isq = j < 4
            op("pool", lambda e: e.tensor_tensor(out=s_.ap[:, 0:n], in0=q_.ap[:, 0:n], in1=q_.ap[:, 0:n], op=ALU.mult),
               reads=[q_], writes=[s_])
            pn = nps()
            mm(pn, pn.ap[:, 0:n], [(None, [(onesb.ap, s_.ap[:, 0:n])])], [onesb, s_])
            op("act", lambda e: e.activation(out=r_.ap[:, 0:n], in_=pn.ap[:, 0:n], func=AF.Ln, scale=(128.0 if isq else 1.0),
                                             bias=(epsq.ap if isq else eps1.ap)), reads=[pn, epsq, eps1], writes=[r_])
            op("act", lambda e: e.activation(out=r_.ap[:, 0:n], in_=r_.ap[:, 0:n], func=AF.Exp, scale=-0.5), reads=[r_], writes=[r_])
            return (j, r_, qp)

        def l2norm_back(jr, n, tok0):
            j, r_, qp = jr
            q_ = qs[qp][j]
            o = rot(ot, "ot")
            op("pool", lambda e: e.tensor_tensor(out=o.ap[:, 0:n], in0=q_.ap[:, 0:n], in1=r_.ap[:, 0:n], op=ALU.mult),
               reads=[q_, r_], writes=[o])
            dst = dq[j] if j < 4 else dk[j - 4]
            P.dma("pool", dst[:, tok0:tok0 + n], o.ap[:, 0:n], reads=[o])

        def stageA(t):
            hx = hxT[t % 2]
            gp = psf[0]
            pend = None
            for sub in range(nsub + 1):
                if sub < nsub:
                    cnt["x"] += 1
                    xtile, xsb = xt[cnt["x"] % 2], xs[cnt["x"] % 2]
                    st = rot(stt, "st")
                    r0 = t * W + sub * 128
                    P.dma("sp", xtile.ap, src[r0:r0 + 128, :], writes=[xtile])
                    op("act", lambda e, st=st: e.memzero(st.ap), writes=[st])
                    op("act", lambda e, jk=junk, xtile=xtile, st=st: e.activation(out=jk.ap, in_=xtile.ap, func=AF.Square,
                                                                                 accum_out=st.ap[:, 0:1]),
                       reads=[xtile, st], writes=[junk, st])
                    op("act", lambda e, st=st: e.activation(out=st.ap[:, 1:2], in_=st.ap[:, 0:1], func=AF.Sqrt, scale=1.0 / D,
                                                            bias=eps1.ap), reads=[st, eps1], writes=[st])
                    op("dve", lambda e, st=st: e.reciprocal(out=st.ap[:, 2:3], in_=st.ap[:, 1:2]), reads=[st], writes=[st])
                    op("dve", lambda e, xsb=xsb, xtile=xtile, st=st: e.tensor_scalar(
                        out=xsb.ap, in0=xtile.ap, scalar1=st.ap[:, 2:3], scalar2=None, op0=ALU.mult),
                       reads=[xtile, st], writes=[xsb])
                if pend is not None:
                    psub, pxs = pend
                    pb = npb()

                    def tr(e, pb=pb, pxs=pxs):
                        ins = None
                        for kc in range(8):
                            ins = e.transpose(out=pb.ap[:, kc * 128:(kc + 1) * 128], in_=pxs.ap[:, kc * 128:(kc + 1) * 128],
                                              identity=identb.ap)
                        return ins
                    op("pe", tr, reads=[pxs, identb], writes=[pb])
                    for kc in range(8):
                        dst = hx.ap[:, kc, psub * 128:(psub + 1) * 128]
                        srcp = pb.ap[:, kc * 128:(kc + 1) * 128]
                        if kc % 2 == 0:
                            op("act", lambda e, dst=dst, srcp=srcp, kc=kc: e.activation(
                                out=dst, in_=srcp, func=AF.Identity, scale=A_.ap[:, kc:kc + 1], bias=B_.ap[:, kc:kc + 1]),
                               reads=[pb, A_, B_], writes=[hx])
                        else:
                            op("dve", lambda e, dst=dst, srcp=srcp, kc=kc: e.tensor_scalar(
                                out=dst, in0=srcp, scalar1=A_.ap[:, kc:kc + 1], scalar2=B_.ap[:, kc:kc + 1], op0=ALU.mult, op1=ALU.add),
                               reads=[pb, A_, B_], writes=[hx])
                    mm(gp, gp.ap[:, psub * 16:(psub + 1) * 16],
                       [(None, [(hx.ap[:, kc, psub * 128:(psub + 1) * 128], w_in_bf.ap[:, kc, 2048:2064]) for kc in range(8)])],
                       [hx, w_in_bf])
                pend = (sub, xsb) if sub < nsub else None
                yield
            gv = gp.ap[:, 0:nsub * 16].rearrange("p (s c) -> p s c", c=16)
            t1 = gt1_.ap[:, 0:nsub, :]
            t2 = gt2_.ap[:, 0:nsub, :]
            op("act", lambda e: e.activation(out=t1, in_=gv[:, :, 0:8], func=AF.Exp, scale=-1.0), reads=[gp], writes=[gt1_])
            for sub in range(nsub):
                op("dve", lambda e, sub=sub: e.tensor_tensor(out=gt2_.ap[:, sub, :], in0=gv[:, sub, 8:16], in1=dtb.ap,
                                                            op=ALU.add), reads=[gp, dtb], writes=[gt2_])
            op("dve", lambda e: e.tensor_scalar(out=t1, in0=t1, scalar1=1.0, scalar2=None, op0=ALU.add), reads=[gt1_], writes=[gt1_])
            op("dve", lambda e: e.reciprocal(out=Bd.ap[:, t * nsub:(t + 1) * nsub, :], in_=t1), reads=[gt1_], writes=[Bd])
            op("act", lambda e: e.activation(out=t2, in_=t2, func=AF.Exp), reads=[gt2_], writes=[gt2_])
            op("act", lambda e: e.activation(out=t2, in_=t2, func=AF.Ln, bias=one1.ap), reads=[gt2_, one1], writes=[gt2_])
            for sub in range(nsub):
                op("dve", lambda e, sub=sub: e.tensor_tensor(out=Gd.ap[:, t * nsub + sub, :], in0=gt2_.ap[:, sub, :],
                                                            in1=negA.ap, op=ALU.mult), reads=[gt2_, negA], writes=[Gd])
            yield

        def stageBC(t):
            hx = hxT[t % 2]
            m0 = 2 if t == 0 else 0
            n = W - m0
            tok0 = t * W - 2 + m0
            order = [j for j in chunks if j < 12] + [j for j in chunks if j >= 12]
            prev = None
            for j in order + [None]:
                if j is not None:
                    c0 = j * 128 if j < 16 else 2064 + (j - 17) * 128
                    ps = nps()
                    mm(ps, ps.ap[:, 0:W], [(None, [(w_in_bf.ap[:, kc, c0:c0 + 128], hx.ap[:, kc, 0:W]) for kc in range(8)])],
                       [w_in_bf, hx])
                    if j < 12:
                        op("act", lambda e, ps=ps, j=j: e.copy(out=stg[j].ap[:, 4:4 + W], in_=ps.ap[:, 0:W]),
                           reads=[ps], writes=[stg[j]])
                    else:
                        z_ = rot(zo, "zo")
                        op("act", lambda e, ps=ps, z_=z_: e.copy(out=z_.ap[:, 0:W], in_=ps.ap[:, 0:W]), reads=[ps], writes=[z_])
                        dst = zT_s[j - 12] if j < 16 else uT_s[j - 17]
                        P.dma("act", dst[:, t * W:(t + 1) * W], z_.ap[:, 0:W], reads=[z_])
                if prev is not None:
                    conv_silu(prev, m0, n, tok0, t % 2)
                prev = j if (j is not None and j < 12) else None
                yield
            for j in chunks:
                if j < 12:
                    op("pool", lambda e, j=j: e.tensor_copy(out=stg[j].ap[:, 0:4], in_=stg[j].ap[:, W:W + 4]),
                       reads=[stg[j]], writes=[stg[j]])
            yield

        def stageL(t):
            m0 = 2 if t == 0 else 0
            n = W - m0
            tok0 = t * W - 2 + m0
            prevn = None
            for j in [j for j in chunks if j < 8] + [None]:
                cur = l2norm_front(j, n, t % 2) if j is not None else None
                if prevn is not None:
                    l2norm_back(prevn, n, tok0)
                prevn = cur
                yield
            if t == ntile - 1:
                for j in chunks:
                    if j < 12:
                        op("pool", lambda e, j=j: e.memset(stg[j].ap[:, 4:8], 0.0), writes=[stg[j]])
                        conv_silu(j, 0, 2, Ttok - 2, t % 2)
                yield
                for j in chunks:
                    if j < 8:
                        l2norm_back(l2norm_front(j, 2, t % 2), 2, Ttok - 2)
                yield

        interleave([stageA(0)])
        for t in range(ntile):
            gens = [stageBC(t)]
            if t + 1 < ntile:
                gens.append(stageA(t + 1))
            if t >= 1:
                gens.append(stageL(t - 1))
            interleave(gens)
        interleave([stageL(ntile - 1)])

    project(ctx, TC, 256, A1c, B1c, list(range(4, 12)), None, [kT_c[h] for h in range(4)], [vT_c[h] for h in range(4)], Gc, Bc)
    if stop_after <= 1:
        return finish(P, nc, out, ar)
    project(x, T, 512, A1, B1, list(range(0, 16)) + list(range(17, 21)), [qT_s[h] for h in range(4)],
            [kT_s[h] for h in range(4)], [vT_s[h] for h in range(4)], Gm, Bt)
    if stop_after <= 2:
        return finish(P, nc, out, ar)

    P.barrier()
    ar.reset()
    U = ar.alloc([64, 64], F32, "poolU")
    PA = ar.alloc([64, 80], F32, "poolA")
    PB = ar.alloc([64, 80], F32, "poolB")
    PC = ar.alloc([80, 64], F32, "poolC")
    PD = ar.alloc([80, 64], F32, "poolD")
    PM = ar.alloc([64, 64], F32, "poolM")
    dTb = ar.alloc([T], BF16, "pooldT")
    pwst = ar.alloc([4, 128], F32, "pwst")
    pw_bf = ar.alloc([4, 128], BF16, "pw_bf")
    po = [ar.alloc([512], BF16, "po%d" % i) for i in range(2)]
    ca = ar.alloc([80], F32, "cnta")
    cb = ar.alloc([80], F32, "cntb")
    rcs = [ar.alloc([64], F32, "rc%d" % g) for g in range(4)]
    cst = [ar.alloc([8, 512], F32, "cst%d" % i) for i in range(2)]
    cbf = [ar.alloc([8, 512], BF16, "cbf%d" % i) for i in range(1)]

    def cast_bg():
        pieces = []
        wo_v = w_out.rearrange("(k p) n -> p k n", p=128)
        wu_v = w_up.rearrange("(k p) n -> p k n", p=128)
        wd_v = w_down.rearrange("(j p) n -> p j n", p=128)
        wds_v = wd_s.rearrange("j p n -> p j n")
        for c0 in range(0, D, 512):
            pieces.append((wo_v[:, :, c0:c0 + 512], wout_s[:, :, c0:c0 + 512], 8, 512))
        for c0 in range(0, 2 * DFF, 512):
            pieces.append((wu_v[:, :, c0:c0 + 512], wup_s[:, :, c0:c0 + 512], 8, 512))
        for j0 in range(0, NFC, 4):
            nj = min(4, NFC - j0)
            pieces.append((wd_v[:, j0:j0 + nj, :], wds_v[:, j0:j0 + nj, :], nj, D))
        for i, (src_, dst_, a_, b_) in enumerate(pieces):
            st_, bf_ = cst[i % 2], cbf[0]
            if b_ == 512:
                sv, bv = st_.ap[:, 0:a_, :], bf_.ap[:, 0:a_, :]
            else:
                sv = st_.ap.rearrange("p a b -> p (a b)")[:, 0:a_ * b_].rearrange("p (a b) -> p a b", a=a_)
                bv = bf_.ap.rearrange("p a b -> p (a b)")[:, 0:a_ * b_].rearrange("p (a b) -> p a b", a=a_)
            P.dma("sp", sv, src_, writes=[st_])
            op("act", lambda e, sv=sv, bv=bv: e.copy(out=bv, in_=sv), reads=[st_], writes=[bf_])
            P.dma("act", dst_, bv, reads=[bf_])
            yield

    bg_ = cast_bg()

    def bgstep(k=1):
        for _ in range(k):
            try:
                next(bg_)
            except StopIteration:
                return

    P.dma("sp", pwst.ap, pool_w.rearrange("g c d -> c g d"), writes=[pwst])
    op("dve", lambda e: e.tensor_copy(out=pw_bf.ap, in_=pwst.ap), reads=[pwst], writes=[pw_bf])
    for g in range(4):
        L = g + 1
        wv = 2 ** L
        left = wv // 2
        lo = 8 - left
        op("pool", lambda e: e.memset(ca.ap, 0.0), writes=[ca])
        op("pool", lambda e: e.memset(ca.ap[:, 8:72], 1.0), writes=[ca])
        src_, dst_ = ca, cb
        for l in range(L):
            sft = 2 ** l
            op("pool", lambda e, src_=src_, dst_=dst_, sft=sft: e.tensor_tensor(
                out=dst_.ap[:, 0:80 - sft], in0=src_.ap[:, 0:80 - sft], in1=src_.ap[:, sft:80], op=ALU.add),
               reads=[src_], writes=[dst_])
            src_, dst_ = dst_, src_
        rc = rcs[g]
        op("dve", lambda e, src_=src_, rc=rc, lo=lo: e.reciprocal(out=rc.ap, in_=src_.ap[:, lo:lo + 64]), reads=[src_], writes=[rc])
        P.dma("sp", U.ap, uT_s[g].rearrange("p (r c) -> p r c", c=64), writes=[U])
        op("pool", lambda e: e.memset(PA.ap, 0.0), writes=[PA])
        op("pool", lambda e: e.tensor_copy(out=PA.ap[:, :, 8:72], in_=U.ap), reads=[U], writes=[PA])
        src_, dst_ = PA, PB
        for l in range(L):
            sft = 2 ** l
            op("dve", lambda e, src_=src_, dst_=dst_, sft=sft: e.tensor_tensor(
                out=dst_.ap[:, :, 0:80 - sft], in0=src_.ap[:, :, 0:80 - sft], in1=src_.ap[:, :, sft:80], op=ALU.add),
               reads=[src_], writes=[dst_])
            src_, dst_ = dst_, src_
            bgstep(1)
        op("pool", lambda e: e.memset(PC.ap, 0.0), writes=[PC])
        op("pool", lambda e, src_=src_, rc=rc, lo=lo: e.tensor_tensor(
            out=PC.ap[:, 8:72, :], in0=src_.ap[:, :, lo:lo + 64], in1=rc.ap.unsqueeze(1).to_broadcast([128, 64, 64]),
            op=ALU.mult), reads=[src_, rc], writes=[PC])
        src_, dst_ = PC, PD
        for l in range(L):
            sft = 2 ** l
            op("dve", lambda e, src_=src_, dst_=dst_, sft=sft: e.tensor_tensor(
                out=dst_.ap[:, 0:80 - sft, :], in0=src_.ap[:, 0:80 - sft, :], in1=src_.ap[:, sft:80, :], op=ALU.add),
               reads=[src_], writes=[dst_])
            src_, dst_ = dst_, src_
        op("pool", lambda e, src_=src_, rc=rc, lo=lo: e.tensor_tensor(
            out=PM.ap, in0=src_.ap[:, lo:lo + 64, :], in1=rc.ap.unsqueeze(2).to_broadcast([128, 64, 64]),
            op=ALU.mult), reads=[src_, rc], writes=[PM])
        op("dve", lambda e: e.tensor_tensor(out=dTb.ap.rearrange("p (r c) -> p r c", c=64), in0=PM.ap, in1=U.ap,
                                            op=ALU.subtract), reads=[PM, U], writes=[dTb])
        bgstep(3)
        for tt in range(8):
            ps = nps()
            mm(ps, ps.ap, [(None, [(pw_bf.ap[:, g, :], dTb.ap[:, tt * 512:(tt + 1) * 512])])], [pw_bf, dTb])
            o = po[tt % 2]
            op("act", lambda e, ps=ps, o=o, g=g: e.activation(out=o.ap, in_=ps.ap, func=AF.Copy, scale=pscale.ap[:, g:g + 1]),
               reads=[ps, pscale], writes=[o])
            P.dma("act", poolT_s[g][:, tt * 512:(tt + 1) * 512], o.ap, reads=[o])
    for _ in bg_:
        pass
    if stop_after <= 3:
        return finish(P, nc, out, ar)

    P.barrier()
    ar.reset()
    op("pool", lambda e: e.memset(S.ap, 0.0), writes=[S])
    op("pool", lambda e: e.memset(Sb.ap, 0.0), writes=[Sb])
    Sd = [Buf("S0"), Buf("S1")]
    Sbd = [Buf("Sb0"), Buf("Sb1")]
    P.barrier()

    def wset(tag):
        W_ = {}
        for nm in ("Kf", "Vf", "Qf", "Vt", "KD", "NTb", "AT", "Mb", "Yb", "Pa", "PTa", "Pb", "PTb", "N0", "M0", "W2b", "YTb"):
            W_[nm] = ar.alloc([4, 128], BF16, nm + tag)
        for nm in ("GM", "DT", "DTs"):
            W_[nm] = ar.alloc([4, 128], F32, nm + tag)
        W_["cg"] = ar.alloc([16], F32, "cg" + tag)
        W_["E"] = ar.alloc([12], F32, "E" + tag)
        W_["nec"] = ar.alloc([4], F32, "nec" + tag)
        return W_

    WS = [[wset("_%d%d" % (d, p_)) for p_ in range(3)] for d in range(2)]
    RS = []
    for d in range(2):
        R_ = {}
        for nm in ("R3", "VN"):
            R_[nm] = ar.alloc([4, 128], BF16, "%s_%d" % (nm, d))
        for nm in ("OI", "O"):
            R_[nm] = ar.alloc([4, 128], F32, "%s_%d" % (nm, d))
        RS.append(R_)

    def v4(ps):
        return ps.ap.rearrange("p (h t) -> p h t", h=4)

    def mm4(ps, pairs_h, reads):
        groups = [(ps.ap[:, h * 128:(h + 1) * 128], pairs_h(h)) for h in range(4)]
        return mm(ps, None, groups, reads)

    def tr4(src_):
        pb = npb()

        def tr(e, src_=src_, pb=pb):
            ins = None
            for h in range(4):
                ins = e.transpose(out=pb.ap[:, h * 128:(h + 1) * 128], in_=src_.ap[:, h, :], identity=identb.ap)
            return ins
        op("pe", tr, reads=[src_, identb], writes=[pb])
        return pb

    def pb4(pb):
        return pb.ap[:, 0:512].rearrange("p (h t) -> p h t", h=4)

    def bfree(ap4):
        return ap4.unsqueeze(2).to_broadcast([128, 4, 128])

    def bhead(ap128):
        return ap128.unsqueeze(1).to_broadcast([128, 4, 128])

    def scan_pre(d, n, par, kd_, vd_, qd_, Gd, Bd):
        W_ = WS[d][par]
        with_out = qd_ is not None
        Kf, Vf, Qf, Vt, KD = W_["Kf"], W_["Vf"], W_["Qf"], W_["Vt"], W_["KD"]
        sl = slice(n * 128, (n + 1) * 128)
        P.dma("sp", Kf.ap, kd_.rearrange("h p t -> p h t")[:, :, sl], writes=[Kf])
        P.dma("sp", Vf.ap, vd_.rearrange("h p t -> p h t")[:, :, sl], writes=[Vf])
        if with_out:
            P.dma("sp", Qf.ap, qd_.rearrange("h p t -> p h t")[:, :, sl], writes=[Qf])
        yield
        g_ap = Gd.ap[:, n, d * 4:(d + 1) * 4]
        b_ap = Bd.ap[:, n, d * 4:(d + 1) * 4]
        mask = Lmask if d == 0 else Umask
        mb = mbF if d == 0 else mbB
        st4 = stF4 if d == 0 else stB4
        cg, E, nec = W_["cg"], W_["E"], W_["nec"]
        pc = nps()
        mm(pc, None, [(pc.ap[:, 0:4], [(mask.ap, g_ap)]), (pc.ap[:, 4:8], [(onesf.ap, g_ap)])], [mask, onesf, Gd])
        op("dve", lambda e: e.tensor_copy(out=cg.ap[:, 0:4], in_=pc.ap[:, 0:4]), reads=[pc], writes=[cg])
        op("dve", lambda e: e.tensor_tensor(out=cg.ap[:, 4:8], in0=pc.ap[:, 4:8], in1=cg.ap[:, 0:4], op=ALU.subtract),
           reads=[pc, cg], writes=[cg])
        op("dve", lambda e: e.tensor_copy(out=cg.ap[:, 8:12], in_=pc.ap[:, 4:8]), reads=[pc], writes=[cg])
        op("dve", lambda e: e.tensor_scalar(out=cg.ap[:, 12:16], in0=cg.ap[:, 0:4], scalar1=-1.0, scalar2=None, op0=ALU.mult),
           reads=[cg], writes=[cg])
        yield
        op("act", lambda e: e.activation(out=E.ap, in_=cg.ap[:, 0:12], func=AF.Exp), reads=[cg], writes=[E])
        op("dve", lambda e: e.tensor_scalar(out=nec.ap, in0=E.ap[:, 0:4], scalar1=-1.0, scalar2=None, op0=ALU.mult),
           reads=[E], writes=[nec])
        yield
        pbv = tr4(Vf)
        op("act", lambda e: e.copy(out=Vt.ap, in_=pb4(pbv)), reads=[pbv], writes=[Vt])
        yield
        pbk = tr4(Kf)
        op("dve", lambda e: e.tensor_tensor(out=KD.ap, in0=pb4(pbk), in1=bfree(E.ap[:, 4:8]), op=ALU.mult),
           reads=[pbk, E], writes=[KD])
        yield
        GM, DT, DTs = W_["GM"], W_["DT"], W_["DTs"]
        op("dve", lambda e: e.tensor_tensor(out=GM.ap, in0=bhead(mask.ap), in1=bfree(g_ap), op=ALU.mult), reads=[mask, Gd], writes=[GM])
        yield
        pd = nps()
        mm4(pd, lambda h: [(onesf.ap, GM.ap[:, h, :]), (identf.ap, mb.ap)], [onesf, GM, identf, mb])
        for h in range(4):
            op("act", lambda e, h=h: e.activation(out=DT.ap[:, h, :], in_=pd.ap[:, h * 128:(h + 1) * 128], func=AF.Exp,
                                                  bias=cg.ap[:, 12 + h:13 + h]), reads=[pd, cg], writes=[DT])
        yield
        op("pool", lambda e: e.tensor_tensor(out=DTs.ap, in0=DT.ap, in1=st4.ap, op=ALU.mult), reads=[DT, st4], writes=[DTs])
        yield
        NTb, AT, Mb, Yb = W_["NTb"], W_["AT"], W_["Mb"], W_["Yb"]
        pk = nps()
        mm4(pk, lambda h: [(Kf.ap[:, h, :], Kf.ap[:, h, :])], [Kf])
        for h in range(4):
            op("dve", lambda e, h=h: e.scalar_tensor_tensor(out=NTb.ap[:, h, :], in0=pk.ap[:, h * 128:(h + 1) * 128],
                                                            scalar=b_ap[:, h:h + 1], in1=DTs.ap[:, h, :], op0=ALU.mult, op1=ALU.mult),
               reads=[pk, Bd, DTs], writes=[NTb])
        yield
        if with_out:
            pq = nps()
            mm4(pq, lambda h: [(Kf.ap[:, h, :], Qf.ap[:, h, :])], [Kf, Qf])
            op("dve", lambda e: e.tensor_tensor(out=AT.ap, in0=v4(pq), in1=DT.ap, op=ALU.mult), reads=[pq, DT], writes=[AT])
            yield
        pbn = tr4(NTb)
        op("act", lambda e: e.copy(out=Mb.ap, in_=pb4(pbn)), reads=[pbn], writes=[Mb])
        yield
        N0, M0, W2b, YTb = W_["N0"], W_["M0"], W_["W2b"], W_["YTb"]
        op("pool", lambda e: e.tensor_tensor(out=N0.ap, in0=NTb.ap, in1=mk4["bm16"].ap, op=ALU.mult), reads=[NTb, mk4["bm16"]], writes=[N0])
        op("pool", lambda e: e.tensor_tensor(out=M0.ap, in0=Mb.ap, in1=mk4["bm16"].ap, op=ALU.mult), reads=[Mb, mk4["bm16"]], writes=[M0])
        op("pool", lambda e: e.tensor_tensor(out=Yb.ap, in0=I4.ap, in1=N0.ap, op=ALU.subtract), reads=[I4, N0], writes=[Yb])
        yield
        Pc, PTc = N0, M0
        nxt = [(W_["Pa"], W_["PTa"]), (W_["Pb"], W_["PTb"])]
        for lvl in range(3):
            Pn, PTn = nxt[lvl % 2]
            if lvl < 2:
                p1 = nps()
                mm4(p1, lambda h, Pc=Pc, PTc=PTc: [(PTc.ap[:, h, :], Pc.ap[:, h, :])], [Pc, PTc])
                op("act", lambda e, p1=p1, Pn=Pn: e.copy(out=Pn.ap, in_=v4(p1)), reads=[p1], writes=[Pn])
            p2 = nps()
            mm4(p2, lambda h, Pc=Pc, PTc=PTc: [(Pc.ap[:, h, :], PTc.ap[:, h, :])], [Pc, PTc])
            op("act", lambda e, p2=p2, PTn=PTn: e.copy(out=PTn.ap, in_=v4(p2)), reads=[p2], writes=[PTn])
            yield
            p3 = nps()
            mm4(p3, lambda h, PTn=PTn: [(PTn.ap[:, h, :], Yb.ap[:, h, :])], [PTn, Yb])
            op("dve", lambda e, p3=p3: e.tensor_tensor(out=Yb.ap, in0=v4(p3), in1=Yb.ap, op=ALU.add), reads=[p3, Yb], writes=[Yb])
            yield
            Pc, PTc = Pn, PTn
        for offk in ("off16", "off32", "off64"):
            pw = nps()
            mm4(pw, lambda h: [(Mb.ap[:, h, :], Yb.ap[:, h, :])], [Mb, Yb])
            pby = tr4(Yb)
            op("dve", lambda e, pw=pw, offk=offk: e.tensor_tensor(out=W2b.ap, in0=v4(pw), in1=mk4[offk].ap, op=ALU.mult),
               reads=[pw, mk4[offk]], writes=[W2b])
            op("act", lambda e, pby=pby: e.copy(out=YTb.ap, in_=pb4(pby)), reads=[pby], writes=[YTb])
            yield
            py = nps()
            mm4(py, lambda h: [(YTb.ap[:, h, :], W2b.ap[:, h, :])], [YTb, W2b])
            op("dve", lambda e, py=py: e.tensor_tensor(out=Yb.ap, in0=Yb.ap, in1=v4(py), op=ALU.subtract), reads=[py, Yb], writes=[Yb])
            yield

    def scan_rec(d, n, par, with_out, Bd, odst):
        W_ = WS[d][par]
        R_ = RS[d]
        Kf, Qf, Vt, KD, Yb, AT, E, nec = W_["Kf"], W_["Qf"], W_["Vt"], W_["KD"], W_["Yb"], W_["AT"], W_["E"], W_["nec"]
        R3, VN, OI, O = R_["R3"], R_["VN"], R_["OI"], R_["O"]
        b_ap = Bd.ap[:, n, d * 4:(d + 1) * 4]
        Sv = S.ap[:, d * 4:(d + 1) * 4, :]
        Sbv = Sb.ap[:, d * 4:(d + 1) * 4, :]
        pks = nps()
        mm4(pks, lambda h: [(Kf.ap[:, h, :], Sbv[:, h, :])], [Kf, Sbd[d]])
        if with_out:
            pqs = nps()
            mm4(pqs, lambda h: [(Qf.ap[:, h, :], Sbv[:, h, :])], [Qf, Sbd[d]])
            for h in range(4):
                op("act", lambda e, h=h: e.activation(out=OI.ap[:, h, :], in_=pqs.ap[:, h * 128:(h + 1) * 128], func=AF.Copy,
                                                      scale=E.ap[:, h:h + 1]), reads=[pqs, E], writes=[OI])
        for h in range(4):
            op("dve", lambda e, h=h: e.scalar_tensor_tensor(out=R3.ap[:, h, :], in0=pks.ap[:, h * 128:(h + 1) * 128],
                                                            scalar=nec.ap[:, h:h + 1], in1=Vt.ap[:, h, :], op0=ALU.mult, op1=ALU.add),
               reads=[pks, nec, Vt], writes=[R3])
        yield
        pv = nps()
        mm4(pv, lambda h: [(Yb.ap[:, h, :], R3.ap[:, h, :])], [Yb, R3])
        for h in range(4):
            op("act", lambda e, h=h: e.activation(out=VN.ap[:, h, :], in_=pv.ap[:, h * 128:(h + 1) * 128], func=AF.Copy,
                                                  scale=b_ap[:, h:h + 1]), reads=[pv, Bd], writes=[VN])
        yield
        pds = nps()
        mm4(pds, lambda h: [(KD.ap[:, h, :], VN.ap[:, h, :])], [KD, VN])
        for h in range(4):
            op("dve", lambda e, h=h: e.scalar_tensor_tensor(out=Sv[:, h, :], in0=Sv[:, h, :], scalar=E.ap[:, 8 + h:9 + h],
                                                            in1=pds.ap[:, h * 128:(h + 1) * 128], op0=ALU.mult, op1=ALU.add),
               reads=[Sd[d], E, pds], writes=[Sd[d]])
        op("act", lambda e: e.copy(out=Sbv, in_=Sv), reads=[Sd[d]], writes=[Sbd[d]])
        yield
        if with_out:
            pav = nps()
            mm4(pav, lambda h: [(AT.ap[:, h, :], VN.ap[:, h, :])], [AT, VN])
            op("dve", lambda e: e.tensor_tensor(out=O.ap, in0=v4(pav), in1=OI.ap, op=ALU.add), reads=[pav, OI], writes=[O])
            P.dma("sp", odst[n], O.ap.rearrange("p h t -> p (h t)"), reads=[O])
            yield

    NPAR = 3

    def scan_pass(nch, kd_, vd_, qd_, Gd, Bd):
        with_out = qd_ is not None
        chunk = [lambda r: r, lambda r: nch - 1 - r]
        odst = [of_s, ob_s]
        pre_done = [set(), set()]
        rec_done = [set(), set()]
        pre_started = [0, 0]
        rec_started = [0, 0]
        active = []
        while True:
            for d in range(2):
                r = pre_started[d]
                if r < nch and (r < NPAR or (r - NPAR) in rec_done[d]):
                    active.append((scan_pre(d, chunk[d](r), r % NPAR, kd_, vd_, qd_, Gd, Bd), "pre", d, r))
                    pre_started[d] += 1
                r = rec_started[d]
                if r < nch and r in pre_done[d] and (r == 0 or (r - 1) in rec_done[d]):
                    active.append((scan_rec(d, chunk[d](r), r % NPAR, with_out, Bd, odst[d]), "rec", d, r))
                    rec_started[d] += 1
            if not active:
                break
            for item in list(active):
                g_, kind, d, r = item
                try:
                    next(g_)
                except StopIteration:
                    active.remove(item)
                    (pre_done if kind == "pre" else rec_done)[d].add(r)

    scan_pass(TC // CH, kT_c, vT_c, None, Gc, Bc)
    if stop_after <= 4:
        return finish(P, nc, out, ar)
    scan_pass(NCH_DEBUG or T // CH, kT_s, vT_s, qT_s, Gm, Bt)
    if stop_after <= 5:
        return finish(P, nc, out, ar)

    P.barrier()
    ar.reset()
    w_out_bf = ar.alloc([8, D], BF16, "w_out_bf")
    P.dma("sp", w_out_bf.ap, wout_s, writes=[w_out_bf])
    xt = [ar.alloc([D], F32, "xtB%d" % i) for i in range(2)]
    xs = [ar.alloc([D], BF16, "xsB%d" % i) for i in range(2)]
    junk = ar.alloc([D], BF16, "junkB")
    stt = [ar.alloc([4], F32, "sttB%d" % i) for i in range(4)]
    hxT = [ar.alloc([8, 512], BF16, "hx2T%d" % i) for i in range(2)]
    zt = [ar.alloc([4, 512], F32, "zt%d" % i) for i in range(2)]
    mixT = [ar.alloc([8, 512], BF16, "mixT%d" % i) for i in range(2)]
    oft = [ar.alloc([4, 128], F32, "oft%d" % i) for i in range(2)]
    obt = [ar.alloc([4, 128], F32, "obt%d" % i) for i in range(2)]
    onb = [ar.alloc([4, 128], BF16, "onb%d" % i) for i in range(2)]
    ost = [ar.alloc([12], F32, "ost%d" % i) for i in range(2)]
    x1t = [ar.alloc([D], F32, "x1t%d" % i) for i in range(2)]
    tmpt = [ar.alloc([D], F32, "tmpt%d" % i) for i in range(2)]
    yst = [ar.alloc([4], F32, "yst%d" % i) for i in range(2)]
    junkf = ar.alloc([512], F32, "junkf")
    zv = zT_s.rearrange("h p t -> p h t")
    pv_ = poolT_s.rearrange("h p t -> p h t")
    h2v = hx2T_s.rearrange("k p t -> p k t")
    c3 = {"c": 0, "s": 0}

    def chainC(t):
        tsl = slice(t * 512, (t + 1) * 512)
        z_ = zt[t % 2]
        mx = mixT[t % 2]
        P.dma("sp", z_.ap, zv[:, :, tsl], writes=[z_])
        P.dma("sp", mx.ap[:, 4:8, :], pv_[:, :, tsl], writes=[mx])
        op("act", lambda e: e.activation(out=z_.ap, in_=z_.ap, func=AF.Silu), reads=[z_], writes=[z_])
        yield
        for c_ in range(4):
            n = t * 4 + c_
            c3["c"] += 1
            ci = c3["c"]
            of_, ob_, on_, os_ = oft[ci % 2], obt[ci % 2], onb[ci % 2], ost[ci % 2]
            P.dma("sp", of_.ap.rearrange("p h t -> p (h t)"), of_s[n], writes=[of_])
            P.dma("sp", ob_.ap.rearrange("p h t -> p (h t)"), ob_s[n], writes=[ob_])
            op("dve", lambda e, of_=of_, ob_=ob_: e.tensor_tensor(out=of_.ap, in0=of_.ap, in1=ob_.ap, op=ALU.add),
               reads=[of_, ob_], writes=[of_])
            op("act", lambda e, os_=os_: e.memzero(os_.ap), writes=[os_])
            for h in range(4):
                op("act", lambda e, of_=of_, os_=os_, h=h, jk=junkf: e.activation(out=jk.ap[:, 0:128], in_=of_.ap[:, h, :], func=AF.Square,
                                                                                 accum_out=os_.ap[:, h:h + 1]),
                   reads=[of_, os_], writes=[junkf, os_])
            op("act", lambda e, os_=os_: e.activation(out=os_.ap[:, 4:8], in_=os_.ap[:, 0:4], func=AF.Sqrt, scale=1.0 / 128, bias=eps1.ap),
               reads=[os_, eps1], writes=[os_])
            op("dve", lambda e, os_=os_: e.reciprocal(out=os_.ap[:, 8:12], in_=os_.ap[:, 4:8]), reads=[os_], writes=[os_])
            yield
            for h in range(4):
                op("dve", lambda e, of_=of_, os_=os_, on_=on_, h=h: e.tensor_scalar(
                    out=on_.ap[:, h, :], in0=of_.ap[:, h, :], scalar1=os_.ap[:, 8 + h:9 + h], scalar2=None, op0=ALU.mult),
                   reads=[of_, os_], writes=[on_])
            pb = npb()

            def tro(e, pb=pb, on_=on_):
                ins = None
                for h in range(4):
                    ins = e.transpose(out=pb.ap[:, h * 128:(h + 1) * 128], in_=on_.ap[:, h, :], identity=identb.ap)
                return ins
            op("pe", tro, reads=[on_, identb], writes=[pb])
            op("dve", lambda e, pb=pb, c_=c_: e.scalar_tensor_tensor(
                out=mx.ap[:, 0:4, c_ * 128:(c_ + 1) * 128], in0=pb.ap[:, 0:512].rearrange("p (h t) -> p h t", h=4),
                scalar=onorm.ap[:, 0:1], in1=z_.ap[:, :, c_ * 128:(c_ + 1) * 128], op0=ALU.mult, op1=ALU.mult),
               reads=[pb, onorm, z_], writes=[mx])
            yield

    def chainS(t):
        tsl = slice(t * 512, (t + 1) * 512)
        mx = mixT[t % 2]
        hx = hxT[t % 2]
        st8 = {}

        def s1(sub):
            r0 = t * 512 + sub * 128
            c3["s"] += 1
            ci = c3["s"]
            xx, x1_, tm_, ys_ = xt[ci % 2], x1t[ci % 2], tmpt[ci % 2], yst[ci % 2]
            st8[sub] = (xx, x1_, tm_, r0)
            P.dma("sp", xx.ap, x[r0:r0 + 128, :], writes=[xx])
            pys = [nps(), nps()]
            for half in range(2):
                mm(pys[half], pys[half].ap, [(None, [(mx.ap[:, k, sub * 128:(sub + 1) * 128],
                                                       w_out_bf.ap[:, k, half * 512:(half + 1) * 512]) for k in range(8)])],
                   [mx, w_out_bf])
            op("act", lambda e: e.memzero(ys_.ap), writes=[ys_])
            for half in range(2):
                op("act", lambda e, half=half, p_=pys[half], jk=junkf: e.activation(
                    out=jk.ap, in_=p_.ap, func=AF.Square, accum_out=ys_.ap[:, half:half + 1]),
                   reads=[pys[half], ys_], writes=[junkf, ys_])
            op("dve", lambda e: e.tensor_tensor(out=ys_.ap[:, 2:3], in0=ys_.ap[:, 0:1], in1=ys_.ap[:, 1:2], op=ALU.add),
               reads=[ys_], writes=[ys_])
            op("act", lambda e: e.activation(out=ys_.ap[:, 3:4], in_=ys_.ap[:, 2:3], func=AF.Sqrt, scale=1.0 / D, bias=eps1.ap),
               reads=[ys_, eps1], writes=[ys_])
            op("dve", lambda e: e.reciprocal(out=ys_.ap[:, 3:4], in_=ys_.ap[:, 3:4]), reads=[ys_], writes=[ys_])
            for half in range(2):
                hs = slice(half * 512, (half + 1) * 512)
                op("dve", lambda e, hs=hs, p_=pys[half]: e.scalar_tensor_tensor(
                    out=tm_.ap[:, hs], in0=p_.ap, scalar=ys_.ap[:, 3:4], in1=gt1g.ap[:, hs], op0=ALU.mult, op1=ALU.mult),
                   reads=[pys[half], ys_, gt1g], writes=[tm_])

        def s2(sub):
            xx, x1_, tm_, r0 = st8[sub]
            op("dve", lambda e: e.tensor_tensor(out=x1_.ap, in0=tm_.ap, in1=xx.ap, op=ALU.add), reads=[tm_, xx], writes=[x1_])
            P.dma("sp", x1_s[r0:r0 + 128, :], x1_.ap, reads=[x1_])
            cnt["x"] += 1
            xsb = xs[cnt["x"] % 2]
            st = rot(stt, "st")
            op("act", lambda e: e.memzero(st.ap), writes=[st])
            op("act", lambda e, jk=junk: e.activation(out=jk.ap, in_=x1_.ap, func=AF.Square, accum_out=st.ap[:, 0:1]),
               reads=[x1_, st], writes=[junk, st])
            op("act", lambda e: e.activation(out=st.ap[:, 1:2], in_=st.ap[:, 0:1], func=AF.Sqrt, scale=1.0 / D, bias=eps1.ap),
               reads=[st, eps1], writes=[st])
            op("dve", lambda e: e.reciprocal(out=st.ap[:, 2:3], in_=st.ap[:, 1:2]), reads=[st], writes=[st])
            op("dve", lambda e: e.tensor_scalar(out=xsb.ap, in0=x1_.ap, scalar1=st.ap[:, 2:3], scalar2=None, op0=ALU.mult),
               reads=[x1_, st], writes=[xsb])
            st8[sub] = (xsb,)

        def s3(sub):
            (xsb,) = st8[sub]
            pb = npb()

            def tr(e):
                ins = None
                for kc in range(8):
                    ins = e.transpose(out=pb.ap[:, kc * 128:(kc + 1) * 128], in_=xsb.ap[:, kc * 128:(kc + 1) * 128],
                                      identity=identb.ap)
                return ins
            op("pe", tr, reads=[xsb, identb], writes=[pb])
            for kc in range(8):
                dst = hx.ap[:, kc, sub * 128:(sub + 1) * 128]
                srcp = pb.ap[:, kc * 128:(kc + 1) * 128]
                if kc % 2 == 0:
                    op("act", lambda e, dst=dst, srcp=srcp, kc=kc: e.activation(
                        out=dst, in_=srcp, func=AF.Identity, scale=A2.ap[:, kc:kc + 1], bias=B2.ap[:, kc:kc + 1]),
                       reads=[pb, A2, B2], writes=[hx])
                else:
                    op("dve", lambda e, dst=dst, srcp=srcp, kc=kc: e.tensor_scalar(
                        out=dst, in0=srcp, scalar1=A2.ap[:, kc:kc + 1], scalar2=B2.ap[:, kc:kc + 1], op0=ALU.mult, op1=ALU.add),
                       reads=[pb, A2, B2], writes=[hx])

        for slot in range(6):
            if slot < 4:
                s1(slot)
            if 1 <= slot < 5:
                s2(slot - 1)
            if slot >= 2:
                s3(slot - 2)
            yield
        P.dma("sp", h2v[:, :, tsl], hx.ap, reads=[hx])
        op("pool", lambda e: e.tensor_copy(out=halp.ap[:, t, :, 0:1], in_=hx.ap[:, :, 0:1]), reads=[hx], writes=[halp])
        op("pool", lambda e: e.tensor_copy(out=halp.ap[:, t, :, 1:2], in_=hx.ap[:, :, 511:512]), reads=[hx], writes=[halp])
        yield

    interleave([chainC(0)])
    for t in range(T // 512):
        gens = [chainS(t)]
        if t + 1 < T // 512:
            gens.append(chainC(t + 1))
        interleave(gens)
    if stop_after <= 6:
        return finish(P, nc, out, ar)

    P.barrier()
    ar.reset()
    w_up_bf = ar.alloc([8, 2 * DFF], BF16, "w_up_bf")
    wub = [Buf("wup%d" % i) for i in range(4)]
    for pi in (0, 1, 2, 3):
        c0 = pi * 1408
        P.dma("sp", w_up_bf.ap[:, :, c0:c0 + 1408], wup_s[:, :, c0:c0 + 1408], writes=[wub[pi]])
    hxf = [ar.alloc([8, 512], BF16, "hxf%d" % i) for i in range(2)]
    hal = [ar.alloc([8, 2], BF16, "hal%d" % i) for i in range(2)]
    hg = ar.alloc([NFC, 2], F32, "hg")
    fT = ar.alloc([NFC, 512], BF16, "fT")
    ut = [ar.alloc([512], BF16, "ut%d" % i) for i in range(3)]
    gstg = [ar.alloc([514], F32, "gstg%d" % i) for i in range(3)]
    facc = [ar.alloc([512], F32, "facc%d" % i) for i in range(3)]
    wdt = [ar.alloc([D], BF16, "wdt%d" % i) for i in range(3)]
    dacc = [ar.alloc([2, D], F32, "dacc%d" % i) for i in range(2)]
    x1t = [ar.alloc([D], F32, "x1f%d" % i) for i in range(2)]
    yst = [ar.alloc([4], F32, "ystf%d" % i) for i in range(4)]
    junkf = ar.alloc([512], BF16, "junkff")
    psacc = [psf[1], psf[2], psf[3], psf[4]]
    cnt3 = {"f": 0, "w": 0, "e": 0, "p": 0}

    def epilogue(t, pair, da):
        for s2 in range(2):
            sub = pair * 2 + s2
            r0 = t * 512 + sub * 128
            cnt3["e"] += 1
            x1_, ys_ = x1t[cnt3["e"] % 2], yst[cnt3["e"] % 4]
            dv_ = da.ap[:, s2, :]
            P.dma("pool", x1_.ap, x1_s[r0:r0 + 128, :], writes=[x1_])
            op("act", lambda e, ys_=ys_: e.memzero(ys_.ap), writes=[ys_])
            for half in range(2):
                hs = slice(half * 512, (half + 1) * 512)
                op("act", lambda e, half=half, hs=hs, ys_=ys_, dv_=dv_, jk=junkf: e.activation(
                    out=jk.ap, in_=dv_[:, hs], func=AF.Square, accum_out=ys_.ap[:, half:half + 1]),
                   reads=[da, ys_], writes=[junkf, ys_])
            yield
            op("pool", lambda e, ys_=ys_: e.tensor_tensor(out=ys_.ap[:, 2:3], in0=ys_.ap[:, 0:1], in1=ys_.ap[:, 1:2], op=ALU.add),
               reads=[ys_], writes=[ys_])
            op("act", lambda e, ys_=ys_: e.activation(out=ys_.ap[:, 3:4], in_=ys_.ap[:, 2:3], func=AF.Sqrt, scale=1.0 / D, bias=eps1.ap),
               reads=[ys_, eps1], writes=[ys_])
            op("dve", lambda e, ys_=ys_: e.reciprocal(out=ys_.ap[:, 3:4], in_=ys_.ap[:, 3:4]), reads=[ys_], writes=[ys_])
            yield
            op("dve", lambda e, ys_=ys_, dv_=dv_: e.scalar_tensor_tensor(
                out=dv_, in0=dv_, scalar=ys_.ap[:, 3:4], in1=gt2g.ap, op0=ALU.mult, op1=ALU.mult),
               reads=[da, ys_, gt2g], writes=[da])
            yield
            op("pool", lambda e, x1_=x1_, dv_=dv_: e.tensor_tensor(out=dv_, in0=dv_, in1=x1_.ap, op=ALU.add),
               reads=[da, x1_], writes=[da])
            P.dma("pool", out[r0:r0 + 128, :], dv_, reads=[da])
            yield

    fTb = [Buf("fT%d" % j) for j in range(NFC)]

    def down_chunk(pair, j):
        cnt3["w"] += 1
        wt = wdt[cnt3["w"] % 3]
        P.dma("sp", wt.ap, wd_s[j], writes=[wt])
        for a in range(4):
            sub, half = pair * 2 + a // 2, a % 2
            acc_ = psacc[a]

            def fn(e, acc_=acc_, wt=wt, j=j, sub=sub, half=half):
                return e.matmul(acc_.ap, lhsT=fT.ap[:, j, sub * 128:(sub + 1) * 128], rhs=wt.ap[:, half * 512:(half + 1) * 512],
                                start=(j == 0), stop=(j == NFC - 1))
            op("pe", fn, reads=[fTb[j], wt] + ([acc_] if j > 0 else []), writes=[acc_])

    def evac_pair(pair):
        cnt3["p"] += 1
        da = dacc[cnt3["p"] % 2]
        for a in range(4):
            s2, half = a // 2, a % 2
            op("act", lambda e, a=a, s2=s2, half=half, da=da: e.copy(out=da.ap[:, s2, half * 512:(half + 1) * 512], in_=psacc[a].ap),
               reads=[psacc[a]], writes=[da])
        pending.append(epilogue(cur_t[0], pair, da))

    cur_t = [0]
    pending = []

    def pump(n=1):
        for _ in range(n):
            if not pending:
                return
            try:
                next(pending[0])
            except StopIteration:
                pending.pop(0)

    P.dma("sp", hxf[0].ap, h2v[:, :, 0:512], writes=[hxf[0]])
    for t in range(T // 512):
        tsl = slice(t * 512, (t + 1) * 512)
        cur_t[0] = t
        hx = hxf[t % 2]
        hl = hal[t % 2]
        if t + 1 < T // 512:
            P.dma("sp", hxf[(t + 1) % 2].ap, h2v[:, :, (t + 1) * 512:(t + 2) * 512], writes=[hxf[(t + 1) % 2]])
        op("pool", lambda e, hl=hl: e.memset(hl.ap, 0.0), writes=[hl])
        if t > 0:
            op("pool", lambda e, hl=hl, t=t: e.tensor_copy(out=hl.ap[:, :, 0:1], in_=halp.ap[:, t - 1, :, 1:2]), reads=[halp], writes=[hl])
        if t < T // 512 - 1:
            op("pool", lambda e, hl=hl, t=t: e.tensor_copy(out=hl.ap[:, :, 1:2], in_=halp.ap[:, t + 1, :, 0:1]), reads=[halp], writes=[hl])
        ph = psf[0]
        mm(ph, None, [(ph.ap[:, j * 2:j * 2 + 2], [(w_up_bf.ap[:, kc, j * 128:(j + 1) * 128], hl.ap[:, kc, :]) for kc in range(8)])
                      for j in range(NFC)], [wub[0], wub[1], hl])
        op("dve", lambda e: e.tensor_copy(out=hg.ap, in_=ph.ap[:, 0:2 * NFC].rearrange("p (j c) -> p j c", c=2)),
           reads=[ph], writes=[hg])
        def ffn_tail(j, gs, fa, u_):
            op("dve", lambda e: e.tensor_scalar(out=fa.ap, in0=gs.ap[:, 0:512], scalar1=cwf.ap[:, j:j + 1],
                                                scalar2=None, op0=ALU.mult), reads=[gs, cwf], writes=[fa])
            for tap in (1, 2):
                op("dve", lambda e, tap=tap: e.scalar_tensor_tensor(
                    out=fa.ap, in0=gs.ap[:, tap:tap + 512], scalar=cwf.ap[:, tap * NFC + j:tap * NFC + j + 1], in1=fa.ap,
                    op0=ALU.mult, op1=ALU.add), reads=[gs, cwf, fa], writes=[fa])
            op("act", lambda e: e.activation(out=fa.ap, in_=fa.ap, func=AF.Silu), reads=[fa], writes=[fa])
            op("dve", lambda e: e.tensor_tensor(out=fT.ap[:, j, :], in0=fa.ap, in1=u_.ap, op=ALU.mult),
               reads=[fa, u_], writes=[fTb[j]])

        prevf = None
        dq_ = []
        for j in list(range(NFC)) + [None]:
            if j is not None:
                cnt3["f"] += 1
                fi = cnt3["f"]
                gs, fa, u_ = gstg[fi % 3], facc[fi % 3], ut[fi % 3]
                pg = nps_ffn()
                mm(pg, pg.ap, [(None, [(w_up_bf.ap[:, kc, j * 128:(j + 1) * 128], hx.ap[:, kc, :]) for kc in range(8)])],
                   [wub[(j * 128) // 1408], hx])
                op("act", lambda e, gs=gs, pg=pg: e.copy(out=gs.ap[:, 1:513], in_=pg.ap), reads=[pg], writes=[gs])
                pu = nps_ffn()
                mm(pu, pu.ap, [(None, [(w_up_bf.ap[:, kc, DFF + j * 128:DFF + (j + 1) * 128], hx.ap[:, kc, :]) for kc in range(8)])],
                   [wub[(DFF + j * 128) // 1408], hx])
                op("act", lambda e, u_=u_, pu=pu: e.copy(out=u_.ap, in_=pu.ap), reads=[pu], writes=[u_])
                op("pool", lambda e, gs=gs, j=j: e.tensor_copy(out=gs.ap[:, 0:1], in_=hg.ap[:, j, 0:1]), reads=[hg], writes=[gs])
                op("pool", lambda e, gs=gs, j=j: e.tensor_copy(out=gs.ap[:, 513:514], in_=hg.ap[:, j, 1:2]), reads=[hg], writes=[gs])
            if prevf is not None:
                ffn_tail(*prevf)
                dq_.append(prevf[0])
            if len(dq_) > 2:
                down_chunk(0, dq_.pop(0))
            prevf = (j, gs, fa, u_) if j is not None else None
            pump(1)
        while dq_:
            down_chunk(0, dq_.pop(0))
        evac_pair(0)
        while pending:
            pump(1)
        for j in range(NFC):
            down_chunk(1, j)
        evac_pair(1)
    while pending:
        pump(1)

    return finish(P, nc, out, ar)


def finish(P, nc, out, ar):
    P.barrier()
    if DEBUG:
        d2, Gm, Bt, Gc, Bc, S = ar.dbg
        P.dma("sp", d2[:, 0:256], Gm.ap.rearrange("p a b -> p (a b)"), reads=[Gm])
        P.dma("sp", d2[:, 256:512], Bt.ap.rearrange("p a b -> p (a b)"), reads=[Bt])
        P.dma("sp", d2[:, 512:528], Gc.ap.rearrange("p a b -> p (a b)"), reads=[Gc])
        P.dma("sp", d2[:, 528:544], Bc.ap.rearrange("p a b -> p (a b)"), reads=[Bc])
        P.dma("sp", d2[:, 1024:2048], S.ap.rearrange("p a b -> p (a b)"), reads=[S])
        P.barrier()
    P.emit()
    return nc


_CACHE = {}


def kernel(**inputs):
    n = 8
    if "nc" not in _CACHE:
        _CACHE["nc"] = build_program()
    nc = _CACHE["nc"]
    f32 = np.float32
    shared = {}
    for k in ("w_mod", "b_mod", "g_pre_mix", "g_post_mix", "g_pre_ffn", "g_post_ffn", "w_in", "conv_qkv",
              "a_log", "dt_bias", "o_norm", "pool_w", "pool_scale", "w_out", "w_up", "conv_ffn", "w_down"):
        a = np.ascontiguousarray(np.asarray(inputs[k], dtype=f32)[0])
        if k in ("a_log", "dt_bias"):
            a = a.reshape(8)
        shared[k] = a
    x = np.asarray(inputs["x"], dtype=f32)
    ctx = np.asarray(inputs["ctx"], dtype=f32)
    c = np.asarray(inputs["c"], dtype=f32)
    c_ctx = np.asarray(inputs["c_ctx"], dtype=f32)
    in_maps = []
    for b in range(n):
        m = dict(shared)
        m["x"] = np.ascontiguousarray(x[b])
        m["ctx"] = np.ascontiguousarray(ctx[b])
        m["cc"] = np.ascontiguousarray(np.stack([c[b], c_ctx], axis=0))
        in_maps.append(m)
    res = run_bass_kernel_spmd(nc, in_maps, core_ids=list(range(n)))
    _CACHE["res"] = res
    return np.stack([np.asarray(r["out"], dtype=f32) for r in res.results], axis=0)
```

```python
import contextlib
import numpy as np
import concourse.bass as bass
import concourse.mybir as mybir
from concourse.bass_utils import run_bass_kernel_spmd

F32 = mybir.dt.float32
BF16 = mybir.dt.bfloat16
AF = mybir.ActivationFunctionType
ALU = mybir.AluOpType

D = 1024
T = 4096
TC = 256
INC = 2576
DFF = 2816
NFC = 22
CH = 128
EPS = 1e-6
DEBUG = False
NCH_DEBUG = 0


class Buf:
    __slots__ = ("name", "w", "r")

    def __init__(self, name=""):
        self.name = name
        self.w = None
        self.r = []


class TL:
    __slots__ = ("ap", "b")

    def __init__(self, ap, name=""):
        self.ap = ap
        self.b = Buf(name)


class Prog:
    ENGS = ("pe", "dve", "act", "pool", "sp")

    def __init__(self, nc, n_dma_sems=16):
        self.nc = nc
        self.stack = contextlib.ExitStack()
        self.ops = {e: [] for e in self.ENGS}
        self.sems = {}
        self.n_dma = n_dma_sems
        self.gen = -1
        self.n_inst = 0
        self._new_gen()

    def _new_gen(self):
        self.gen += 1
        g = self.gen
        self.count = {e: 0 for e in self.ENGS}
        old_seen = getattr(self, "seen", None)
        self.seen = {e: {} for e in self.ENGS}
        for e in ("pe", "dve", "act", "pool"):
            self.sems[(e, g)] = self.stack.enter_context(self.nc.semaphore("s_%s_%d" % (e, g)))
        if g == 0:
            self.dma_cnt = {}
            self.dma_rr = {}
            self.dma_n = {"sp": self.n_dma, "act": 8, "pool": 8}
            for q, nq in self.dma_n.items():
                self.dma_rr[q] = 0
                for k in range(nq):
                    key = ("d_%s%d" % (q, k), -1)
                    self.sems[key] = self.stack.enter_context(self.nc.semaphore("d_%s%d" % (q, k)))
                    self.dma_cnt[key] = 0
        else:
            for e in self.ENGS:
                for key, v in old_seen[e].items():
                    if key[1] == -1:
                        self.seen[e][key] = v

    def sbuf(self, name, shape, dtype):
        return self.stack.enter_context(self.nc.sbuf_tensor(name, list(shape), dtype))

    def psum(self, name, shape, dtype):
        return self.stack.enter_context(self.nc.psum_tensor(name, list(shape), dtype))

    def _deps(self, eng, reads, writes):
        deps = {}

        def add(tok):
            if tok is None:
                return
            k, v = tok
            if k[1] != self.gen and k[1] != -1:
                return
            if deps.get(k, 0) < v:
                deps[k] = v
        for b in reads:
            add(b.w)
        for b in writes:
            add(b.w)
            for t in b.r:
                add(t)
        waits = []
        seen = self.seen[eng]
        for k, v in deps.items():
            if seen.get(k, 0) < v:
                seen[k] = v
                waits.append((k, v))
        return waits

    def _commit(self, tok, reads, writes):
        for b in reads:
            b.r.append(tok)
            if len(b.r) > 64:
                best = {}
                for k, v in b.r:
                    if best.get(k, 0) < v:
                        best[k] = v
                b.r = list(best.items())
        for b in writes:
            b.w = tok
            b.r = []

    def op(self, eng, fn, reads=(), writes=()):
        reads = [t.b if isinstance(t, TL) else t for t in reads]
        writes = [t.b if isinstance(t, TL) else t for t in writes]
        waits = self._deps(eng, reads, writes)
        self.count[eng] += 1
        tok = ((eng, self.gen), self.count[eng])
        self.ops[eng].append((waits, fn, ((eng, self.gen), 1)))
        self._commit(tok, reads, writes)
        self.n_inst += 1
        return tok

    def dma(self, q, out, in_, reads=(), writes=(), **kw):
        reads = [t.b if isinstance(t, TL) else t for t in reads]
        writes = [t.b if isinstance(t, TL) else t for t in writes]
        k = self.dma_rr[q]
        self.dma_rr[q] = (k + 1) % self.dma_n[q]
        key = ("d_%s%d" % (q, k), -1)
        waits = self._deps(q, reads, writes)
        prev = self.dma_cnt[key]
        if prev > 0 and self.seen[q].get(key, 0) < prev:
            self.seen[q][key] = prev
            waits.append((key, prev))
        self.dma_cnt[key] = prev + 16
        tok = (key, prev + 16)

        def fn(e, out=out, in_=in_, kw=kw):
            return e.dma_start(out=out, in_=in_, **kw)
        self.ops[q].append((waits, fn, (key, 16)))
        self._commit(tok, reads, writes)
        self.n_inst += 1
        return tok

    def barrier(self):
        for e in self.ENGS:
            waits = []
            for k in ("pe", "dve", "act", "pool"):
                v = self.count[k]
                key = (k, self.gen)
                if k != e and v > 0 and self.seen[e].get(key, 0) < v:
                    self.seen[e][key] = v
                    waits.append((key, v))
            for key, v in self.dma_cnt.items():
                if v > 0 and self.seen[e].get(key, 0) < v:
                    self.seen[e][key] = v
                    waits.append((key, v))
            if waits:
                self.ops[e].append((waits, None, None))
        self._new_gen()

    def emit(self):
        nc = self.nc
        sems = self.sems

        def replay(eng_name, e):
            for waits, fn, inc in self.ops[eng_name]:
                for k, v in waits:
                    e.wait_ge(sems[k], v)
                if fn is not None:
                    ins = fn(e)
                    ins.then_inc(sems[inc[0]], inc[1])

        with nc.Block() as block:
            @block.tensor
            def _(e):
                replay("pe", e)

            @block.vector
            def _(e):
                replay("dve", e)

            @block.scalar
            def _(e):
                replay("act", e)

            @block.gpsimd
            def _(e):
                replay("pool", e)

            @block.sync
            def _(e):
                replay("sp", e)
        self.stack.close()


class Arena:
    def __init__(self, P, nbytes):
        self.cap = nbytes // 2
        self.t = P.sbuf("arena", [128, self.cap], BF16)
        self.off = 0

    def reset(self):
        self.off = 0

    def alloc(self, shape, dtype, name=""):
        n = 1
        for s in shape:
            n *= s
        w = n * (2 if dtype == F32 else 1)
        off = (self.off + 31) // 32 * 32
        assert off + w <= self.cap, ("arena overflow", name, off + w, self.cap)
        ap = self.t[:, off:off + w]
        self.off = off + w
        if dtype == F32:
            ap = ap.bitcast(F32)
        if len(shape) == 2:
            ap = ap.rearrange("p (a b) -> p a b", a=shape[0])
        elif len(shape) == 3:
            ap = ap.rearrange("p (a b c) -> p a b c", a=shape[0], b=shape[1])
        return TL(ap, name)


def build_program(stop_after=99):
    nc = bass.Bass("TRN2", target_bir_lowering=False)

    def din(name, shape):
        return nc.dram_tensor(name, list(shape), F32, kind="ExternalInput").ap()

    x = din("x", [T, D])
    ctx = din("ctx", [TC, D])
    cc = din("cc", [2, D])
    w_mod = din("w_mod", [D, 6 * D])
    b_mod = din("b_mod", [6 * D])
    g_pre_mix = din("g_pre_mix", [D])
    g_post_mix = din("g_post_mix", [D])
    g_pre_ffn = din("g_pre_ffn", [D])
    g_post_ffn = din("g_post_ffn", [D])
    w_in = din("w_in", [D, INC])
    conv_qkv = din("conv_qkv", [5, 1536])
    a_log = din("a_log", [8])
    dt_bias = din("dt_bias", [8])
    o_norm = din("o_norm", [128])
    pool_w = din("pool_w", [4, 128, 128])
    pool_scale = din("pool_scale", [512])
    w_out = din("w_out", [D, D])
    w_up = din("w_up", [D, 2 * DFF])
    conv_ffn = din("conv_ffn", [3, DFF])
    w_down = din("w_down", [DFF, D])
    out = nc.dram_tensor("out", [T, D], F32, kind="ExternalOutput").ap()

    skind = "ExternalOutput" if DEBUG else "Internal"

    def scr(name, shape, dt):
        return nc.dram_tensor(name, list(shape), dt, kind=skind).ap()

    qT_s = scr("qT_s", [4, 128, T], BF16)
    kT_s = scr("kT_s", [4, 128, T], BF16)
    vT_s = scr("vT_s", [4, 128, T], BF16)
    kT_c = scr("kT_c", [4, 128, TC], BF16)
    vT_c = scr("vT_c", [4, 128, TC], BF16)
    zT_s = scr("zT_s", [4, 128, T], F32)
    uT_s = scr("uT_s", [4, 128, T], F32)
    poolT_s = scr("poolT_s", [4, 128, T], BF16)
    of_s = scr("of_s", [T // CH, 128, 512], F32)
    ob_s = scr("ob_s", [T // CH, 128, 512], F32)
    x1_s = scr("x1_s", [T, D], F32)
    hx2T_s = scr("hx2T_s", [8, 128, T], BF16)
    wd_s = scr("wd_s", [NFC, 128, D], BF16)
    wout_s = scr("wout_s", [128, 8, D], BF16)
    wup_s = scr("wup_s", [128, 8, 2 * DFF], BF16)
    dbg_s = scr("dbg_s", [128, 4096], F32)
    dbg2_s = scr("dbg2_s", [128, 4096], F32)

    P = Prog(nc)
    op = P.op

    def pers(name, shape, dt):
        return TL(P.sbuf(name, [128] + list(shape), dt)[:], name)

    identf = pers("identf", [128], F32)
    identb = pers("identb", [128], BF16)
    onesf = pers("onesf", [128], F32)
    onesb = pers("onesb", [128], BF16)
    Lmask = pers("Lmask", [128], F32)
    Umask = pers("Umask", [128], F32)
    mbF = pers("mbF", [128], F32)
    mbB = pers("mbB", [128], F32)
    stF4 = pers("stF4", [4, 128], F32)
    stB4 = pers("stB4", [4, 128], F32)
    I4 = pers("I4", [4, 128], F32)
    eps1 = pers("eps1", [1], F32)
    epsq = pers("epsq", [1], F32)
    one1 = pers("one1", [1], F32)
    modfm = pers("modfm", [32, 2], F32)
    bmodfm = pers("bmodfm", [48], F32)
    gfm = pers("gfm", [16], F32)
    A1 = pers("A1", [8], F32)
    B1 = pers("B1", [8], F32)
    A1c = pers("A1c", [8], F32)
    B1c = pers("B1c", [8], F32)
    A2 = pers("A2", [8], F32)
    B2 = pers("B2", [8], F32)
    gt1g = pers("gt1g", [D], F32)
    gt2g = pers("gt2g", [D], F32)
    cw = pers("cw", [60], F32)
    cwf = pers("cwf", [66], F32)
    onorm = pers("onorm", [1], F32)
    pscale = pers("pscale", [4], F32)
    negA = pers("negA", [8], F32)
    dtb = pers("dtb", [8], F32)
    Gm = pers("Gm", [T // CH, 8], F32)
    Bt = pers("Bt", [T // CH, 8], F32)
    Gc = pers("Gc", [TC // CH, 8], F32)
    Bc = pers("Bc", [TC // CH, 8], F32)
    S = pers("S", [8, 128], F32)
    Sb = pers("Sb", [8, 128], BF16)
    halp = pers("halp", [T // 512, 8, 2], BF16)

    psf = [TL(P.psum("psf%d" % i, [128, 512], F32)[:], "psf%d" % i) for i in range(6)]
    psb = [TL(P.psum("psb%d" % i, [128, 1024], BF16)[:], "psb%d" % i) for i in range(2)]
    rr = {"f": 0, "b": 0}

    def nps():
        rr["f"] = rr["f"] % 5 + 1
        return psf[rr["f"]]

    def npb():
        rr["b"] = (rr["b"] + 1) % 2
        return psb[rr["b"]]

    ffn_banks = [psf[5], TL(psb[0].ap.bitcast(F32), "psb0f"), TL(psb[1].ap.bitcast(F32), "psb1f")]
    ffn_banks[1].b = psb[0].b
    ffn_banks[2].b = psb[1].b

    def nps_ffn():
        rr["b"] = (rr["b"] + 1) % 3
        return ffn_banks[rr["b"]]

    ar = Arena(P, 177152)
    if DEBUG:
        print("sbuf bytes remaining after arena:", nc.sbuf_bytes_remaining)
    ar.dbg = (dbg2_s, Gm, Bt, Gc, Bc, S)

    def mm(outT, out_ap, groups, reads):
        def fn(e, groups=groups, out_ap=out_ap):
            ins = None
            for oa, pairs in groups:
                n = len(pairs)
                for i, (l, r) in enumerate(pairs):
                    ins = e.matmul(oa if oa is not None else out_ap, lhsT=l, rhs=r, start=(i == 0), stop=(i == n - 1))
            return ins
        return op("pe", fn, reads=reads, writes=[outT])

    def sel(t, pattern, cm, cmp_op, fill=0.0):
        op("pool", lambda e: e.affine_select(out=t.ap, in_=t.ap, pattern=pattern, compare_op=cmp_op, fill=fill,
                                             base=0, channel_multiplier=cm), reads=[t], writes=[t])

    for t_ in (identf, onesf, Lmask, Umask):
        op("pool", lambda e, t_=t_: e.memset(t_.ap, 1.0), writes=[t_])
    sel(identf, [[-1, 128]], 1, ALU.is_equal)
    sel(Lmask, [[1, 128]], -1, ALU.is_ge)
    sel(Umask, [[-1, 128]], 1, ALU.is_ge)
    op("dve", lambda e: e.tensor_copy(out=identb.ap, in_=identf.ap), reads=[identf], writes=[identb])
    op("dve", lambda e: e.tensor_copy(out=onesb.ap, in_=onesf.ap), reads=[onesf], writes=[onesb])
    op("dve", lambda e: e.tensor_scalar(out=mbF.ap, in0=Lmask.ap, scalar1=-1.0, scalar2=1e5, op0=ALU.add, op1=ALU.mult),
       reads=[Lmask], writes=[mbF])
    op("dve", lambda e: e.tensor_scalar(out=mbB.ap, in0=Umask.ap, scalar1=-1.0, scalar2=1e5, op0=ALU.add, op1=ALU.mult),
       reads=[Umask], writes=[mbB])
    for h in range(4):
        op("dve", lambda e, h=h: e.tensor_tensor(out=stF4.ap[:, h, :], in0=Lmask.ap, in1=identf.ap, op=ALU.subtract),
           reads=[Lmask, identf], writes=[stF4])
        op("dve", lambda e, h=h: e.tensor_tensor(out=stB4.ap[:, h, :], in0=Umask.ap, in1=identf.ap, op=ALU.subtract),
           reads=[Umask, identf], writes=[stB4])
        op("dve", lambda e, h=h: e.tensor_copy(out=I4.ap[:, h, :], in_=identf.ap), reads=[identf], writes=[I4])
    blkm = {}
    for bsz in (16, 32, 64):
        E_ = ar.alloc([128], F32, "Eblk%d" % bsz)
        op("pool", lambda e, E_=E_: e.memset(E_.ap, 1.0), writes=[E_])
        op("pool", lambda e, E_=E_, bsz=bsz: e.affine_select(out=E_.ap, in_=E_.ap, pattern=[[1, 128]], compare_op=ALU.is_ge,
                                                             fill=0.0, base=0, channel_multiplier=-bsz), reads=[E_], writes=[E_])
        op("pool", lambda e, E_=E_, bsz=bsz: e.affine_select(out=E_.ap, in_=E_.ap, pattern=[[-1, 128]], compare_op=ALU.is_ge,
                                                             fill=0.0, base=bsz - 1, channel_multiplier=bsz), reads=[E_], writes=[E_])
        ps = nps()
        mm(ps, ps.ap[:, 0:128], [(None, [(E_.ap, E_.ap)])], [E_])
        blkm[bsz] = ar.alloc([128], F32, "blk%d" % bsz)
        op("dve", lambda e, ps=ps, bsz=bsz: e.tensor_copy(out=blkm[bsz].ap, in_=ps.ap[:, 0:128]), reads=[ps], writes=[blkm[bsz]])
    off16 = ar.alloc([128], F32, "off16")
    off32 = ar.alloc([128], F32, "off32")
    off64 = ar.alloc([128], F32, "off64")
    op("dve", lambda e: e.tensor_tensor(out=off16.ap, in0=blkm[32].ap, in1=blkm[16].ap, op=ALU.subtract), reads=[blkm[32], blkm[16]], writes=[off16])
    op("dve", lambda e: e.tensor_tensor(out=off32.ap, in0=blkm[64].ap, in1=blkm[32].ap, op=ALU.subtract), reads=[blkm[64], blkm[32]], writes=[off32])
    op("dve", lambda e: e.tensor_tensor(out=off64.ap, in0=onesf.ap, in1=blkm[64].ap, op=ALU.subtract), reads=[onesf, blkm[64]], writes=[off64])
    mk4 = {}
    for nm_, src_m in (("bm16", blkm[16]), ("off16", off16), ("off32", off32), ("off64", off64)):
        mk4[nm_] = pers(nm_ + "_4b", [4, 128], BF16)
        for h in range(4):
            op("dve", lambda e, nm_=nm_, src_m=src_m, h=h: e.tensor_copy(out=mk4[nm_].ap[:, h, :], in_=src_m.ap),
               reads=[src_m], writes=[mk4[nm_]])
    op("dve", lambda e: e.memset(eps1.ap, EPS), writes=[eps1])
    op("dve", lambda e: e.memset(epsq.ap, EPS * 128.0), writes=[epsq])
    op("dve", lambda e: e.memset(one1.ap, 1.0), writes=[one1])

    ldst = [ar.alloc([128], F32, "ldst%d" % i) for i in range(2)]
    ldi = [0]

    def load_fm(src2d, n, dst_ap, dstT):
        st = ldst[ldi[0] % 2]
        ldi[0] += 1
        P.dma("sp", st.ap[0:n, :], src2d, writes=[st])
        ps = nps()
        mm(ps, ps.ap[:, 0:n], [(None, [(st.ap[0:n, :], identf.ap[0:n, 0:n])])], [st, identf])
        op("dve", lambda e: e.tensor_copy(out=dst_ap, in_=ps.ap[:, 0:n]), reads=[ps], writes=[dstT])

    load_fm(b_mod.rearrange("(c p) -> c p", p=128), 48, bmodfm.ap, bmodfm)
    load_fm(g_pre_mix.rearrange("(c p) -> c p", p=128), 8, gfm.ap[:, 0:8], gfm)
    load_fm(g_pre_ffn.rearrange("(c p) -> c p", p=128), 8, gfm.ap[:, 8:16], gfm)
    load_fm(conv_qkv.rearrange("t (c p) -> (t c) p", p=128), 60, cw.ap, cw)
    load_fm(conv_ffn.rearrange("t (c p) -> (t c) p", p=128), 66, cwf.ap, cwf)
    load_fm(pool_scale.rearrange("(c p) -> c p", p=128), 4, pscale.ap, pscale)
    load_fm(o_norm.rearrange("(c p) -> c p", p=128), 1, onorm.ap, onorm)
    P.dma("sp", negA.ap, a_log.partition_broadcast(128), writes=[negA])
    P.dma("sp", dtb.ap, dt_bias.partition_broadcast(128), writes=[dtb])
    op("act", lambda e: e.activation(out=negA.ap, in_=negA.ap, func=AF.Exp), reads=[negA], writes=[negA])
    op("dve", lambda e: e.tensor_scalar(out=negA.ap, in0=negA.ap, scalar1=-1.0, scalar2=None, op0=ALU.mult),
       reads=[negA], writes=[negA])

    ccst = ar.alloc([128], F32, "ccst")
    sT = ar.alloc([8, 2], F32, "sT")
    sRep = ar.alloc([8, 128], F32, "sRep")
    wst = [ar.alloc([8, 512], F32, "wst%d" % i) for i in range(5)]
    browst = [ar.alloc([512], F32, "brow%d" % i) for i in range(2)]
    grow = [ar.alloc([512], F32, "grow%d" % i) for i in range(2)]
    P.dma("sp", ccst.ap[0:16, :], cc.rearrange("r (k p) -> (r k) p", p=128), writes=[ccst])
    ps = nps()
    mm(ps, ps.ap[:, 0:16], [(None, [(ccst.ap[0:16, :], identf.ap[0:16, 0:16])])], [ccst, identf])
    for r in range(2):
        op("act", lambda e, r=r, ps=ps: e.activation(out=sT.ap[:, :, r], in_=ps.ap[:, r * 8:(r + 1) * 8], func=AF.Silu),
           reads=[ps], writes=[sT])
    for kc in range(8):
        op("dve", lambda e, kc=kc: e.tensor_scalar(out=sRep.ap[:, kc, :], in0=onesf.ap, scalar1=sT.ap[:, kc, 0:1],
                                                   scalar2=None, op0=ALU.mult), reads=[onesf, sT], writes=[sRep])
    psm = psf[0]
    wm_v = w_mod.rearrange("(k p) n -> p k n", p=128)
    fmi = 0
    for hb in range(12):
        blk = hb // 2
        w = wst[hb % 5]
        P.dma("sp", w.ap, wm_v[:, :, hb * 512:(hb + 1) * 512], writes=[w])
        if blk in (2, 5):
            ps = nps()
            mm(ps, ps.ap, [(None, [(sRep.ap[:, kc, :], w.ap[:, kc, :]) for kc in range(8)])], [sRep, w])
            br = browst[hb % 2]
            gr = grow[hb % 2]
            dst = gt1g if blk == 2 else gt2g
            gsrc = g_post_mix if blk == 2 else g_post_ffn
            half = hb % 2
            P.dma("sp", br.ap, b_mod[hb * 512:(hb + 1) * 512].partition_broadcast(128), writes=[br])
            P.dma("sp", gr.ap, gsrc[half * 512:(half + 1) * 512].partition_broadcast(128), writes=[gr])
            op("dve", lambda e, ps=ps, br=br: e.tensor_tensor(out=br.ap, in0=ps.ap, in1=br.ap, op=ALU.add),
               reads=[ps, br], writes=[br])
            op("dve", lambda e, br=br, gr=gr, dst=dst, half=half: e.tensor_tensor(
                out=dst.ap[:, half * 512:(half + 1) * 512], in0=br.ap, in1=gr.ap, op=ALU.mult),
               reads=[br, gr], writes=[dst])
        else:
            groups = []
            for fc in range(4):
                groups.append((psm.ap[:, fmi * 2:fmi * 2 + 2],
                               [(w.ap[:, kc, fc * 128:(fc + 1) * 128], sT.ap[:, kc, :]) for kc in range(8)]))
                fmi += 1
            mm(psm, None, groups, [w, sT])
    op("dve", lambda e: e.tensor_copy(out=modfm.ap, in_=psm.ap[:, 0:64].rearrange("p (i r) -> p i r", r=2)),
       reads=[psm], writes=[modfm])
    bsel = {0: 0, 1: 8, 2: 24, 3: 32}

    def mk_ab(Adst, Bdst, r, sh_i, sc_i, g_off, bsh, bsc):
        op("dve", lambda e: e.tensor_tensor(out=Bdst.ap, in0=modfm.ap[:, sh_i:sh_i + 8, r], in1=bmodfm.ap[:, bsh:bsh + 8],
                                            op=ALU.add), reads=[modfm, bmodfm], writes=[Bdst])
        op("dve", lambda e: e.scalar_tensor_tensor(out=Adst.ap, in0=modfm.ap[:, sc_i:sc_i + 8, r], scalar=1.0,
                                                   in1=bmodfm.ap[:, bsc:bsc + 8], op0=ALU.add, op1=ALU.add),
           reads=[modfm, bmodfm], writes=[Adst])
        op("dve", lambda e: e.tensor_tensor(out=Adst.ap, in0=Adst.ap, in1=gfm.ap[:, g_off:g_off + 8], op=ALU.mult),
           reads=[Adst, gfm], writes=[Adst])

    mk_ab(A1, B1, 0, 0, 8, 0, 0, 8)
    mk_ab(A1c, B1c, 1, 0, 8, 0, 0, 8)
    mk_ab(A2, B2, 0, 16, 24, 8, 24, 32)

    if DEBUG:
        dbgt = ar.alloc([4096], F32, "dbgt")
        op("dve", lambda e: e.memset(dbgt.ap, 0.0), writes=[dbgt])
        for i, (t_, n) in enumerate(((A1, 8), (B1, 8), (A1c, 8), (B1c, 8), (A2, 8), (B2, 8), (cw, 60), (cwf, 66),
                                     (negA, 8), (dtb, 8), (pscale, 4), (onorm, 1))):
            op("dve", lambda e, t_=t_, n=n, i=i: e.tensor_copy(out=dbgt.ap[:, i * 128:i * 128 + n], in_=t_.ap),
               reads=[t_], writes=[dbgt])
        for i, t_ in enumerate((blkm[16], off16, off32, off64)):
            src_ap = t_.ap
            op("dve", lambda e, src_ap=src_ap, i=i: e.tensor_copy(out=dbgt.ap[:, 1536 + i * 128:1664 + i * 128], in_=src_ap),
               reads=[t_], writes=[dbgt])
        op("dve", lambda e: e.tensor_copy(out=dbgt.ap[:, 2048:3072], in_=gt1g.ap), reads=[gt1g], writes=[dbgt])
        op("dve", lambda e: e.tensor_copy(out=dbgt.ap[:, 3072:4096], in_=gt2g.ap), reads=[gt2g], writes=[dbgt])
        P.dma("sp", dbg_s, dbgt.ap, reads=[dbgt])

    if stop_after <= 0:
        return finish(P, nc, out, ar)

    P.barrier()
    ar.reset()
    w_in_bf = ar.alloc([8, INC], BF16, "w_in_bf")
    mark1 = ar.off
    wst2 = [ar.alloc([8, 512], F32, "wstA%d" % i) for i in range(2)]
    cast_i = [0]

    def load_cast(dstT, src_view, ncols, piece=512):
        for c0 in range(0, ncols, piece):
            n = min(piece, ncols - c0)
            i = cast_i[0]
            cast_i[0] += 1
            st = wst2[i % 2]
            P.dma("sp", st.ap[:, :, 0:n], src_view[:, :, c0:c0 + n], writes=[st])
            eng = ("dve", "act", "pool")[i % 3]
            if eng == "act":
                op("act", lambda e, st=st, c0=c0, n=n: e.copy(out=dstT.ap[:, :, c0:c0 + n], in_=st.ap[:, :, 0:n]),
                   reads=[st], writes=[dstT])
            else:
                op(eng, lambda e, st=st, c0=c0, n=n: e.tensor_copy(out=dstT.ap[:, :, c0:c0 + n], in_=st.ap[:, :, 0:n]),
                   reads=[st], writes=[dstT])

    load_cast(w_in_bf, w_in.rearrange("(k p) n -> p k n", p=128), INC)
    P.barrier()
    ar.off = mark1

    stg = [ar.alloc([516], F32, "stg%d" % j) for j in range(12)]
    xt = [ar.alloc([D], F32, "xt%d" % i) for i in range(2)]
    xs = [ar.alloc([D], BF16, "xs%d" % i) for i in range(2)]
    junk = ar.alloc([D], BF16, "junk")
    hxT = [ar.alloc([8, 512], BF16, "hxT%d" % i) for i in range(2)]
    stt = [ar.alloc([4], F32, "stt%d" % i) for i in range(4)]
    acc = [ar.alloc([512], F32, "acc%d" % i) for i in range(2)]
    qs = [[ar.alloc([512], F32, "qs%d_%d" % (p_, i)) for i in range(8)] for p_ in range(2)]
    sq = [ar.alloc([512], BF16, "sq%d" % i) for i in range(3)]
    rt = [ar.alloc([512], F32, "rt%d" % i) for i in range(3)]
    ot = [ar.alloc([512], BF16, "ot%d" % i) for i in range(4)]
    zo = [ar.alloc([512], F32, "zo%d" % i) for i in range(3)]
    gt1_ = ar.alloc([4, 8], F32, "gtmp1")
    gt2_ = ar.alloc([4, 8], F32, "gtmp2")
    cnt = {"x": 0, "acc": 0, "sq": 0, "ot": 0, "zo": 0, "st": 0}

    def rot(lst, key):
        cnt[key] += 1
        return lst[cnt[key] % len(lst)]

    def make_hxT(hx, src, r0, sub, A_, B_, xtile_in=None):
        cnt["x"] += 1
        xtile = xt[cnt["x"] % 2] if xtile_in is None else xtile_in
        xsb = xs[cnt["x"] % 2]
        st = rot(stt, "st")
        if xtile_in is None:
            P.dma("sp", xtile.ap, src[r0:r0 + 128, :], writes=[xtile])
        op("act", lambda e: e.memzero(st.ap), writes=[st])
        op("act", lambda e, jk=junk: e.activation(out=jk.ap, in_=xtile.ap, func=AF.Square, accum_out=st.ap[:, 0:1]),
           reads=[xtile, st], writes=[junk, st])
        op("act", lambda e: e.activation(out=st.ap[:, 1:2], in_=st.ap[:, 0:1], func=AF.Sqrt, scale=1.0 / D, bias=eps1.ap),
           reads=[st, eps1], writes=[st])
        op("dve", lambda e: e.reciprocal(out=st.ap[:, 2:3], in_=st.ap[:, 1:2]), reads=[st], writes=[st])
        op("dve", lambda e: e.tensor_scalar(out=xsb.ap, in0=xtile.ap, scalar1=st.ap[:, 2:3], scalar2=None, op0=ALU.mult),
           reads=[xtile, st], writes=[xsb])
        pb = npb()

        def tr(e):
            ins = None
            for kc in range(8):
                ins = e.transpose(out=pb.ap[:, kc * 128:(kc + 1) * 128], in_=xsb.ap[:, kc * 128:(kc + 1) * 128],
                                  identity=identb.ap)
            return ins
        op("pe", tr, reads=[xsb, identb], writes=[pb])
        for kc in range(8):
            dst = hx.ap[:, kc, sub * 128:(sub + 1) * 128]
            srcp = pb.ap[:, kc * 128:(kc + 1) * 128]
            if kc % 2 == 0:
                op("act", lambda e, dst=dst, srcp=srcp, kc=kc: e.activation(
                    out=dst, in_=srcp, func=AF.Identity, scale=A_.ap[:, kc:kc + 1], bias=B_.ap[:, kc:kc + 1]),
                   reads=[pb, A_, B_], writes=[hx])
            else:
                op("dve", lambda e, dst=dst, srcp=srcp, kc=kc: e.tensor_scalar(
                    out=dst, in0=srcp, scalar1=A_.ap[:, kc:kc + 1], scalar2=B_.ap[:, kc:kc + 1], op0=ALU.mult, op1=ALU.add),
                   reads=[pb, A_, B_], writes=[hx])

    def interleave(gens):
        gens = list(gens)
        while gens:
            for g_ in list(gens):
                try:
                    next(g_)
                except StopIteration:
                    gens.remove(g_)

    def project(src, Ttok, W, A_, B_, chunks, dq, dk, dv, Gd, Bd):
        ntile = Ttok // W
        nsub = W // 128
        for j in range(12):
            op("pool", lambda e, j=j: e.memset(stg[j].ap[:, 0:4], 0.0), writes=[stg[j]])

        def conv_silu(j, m0, n, tok0, qp):
            a = rot(acc, "acc")
            sj = stg[j]
            op("dve", lambda e, a=a, sj=sj, j=j: e.tensor_scalar(
                out=a.ap[:, 0:n], in0=sj.ap[:, m0:m0 + n], scalar1=cw.ap[:, j:j + 1], scalar2=None, op0=ALU.mult),
               reads=[sj, cw], writes=[a])
            for tap in range(1, 5):
                op("dve", lambda e, a=a, sj=sj, j=j, tap=tap: e.scalar_tensor_tensor(
                    out=a.ap[:, 0:n], in0=sj.ap[:, m0 + tap:m0 + tap + n], scalar=cw.ap[:, tap * 12 + j:tap * 12 + j + 1],
                    in1=a.ap[:, 0:n], op0=ALU.mult, op1=ALU.add), reads=[sj, cw, a], writes=[a])
            if j < 8:
                op("act", lambda e, a=a, j=j: e.activation(out=qs[qp][j].ap[:, 0:n], in_=a.ap[:, 0:n], func=AF.Silu),
                   reads=[a], writes=[qs[qp][j]])
            else:
                o = rot(ot, "ot")
                op("act", lambda e, a=a, o=o: e.activation(out=o.ap[:, 0:n], in_=a.ap[:, 0:n], func=AF.Silu),
                   reads=[a], writes=[o])
                P.dma("act", dv[j - 8][:, tok0:tok0 + n], o.ap[:, 0:n], reads=[o])

        def l2norm_front(j, n, qp):
            s_ = rot(sq, "sq")
            r_ = rt[cnt["sq"] % 3]
            q_ = qs[qp][j]
            isq = j < 4
            op("pool", lambda e: e.tensor_tensor(out=s_.ap[:, 0:n], in0=q_.ap[:, 0:n], in1=q_.ap[:, 0:n], op=ALU.mult),
               reads=[q_], writes=[s_])
            pn = nps()
            mm(pn, pn.ap[:, 0:n], [(None, [(onesb.ap, s_.ap[:, 0:n])])], [onesb, s_])
            op("act", lambda e: e.activation(out=r_.ap[:, 0:n], in_=pn.ap[:, 0:n], func=AF.Ln, scale=(128.0 if isq else 1.0),
                                             bias=(epsq.ap if isq else eps1.ap)), reads=[pn, epsq, eps1], writes=[r_])
            op("act", lambda e: e.activation(out=r_.ap[:, 0:n], in_=r_.ap[:, 0:n], func=AF.Exp, scale=-0.5), reads=[r_], writes=[r_])
            return (j, r_, qp)

        def l2norm_back(jr, n, tok0):
            j, r_, qp = jr
            q_ = qs[qp][j]
            o = rot(ot, "ot")
            op("pool", lambda e: e.tensor_tensor(out=o.ap[:, 0:n], in0=q_.ap[:, 0:n], in1=r_.ap[:, 0:n], op=ALU.mult),
               reads=[q_, r_], writes=[o])
            dst = dq[j] if j < 4 else dk[j - 4]
            P.dma("pool", dst[:, tok0:tok0 + n], o.ap[:, 0:n], reads=[o])

        def stageA(t):
            hx = hxT[t % 2]
            gp = psf[0]
            pend = None
            for sub in range(nsub + 1):
                if sub < nsub:
                    cnt["x"] += 1
                    xtile, xsb = xt[cnt["x"] % 2], xs[cnt["x"] % 2]
                    st = rot(stt, "st")
                    r0 = t * W + sub * 128
                    P.dma("sp", xtile.ap, src[r0:r0 + 128, :], writes=[xtile])
                    op("act", lambda e, st=st: e.memzero(st.ap), writes=[st])
                    op("act", lambda e, jk=junk, xtile=xtile, st=st: e.activation(out=jk.ap, in_=xtile.ap, func=AF.Square,
                                                                                 accum_out=st.ap[:, 0:1]),
                       reads=[xtile, st], writes=[junk, st])
                    op("act", lambda e, st=st: e.activation(out=st.ap[:, 1:2], in_=st.ap[:, 0:1], func=AF.Sqrt, scale=1.0 / D,
                                                            bias=eps1.ap), reads=[st, eps1], writes=[st])
                    op("dve", lambda e, st=st: e.reciprocal(out=st.ap[:, 2:3], in_=st.ap[:, 1:2]), reads=[st], writes=[st])
                    op("dve", lambda e, xsb=xsb, xtile=xtile, st=st: e.tensor_scalar(
                        out=xsb.ap, in0=xtile.ap, scalar1=st.ap[:, 2:3], scalar2=None, op0=ALU.mult),
                       reads=[xtile, st], writes=[xsb])
                if pend is not None:
                    psub, pxs = pend
                    pb = npb()

                    def tr(e, pb=pb, pxs=pxs):
                        ins = None
                        for kc in range(8):
                            ins = e.transpose(out=pb.ap[:, kc * 128:(kc + 1) * 128], in_=pxs.ap[:, kc * 128:(kc + 1) * 128],
                                              identity=identb.ap)
                        return ins
                    op("pe", tr, reads=[pxs, identb], writes=[pb])
                    for kc in range(8):
                        dst = hx.ap[:, kc, psub * 128:(psub + 1) * 128]
                        srcp = pb.ap[:, kc * 128:(kc + 1) * 128]
                        if kc % 2 == 0:
                            op("act", lambda e, dst=dst, srcp=srcp, kc=kc: e.activation(
                                out=dst, in_=srcp, func=AF.Identity, scale=A_.ap[:, kc:kc + 1], bias=B_.ap[:, kc:kc + 1]),
                               reads=[pb, A_, B_], writes=[hx])
                        else:
                            op("dve", lambda e, dst=dst, srcp=srcp, kc=kc: e.tensor_scalar(
                                out=dst, in0=srcp, scalar1=A_.ap[:, kc:kc + 1], scalar2=B_.ap[:, kc:kc + 1], op0=ALU.mult, op1=ALU.add),
                               reads=[pb, A_, B_], writes=[hx])
                    mm(gp, gp.ap[:, psub * 16:(psub + 1) * 16],
                       [(None, [(hx.ap[:, kc, psub * 128:(psub + 1) * 128], w_in_bf.ap[:, kc, 2048:2064]) for kc in range(8)])],
                       [hx, w_in_bf])
                pend = (sub, xsb) if sub < nsub else None
                yield
            gv = gp.ap[:, 0:nsub * 16].rearrange("p (s c) -> p s c", c=16)
            t1 = gt1_.ap[:, 0:nsub, :]
            t2 = gt2_.ap[:, 0:nsub, :]
            op("act", lambda e: e.activation(out=t1, in_=gv[:, :, 0:8], func=AF.Exp, scale=-1.0), reads=[gp], writes=[gt1_])
            for sub in range(nsub):
                op("dve", lambda e, sub=sub: e.tensor_tensor(out=gt2_.ap[:, sub, :], in0=gv[:, sub, 8:16], in1=dtb.ap,
                                                            op=ALU.add), reads=[gp, dtb], writes=[gt2_])
            op("dve", lambda e: e.tensor_scalar(out=t1, in0=t1, scalar1=1.0, scalar2=None, op0=ALU.add), reads=[gt1_], writes=[gt1_])
            op("dve", lambda e: e.reciprocal(out=Bd.ap[:, t * nsub:(t + 1) * nsub, :], in_=t1), reads=[gt1_], writes=[Bd])
            op("act", lambda e: e.activation(out=t2, in_=t2, func=AF.Exp), reads=[gt2_], writes=[gt2_])
            op("act", lambda e: e.activation(out=t2, in_=t2, func=AF.Ln, bias=one1.ap), reads=[gt2_, one1], writes=[gt2_])
            for sub in range(nsub):
                op("dve", lambda e, sub=sub: e.tensor_tensor(out=Gd.ap[:, t * nsub + sub, :], in0=gt2_.ap[:, sub, :],
                                                            in1=negA.ap, op=ALU.mult), reads=[gt2_, negA], writes=[Gd])
            yield

        def stageBC(t):
            hx = hxT[t % 2]
            m0 = 2 if t == 0 else 0
            n = W - m0
            tok0 = t * W - 2 + m0
            order = [j for j in chunks if j < 12] + [j for j in chunks if j >= 12]
            prev = None
            for j in order + [None]:
                if j is not None:
                    c0 = j * 128 if j < 16 else 2064 + (j - 17) * 128
                    ps = nps()
                    mm(ps, ps.ap[:, 0:W], [(None, [(w_in_bf.ap[:, kc, c0:c0 + 128], hx.ap[:, kc, 0:W]) for kc in range(8)])],
                       [w_in_bf, hx])
                    if j < 12:
                        op("act", lambda e, ps=ps, j=j: e.copy(out=stg[j].ap[:, 4:4 + W], in_=ps.ap[:, 0:W]),
                           reads=[ps], writes=[stg[j]])
                    else:
                        z_ = rot(zo, "zo")
                        op("act", lambda e, ps=ps, z_=z_: e.copy(out=z_.ap[:, 0:W], in_=ps.ap[:, 0:W]), reads=[ps], writes=[z_])
                        dst = zT_s[j - 12] if j < 16 else uT_s[j - 17]
                        P.dma("act", dst[:, t * W:(t + 1) * W], z_.ap[:, 0:W], reads=[z_])
                if prev is not None:
                    conv_silu(prev, m0, n, tok0, t % 2)
                prev = j if (j is not None and j < 12) else None
                yield
            for j in chunks:
                if j < 12:
                    op("pool", lambda e, j=j: e.tensor_copy(out=stg[j].ap[:, 0:4], in_=stg[j].ap[:, W:W + 4]),
                       reads=[stg[j]], writes=[stg[j]])
            yield

        def stageL(t):
            m0 = 2 if t == 0 else 0
            n = W - m0
            tok0 = t * W - 2 + m0
            prevn = None
            for j in [j for j in chunks if j < 8] + [None]:
                cur = l2norm_front(j, n, t % 2) if j is not None else None
                if prevn is not None:
                    l2norm_back(prevn, n, tok0)
                prevn = cur
                yield
            if t == ntile - 1:
                for j in chunks:
                    if j < 12:
                        op("pool", lambda e, j=j: e.memset(stg[j].ap[:, 4:8], 0.0), writes=[stg[j]])
                        conv_silu(j, 0, 2, Ttok - 2, t % 2)
                yield
                for j in chunks:
                    if j < 8:
                        l2norm_back(l2norm_front(j, 2, t % 2), 2, Ttok - 2)
                yield

        interleave([stageA(0)])
        for t in range(ntile):
            gens = [stageBC(t)]
            if t + 1 < ntile:
                gens.append(stageA(t + 1))
            if t >= 1:
                gens.append(stageL(t - 1))
            interleave(gens)
        interleave([stageL(ntile - 1)])

    project(ctx, TC, 256, A1c, B1c, list(range(4, 12)), None, [kT_c[h] for h in range(4)], [vT_c[h] for h in range(4)], Gc, Bc)
    if stop_after <= 1:
        return finish(P, nc, out, ar)
    project(x, T, 512, A1, B1, list(range(0, 16)) + list(range(17, 21)), [qT_s[h] for h in range(4)],
            [kT_s[h] for h in range(4)], [vT_s[h] for h in range(4)], Gm, Bt)
    if stop_after <= 2:
        return finish(P, nc, out, ar)

    P.barrier()
    ar.reset()
    U = ar.alloc([64, 64], F32, "poolU")
    PA = ar.alloc([64, 80], F32, "poolA")
    PB = ar.alloc([64, 80], F32, "poolB")
    PC = ar.alloc([80, 64], F32, "poolC")
    PD = ar.alloc([80, 64], F32, "poolD")
    PM = ar.alloc([64, 64], F32, "poolM")
    dTb = ar.alloc([T], BF16, "pooldT")
    pwst = ar.alloc([4, 128], F32, "pwst")
    pw_bf = ar.alloc([4, 128], BF16, "pw_bf")
    po = [ar.alloc([512], BF16, "po%d" % i) for i in range(2)]
    ca = ar.alloc([80], F32, "cnta")
    cb = ar.alloc([80], F32, "cntb")
    rcs = [ar.alloc([64], F32, "rc%d" % g) for g in range(4)]
    cst = [ar.alloc([8, 512], F32, "cst%d" % i) for i in range(2)]
    cbf = [ar.alloc([8, 512], BF16, "cbf%d" % i) for i in range(1)]

    def cast_bg():
        pieces = []
        wo_v = w_out.rearrange("(k p) n -> p k n", p=128)
        wu_v = w_up.rearrange("(k p) n -> p k n", p=128)
        wd_v = w_down.rearrange("(j p) n -> p j n", p=128)
        wds_v = wd_s.rearrange("j p n -> p j n")
        for c0 in range(0, D, 512):
            pieces.append((wo_v[:, :, c0:c0 + 512], wout_s[:, :, c0:c0 + 512], 8, 512))
        for c0 in range(0, 2 * DFF, 512):
            pieces.append((wu_v[:, :, c0:c0 + 512], wup_s[:, :, c0:c0 + 512], 8, 512))
        for j0 in range(0, NFC, 4):
            nj = min(4, NFC - j0)
            pieces.append((wd_v[:, j0:j0 + nj, :], wds_v[:, j0:j0 + nj, :], nj, D))
        for i, (src_, dst_, a_, b_) in enumerate(pieces):
            st_, bf_ = cst[i % 2], cbf[0]
            if b_ == 512:
                sv, bv = st_.ap[:, 0:a_, :], bf_.ap[:, 0:a_, :]
            else:
                sv = st_.ap.rearrange("p a b -> p (a b)")[:, 0:a_ * b_].rearrange("p (a b) -> p a b", a=a_)
                bv = bf_.ap.rearrange("p a b -> p (a b)")[:, 0:a_ * b_].rearrange("p (a b) -> p a b", a=a_)
            P.dma("sp", sv, src_, writes=[st_])
            op("act", lambda e, sv=sv, bv=bv: e.copy(out=bv, in_=sv), reads=[st_], writes=[bf_])
            P.dma("act", dst_, bv, reads=[bf_])
            yield

    bg_ = cast_bg()

    def bgstep(k=1):
        for _ in range(k):
            try:
                next(bg_)
            except StopIteration:
                return

    P.dma("sp", pwst.ap, pool_w.rearrange("g c d -> c g d"), writes=[pwst])
    op("dve", lambda e: e.tensor_copy(out=pw_bf.ap, in_=pwst.ap), reads=[pwst], writes=[pw_bf])
    for g in range(4):
        L = g + 1
        wv = 2 ** L
        left = wv // 2
        lo = 8 - left
        op("pool", lambda e: e.memset(ca.ap, 0.0), writes=[ca])
        op("pool", lambda e: e.memset(ca.ap[:, 8:72], 1.0), writes=[ca])
        src_, dst_ = ca, cb
        for l in range(L):
            sft = 2 ** l
            op("pool", lambda e, src_=src_, dst_=dst_, sft=sft: e.tensor_tensor(
                out=dst_.ap[:, 0:80 - sft], in0=src_.ap[:, 0:80 - sft], in1=src_.ap[:, sft:80], op=ALU.add),
               reads=[src_], writes=[dst_])
            src_, dst_ = dst_, src_
        rc = rcs[g]
        op("dve", lambda e, src_=src_, rc=rc, lo=lo: e.reciprocal(out=rc.ap, in_=src_.ap[:, lo:lo + 64]), reads=[src_], writes=[rc])
        P.dma("sp", U.ap, uT_s[g].rearrange("p (r c) -> p r c", c=64), writes=[U])
        op("pool", lambda e: e.memset(PA.ap, 0.0), writes=[PA])
        op("pool", lambda e: e.tensor_copy(out=PA.ap[:, :, 8:72], in_=U.ap), reads=[U], writes=[PA])
        src_, dst_ = PA, PB
        for l in range(L):
            sft = 2 ** l
            op("dve", lambda e, src_=src_, dst_=dst_, sft=sft: e.tensor_tensor(
                out=dst_.ap[:, :, 0:80 - sft], in0=src_.ap[:, :, 0:80 - sft], in1=src_.ap[:, :, sft:80], op=ALU.add),
               reads=[src_], writes=[dst_])
            src_, dst_ = dst_, src_
            bgstep(1)
        op("pool", lambda e: e.memset(PC.ap, 0.0), writes=[PC])
        op("pool", lambda e, src_=src_, rc=rc, lo=lo: e.tensor_tensor(
            out=PC.ap[:, 8:72, :], in0=src_.ap[:, :, lo:lo + 64], in1=rc.ap.unsqueeze(1).to_broadcast([128, 64, 64]),
            op=ALU.mult), reads=[src_, rc], writes=[PC])
        src_, dst_ = PC, PD
        for l in range(L):
            sft = 2 ** l
            op("dve", lambda e, src_=src_, dst_=dst_, sft=sft: e.tensor_tensor(
                out=dst_.ap[:, 0:80 - sft, :], in0=src_.ap[:, 0:80 - sft, :], in1=src_.ap[:, sft:80, :], op=ALU.add),
               reads=[src_], writes=[dst_])
            src_, dst_ = dst_, src_
        op("pool", lambda e, src_=src_, rc=rc, lo=lo: e.tensor_tensor(
            out=PM.ap, in0=src_.ap[:, lo:lo + 64, :], in1=rc.ap.unsqueeze(2).to_broadcast([128, 64, 64]),
            op=ALU.mult), reads=[src_, rc], writes=[PM])
        op("dve", lambda e: e.tensor_tensor(out=dTb.ap.rearrange("p (r c) -> p r c", c=64), in0=PM.ap, in1=U.ap,
                                            op=ALU.subtract), reads=[PM, U], writes=[dTb])
        bgstep(3)
        for tt in range(8):
            ps = nps()
            mm(ps, ps.ap, [(None, [(pw_bf.ap[:, g, :], dTb.ap[:, tt * 512:(tt + 1) * 512])])], [pw_bf, dTb])
            o = po[tt % 2]
            op("act", lambda e, ps=ps, o=o, g=g: e.activation(out=o.ap, in_=ps.ap, func=AF.Copy, scale=pscale.ap[:, g:g + 1]),
               reads=[ps, pscale], writes=[o])
            P.dma("act", poolT_s[g][:, tt * 512:(tt + 1) * 512], o.ap, reads=[o])
    for _ in bg_:
        pass
    if stop_after <= 3:
        return finish(P, nc, out, ar)

    P.barrier()
    ar.reset()
    op("pool", lambda e: e.memset(S.ap, 0.0), writes=[S])
    op("pool", lambda e: e.memset(Sb.ap, 0.0), writes=[Sb])
    Sd = [Buf("S0"), Buf("S1")]
    Sbd = [Buf("Sb0"), Buf("Sb1")]
    P.barrier()

    def wset(tag):
        W_ = {}
        for nm in ("Kf", "Vf", "Qf", "Vt", "KD", "NTb", "AT", "Mb", "Yb", "Pa", "PTa", "Pb", "PTb", "N0", "M0", "W2b", "YTb"):
            W_[nm] = ar.alloc([4, 128], BF16, nm + tag)
        for nm in ("GM", "DT", "DTs"):
            W_[nm] = ar.alloc([4, 128], F32, nm + tag)
        W_["cg"] = ar.alloc([16], F32, "cg" + tag)
        W_["E"] = ar.alloc([12], F32, "E" + tag)
        W_["nec"] = ar.alloc([4], F32, "nec" + tag)
        return W_

    WS = [[wset("_%d%d" % (d, p_)) for p_ in range(3)] for d in range(2)]
    RS = []
    for d in range(2):
        R_ = {}
        for nm in ("R3", "VN"):
            R_[nm] = ar.alloc([4, 128], BF16, "%s_%d" % (nm, d))
        for nm in ("OI", "O"):
            R_[nm] = ar.alloc([4, 128], F32, "%s_%d" % (nm, d))
        RS.append(R_)

    def v4(ps):
        return ps.ap.rearrange("p (h t) -> p h t", h=4)

    def mm4(ps, pairs_h, reads):
        groups = [(ps.ap[:, h * 128:(h + 1) * 128], pairs_h(h)) for h in range(4)]
        return mm(ps, None, groups, reads)

    def tr4(src_):
        pb = npb()

        def tr(e, src_=src_, pb=pb):
            ins = None
            for h in range(4):
                ins = e.transpose(out=pb.ap[:, h * 128:(h + 1) * 128], in_=src_.ap[:, h, :], identity=identb.ap)
            return ins
        op("pe", tr, reads=[src_, identb], writes=[pb])
        return pb

    def pb4(pb):
        return pb.ap[:, 0:512].rearrange("p (h t) -> p h t", h=4)

    def bfree(ap4):
        return ap4.unsqueeze(2).to_broadcast([128, 4, 128])

    def bhead(ap128):
        return ap128.unsqueeze(1).to_broadcast([128, 4, 128])

    def scan_pre(d, n, par, kd_, vd_, qd_, Gd, Bd):
        W_ = WS[d][par]
        with_out = qd_ is not None
        Kf, Vf, Qf, Vt, KD = W_["Kf"], W_["Vf"], W_["Qf"], W_["Vt"], W_["KD"]
        sl = slice(n * 128, (n + 1) * 128)
        P.dma("sp", Kf.ap, kd_.rearrange("h p t -> p h t")[:, :, sl], writes=[Kf])
        P.dma("sp", Vf.ap, vd_.rearrange("h p t -> p h t")[:, :, sl], writes=[Vf])
        if with_out:
            P.dma("sp", Qf.ap, qd_.rearrange("h p t -> p h t")[:, :, sl], writes=[Qf])
        yield
        g_ap = Gd.ap[:, n, d * 4:(d + 1) * 4]
        b_ap = Bd.ap[:, n, d * 4:(d + 1) * 4]
        mask = Lmask if d == 0 else Umask
        mb = mbF if d == 0 else mbB
        st4 = stF4 if d == 0 else stB4
        cg, E, nec = W_["cg"], W_["E"], W_["nec"]
        pc = nps()
        mm(pc, None, [(pc.ap[:, 0:4], [(mask.ap, g_ap)]), (pc.ap[:, 4:8], [(onesf.ap, g_ap)])], [mask, onesf, Gd])
        op("dve", lambda e: e.tensor_copy(out=cg.ap[:, 0:4], in_=pc.ap[:, 0:4]), reads=[pc], writes=[cg])
        op("dve", lambda e: e.tensor_tensor(out=cg.ap[:, 4:8], in0=pc.ap[:, 4:8], in1=cg.ap[:, 0:4], op=ALU.subtract),
           reads=[pc, cg], writes=[cg])
        op("dve", lambda e: e.tensor_copy(out=cg.ap[:, 8:12], in_=pc.ap[:, 4:8]), reads=[pc], writes=[cg])
        op("dve", lambda e: e.tensor_scalar(out=cg.ap[:, 12:16], in0=cg.ap[:, 0:4], scalar1=-1.0, scalar2=None, op0=ALU.mult),
           reads=[cg], writes=[cg])
        yield
        op("act", lambda e: e.activation(out=E.ap, in_=cg.ap[:, 0:12], func=AF.Exp), reads=[cg], writes=[E])
        op("dve", lambda e: e.tensor_scalar(out=nec.ap, in0=E.ap[:, 0:4], scalar1=-1.0, scalar2=None, op0=ALU.mult),
           reads=[E], writes=[nec])
        yield
        pbv = tr4(Vf)
        op("act", lambda e: e.copy(out=Vt.ap, in_=pb4(pbv)), reads=[pbv], writes=[Vt])
        yield
        pbk = tr4(Kf)
        op("dve", lambda e: e.tensor_tensor(out=KD.ap, in0=pb4(pbk), in1=bfree(E.ap[:, 4:8]), op=ALU.mult),
           reads=[pbk, E], writes=[KD])
        yield
        GM, DT, DTs = W_["GM"], W_["DT"], W_["DTs"]
        op("dve", lambda e: e.tensor_tensor(out=GM.ap, in0=bhead(mask.ap), in1=bfree(g_ap), op=ALU.mult), reads=[mask, Gd], writes=[GM])
        yield
        pd = nps()
        mm4(pd, lambda h: [(onesf.ap, GM.ap[:, h, :]), (identf.ap, mb.ap)], [onesf, GM, identf, mb])
        for h in range(4):
            op("act", lambda e, h=h: e.activation(out=DT.ap[:, h, :], in_=pd.ap[:, h * 128:(h + 1) * 128], func=AF.Exp,
                                                  bias=cg.ap[:, 12 + h:13 + h]), reads=[pd, cg], writes=[DT])
        yield
        op("pool", lambda e: e.tensor_tensor(out=DTs.ap, in0=DT.ap, in1=st4.ap, op=ALU.mult), reads=[DT, st4], writes=[DTs])
        yield
        NTb, AT, Mb, Yb = W_["NTb"], W_["AT"], W_["Mb"], W_["Yb"]
        pk = nps()
        mm4(pk, lambda h: [(Kf.ap[:, h, :], Kf.ap[:, h, :])], [Kf])
        for h in range(4):
            op("dve", lambda e, h=h: e.scalar_tensor_tensor(out=NTb.ap[:, h, :], in0=pk.ap[:, h * 128:(h + 1) * 128],
                                                            scalar=b_ap[:, h:h + 1], in1=DTs.ap[:, h, :], op0=ALU.mult, op1=ALU.mult),
               reads=[pk, Bd, DTs], writes=[NTb])
        yield
        if with_out:
            pq = nps()
            mm4(pq, lambda h: [(Kf.ap[:, h, :], Qf.ap[:, h, :])], [Kf, Qf])
            op("dve", lambda e: e.tensor_tensor(out=AT.ap, in0=v4(pq), in1=DT.ap, op=ALU.mult), reads=[pq, DT], writes=[AT])
            yield
        pbn = tr4(NTb)
        op("act", lambda e: e.copy(out=Mb.ap, in_=pb4(pbn)), reads=[pbn], writes=[Mb])
        yield
        N0, M0, W2b, YTb = W_["N0"], W_["M0"], W_["W2b"], W_["YTb"]
        op("pool", lambda e: e.tensor_tensor(out=N0.ap, in0=NTb.ap, in1=mk4["bm16"].ap, op=ALU.mult), reads=[NTb, mk4["bm16"]], writes=[N0])
        op("pool", lambda e: e.tensor_tensor(out=M0.ap, in0=Mb.ap, in1=mk4["bm16"].ap, op=ALU.mult), reads=[Mb, mk4["bm16"]], writes=[M0])
        op("pool", lambda e: e.tensor_tensor(out=Yb.ap, in0=I4.ap, in1=N0.ap, op=ALU.subtract), reads=[I4, N0], writes=[Yb])
        yield
        Pc, PTc = N0, M0
        nxt = [(W_["Pa"], W_["PTa"]), (W_["Pb"], W_["PTb"])]
        for lvl in range(3):
            Pn, PTn = nxt[lvl % 2]
            if lvl < 2:
                p1 = nps()
                mm4(p1, lambda h, Pc=Pc, PTc=PTc: [(PTc.ap[:, h, :], Pc.ap[:, h, :])], [Pc, PTc])
                op("act", lambda e, p1=p1, Pn=Pn: e.copy(out=Pn.ap, in_=v4(p1)), reads=[p1], writes=[Pn])
            p2 = nps()
            mm4(p2, lambda h, Pc=Pc, PTc=PTc: [(Pc.ap[:, h, :], PTc.ap[:, h, :])], [Pc, PTc])
            op("act", lambda e, p2=p2, PTn=PTn: e.copy(out=PTn.ap, in_=v4(p2)), reads=[p2], writes=[PTn])
            yield
            p3 = nps()
            mm4(p3, lambda h, PTn=PTn: [(PTn.ap[:, h, :], Yb.ap[:, h, :])], [PTn, Yb])
            op("dve", lambda e, p3=p3: e.tensor_tensor(out=Yb.ap, in0=v4(p3), in1=Yb.ap, op=ALU.add), reads=[p3, Yb], writes=[Yb])
            yield
            Pc, PTc = Pn, PTn
        for offk in ("off16", "off32", "off64"):
            pw = nps()
            mm4(pw, lambda h: [(Mb.ap[:, h, :], Yb.ap[:, h, :])], [Mb, Yb])
            pby = tr4(Yb)
            op("dve", lambda e, pw=pw, offk=offk: e.tensor_tensor(out=W2b.ap, in0=v4(pw), in1=mk4[offk].ap, op=ALU.mult),
               reads=[pw, mk4[offk]], writes=[W2b])
            op("act", lambda e, pby=pby: e.copy(out=YTb.ap, in_=pb4(pby)), reads=[pby], writes=[YTb])
            yield
            py = nps()
            mm4(py, lambda h: [(YTb.ap[:, h, :], W2b.ap[:, h, :])], [YTb, W2b])
            op("dve", lambda e, py=py: e.tensor_tensor(out=Yb.ap, in0=Yb.ap, in1=v4(py), op=ALU.subtract), reads=[py, Yb], writes=[Yb])
            yield

    def scan_rec(d, n, par, with_out, Bd, odst):
        W_ = WS[d][par]
        R_ = RS[d]
        Kf, Qf, Vt, KD, Yb, AT, E, nec = W_["Kf"], W_["Qf"], W_["Vt"], W_["KD"], W_["Yb"], W_["AT"], W_["E"], W_["nec"]
        R3, VN, OI, O = R_["R3"], R_["VN"], R_["OI"], R_["O"]
        b_ap = Bd.ap[:, n, d * 4:(d + 1) * 4]
        Sv = S.ap[:, d * 4:(d + 1) * 4, :]
        Sbv = Sb.ap[:, d * 4:(d + 1) * 4, :]
        pks = nps()
        mm4(pks, lambda h: [(Kf.ap[:, h, :], Sbv[:, h, :])], [Kf, Sbd[d]])
        if with_out:
            pqs = nps()
            mm4(pqs, lambda h: [(Qf.ap[:, h, :], Sbv[:, h, :])], [Qf, Sbd[d]])
            for h in range(4):
                op("act", lambda e, h=h: e.activation(out=OI.ap[:, h, :], in_=pqs.ap[:, h * 128:(h + 1) * 128], func=AF.Copy,
                                                      scale=E.ap[:, h:h + 1]), reads=[pqs, E], writes=[OI])
        for h in range(4):
            op("dve", lambda e, h=h: e.scalar_tensor_tensor(out=R3.ap[:, h, :], in0=pks.ap[:, h * 128:(h + 1) * 128],
                                                            scalar=nec.ap[:, h:h + 1], in1=Vt.ap[:, h, :], op0=ALU.mult, op1=ALU.add),
               reads=[pks, nec, Vt], writes=[R3])
        yield
        pv = nps()
        mm4(pv, lambda h: [(Yb.ap[:, h, :], R3.ap[:, h, :])], [Yb, R3])
        for h in range(4):
            op("act", lambda e, h=h: e.activation(out=VN.ap[:, h, :], in_=pv.ap[:, h * 128:(h + 1) * 128], func=AF.Copy,
                                                  scale=b_ap[:, h:h + 1]), reads=[pv, Bd], writes=[VN])
        yield
        pds = nps()
        mm4(pds, lambda h: [(KD.ap[:, h, :], VN.ap[:, h, :])], [KD, VN])
        for h in range(4):
            op("dve", lambda e, h=h: e.scalar_tensor_tensor(out=Sv[:, h, :], in0=Sv[:, h, :], scalar=E.ap[:, 8 + h:9 + h],
                                                            in1=pds.ap[:, h * 128:(h + 1) * 128], op0=ALU.mult, op1=ALU.add),
               reads=[Sd[d], E, pds], writes=[Sd[d]])
        op("act", lambda e: e.copy(out=Sbv, in_=Sv), reads=[Sd[d]], writes=[Sbd[d]])
        yield
        if with_out:
            pav = nps()
            mm4(pav, lambda h: [(AT.ap[:, h, :], VN.ap[:, h, :])], [AT, VN])
            op("dve", lambda e: e.tensor_tensor(out=O.ap, in0=v4(pav), in1=OI.ap, op=ALU.add), reads=[pav, OI], writes=[O])
            P.dma("sp", odst[n], O.ap.rearrange("p h t -> p (h t)"), reads=[O])
            yield

    NPAR = 3

    def scan_pass(nch, kd_, vd_, qd_, Gd, Bd):
        with_out = qd_ is not None
        chunk = [lambda r: r, lambda r: nch - 1 - r]
        odst = [of_s, ob_s]
        pre_done = [set(), set()]
        rec_done = [set(), set()]
        pre_started = [0, 0]
        rec_started = [0, 0]
        active = []
        while True:
            for d in range(2):
                r = pre_started[d]
                if r < nch and (r < NPAR or (r - NPAR) in rec_done[d]):
                    active.append((scan_pre(d, chunk[d](r), r % NPAR, kd_, vd_, qd_, Gd, Bd), "pre", d, r))
                    pre_started[d] += 1
                r = rec_started[d]
                if r < nch and r in pre_done[d] and (r == 0 or (r - 1) in rec_done[d]):
                    active.append((scan_rec(d, chunk[d](r), r % NPAR, with_out, Bd, odst[d]), "rec", d, r))
                    rec_started[d] += 1
            if not active:
                break
            for item in list(active):
                g_, kind, d, r = item
                try:
                    next(g_)
                except StopIteration:
                    active.remove(item)
                    (pre_done if kind == "pre" else rec_done)[d].add(r)

    scan_pass(TC // CH, kT_c, vT_c, None, Gc, Bc)
    if stop_after <= 4:
        return finish(P, nc, out, ar)
    scan_pass(NCH_DEBUG or T // CH, kT_s, vT_s, qT_s, Gm, Bt)
    if stop_after <= 5:
        return finish(P, nc, out, ar)

    P.barrier()
    ar.reset()
    w_out_bf = ar.alloc([8, D], BF16, "w_out_bf")
    P.dma("sp", w_out_bf.ap, wout_s, writes=[w_out_bf])
    xt = [ar.alloc([D], F32, "xtB%d" % i) for i in range(2)]
    xs = [ar.alloc([D], BF16, "xsB%d" % i) for i in range(2)]
    junk = ar.alloc([D], BF16, "junkB")
    stt = [ar.alloc([4], F32, "sttB%d" % i) for i in range(4)]
    hxT = [ar.alloc([8, 512], BF16, "hx2T%d" % i) for i in range(2)]
    zt = [ar.alloc([4, 512], F32, "zt%d" % i) for i in range(2)]
    mixT = [ar.alloc([8, 512], BF16, "mixT%d" % i) for i in range(2)]
    oft = [ar.alloc([4, 128], F32, "oft%d" % i) for i in range(2)]
    obt = [ar.alloc([4, 128], F32, "obt%d" % i) for i in range(2)]
    onb = [ar.alloc([4, 128], BF16, "onb%d" % i) for i in range(2)]
    ost = [ar.alloc([12], F32, "ost%d" % i) for i in range(2)]
    x1t = [ar.alloc([D], F32, "x1t%d" % i) for i in range(2)]
    tmpt = [ar.alloc([D], F32, "tmpt%d" % i) for i in range(2)]
    yst = [ar.alloc([4], F32, "yst%d" % i) for i in range(2)]
    junkf = ar.alloc([512], F32, "junkf")
    zv = zT_s.rearrange("h p t -> p h t")
    pv_ = poolT_s.rearrange("h p t -> p h t")
    h2v = hx2T_s.rearrange("k p t -> p k t")
    c3 = {"c": 0, "s": 0}

    def chainC(t):
        tsl = slice(t * 512, (t + 1) * 512)
        z_ = zt[t % 2]
        mx = mixT[t % 2]
        P.dma("sp", z_.ap, zv[:, :, tsl], writes=[z_])
        P.dma("sp", mx.ap[:, 4:8, :], pv_[:, :, tsl], writes=[mx])
        op("act", lambda e: e.activation(out=z_.ap, in_=z_.ap, func=AF.Silu), reads=[z_], writes=[z_])
        yield
        for c_ in range(4):
            n = t * 4 + c_
            c3["c"] += 1
            ci = c3["c"]
            of_, ob_, on_, os_ = oft[ci % 2], obt[ci % 2], onb[ci % 2], ost[ci % 2]
            P.dma("sp", of_.ap.rearrange("p h t -> p (h t)"), of_s[n], writes=[of_])
            P.dma("sp", ob_.ap.rearrange("p h t -> p (h t)"), ob_s[n], writes=[ob_])
            op("dve", lambda e, of_=of_, ob_=ob_: e.tensor_tensor(out=of_.ap, in0=of_.ap, in1=ob_.ap, op=ALU.add),
               reads=[of_, ob_], writes=[of_])
            op("act", lambda e, os_=os_: e.memzero(os_.ap), writes=[os_])
            for h in range(4):
                op("act", lambda e, of_=of_, os_=os_, h=h, jk=junkf: e.activation(out=jk.ap[:, 0:128], in_=of_.ap[:, h, :], func=AF.Square,
                                                                                 accum_out=os_.ap[:, h:h + 1]),
                   reads=[of_, os_], writes=[junkf, os_])
            op("act", lambda e, os_=os_: e.activation(out=os_.ap[:, 4:8], in_=os_.ap[:, 0:4], func=AF.Sqrt, scale=1.0 / 128, bias=eps1.ap),
               reads=[os_, eps1], writes=[os_])
            op("dve", lambda e, os_=os_: e.reciprocal(out=os_.ap[:, 8:12], in_=os_.ap[:, 4:8]), reads=[os_], writes=[os_])
            yield
            for h in range(4):
                op("dve", lambda e, of_=of_, os_=os_, on_=on_, h=h: e.tensor_scalar(
                    out=on_.ap[:, h, :], in0=of_.ap[:, h, :], scalar1=os_.ap[:, 8 + h:9 + h], scalar2=None, op0=ALU.mult),
                   reads=[of_, os_], writes=[on_])
            pb = npb()

            def tro(e, pb=pb, on_=on_):
                ins = None
                for h in range(4):
                    ins = e.transpose(out=pb.ap[:, h * 128:(h + 1) * 128], in_=on_.ap[:, h, :], identity=identb.ap)
                return ins
            op("pe", tro, reads=[on_, identb], writes=[pb])
            op("dve", lambda e, pb=pb, c_=c_: e.scalar_tensor_tensor(
                out=mx.ap[:, 0:4, c_ * 128:(c_ + 1) * 128], in0=pb.ap[:, 0:512].rearrange("p (h t) -> p h t", h=4),
                scalar=onorm.ap[:, 0:1], in1=z_.ap[:, :, c_ * 128:(c_ + 1) * 128], op0=ALU.mult, op1=ALU.mult),
               reads=[pb, onorm, z_], writes=[mx])
            yield

    def chainS(t):
        tsl = slice(t * 512, (t + 1) * 512)
        mx = mixT[t % 2]
        hx = hxT[t % 2]
        st8 = {}

        def s1(sub):
            r0 = t * 512 + sub * 128
            c3["s"] += 1
            ci = c3["s"]
            xx, x1_, tm_, ys_ = xt[ci % 2], x1t[ci % 2], tmpt[ci % 2], yst[ci % 2]
            st8[sub] = (xx, x1_, tm_, r0)
            P.dma("sp", xx.ap, x[r0:r0 + 128, :], writes=[xx])
            pys = [nps(), nps()]
            for half in range(2):
                mm(pys[half], pys[half].ap, [(None, [(mx.ap[:, k, sub * 128:(sub + 1) * 128],
                                                       w_out_bf.ap[:, k, half * 512:(half + 1) * 512]) for k in range(8)])],
                   [mx, w_out_bf])
            op("act", lambda e: e.memzero(ys_.ap), writes=[ys_])
            for half in range(2):
                op("act", lambda e, half=half, p_=pys[half], jk=junkf: e.activation(
                    out=jk.ap, in_=p_.ap, func=AF.Square, accum_out=ys_.ap[:, half:half + 1]),
                   reads=[pys[half], ys_], writes=[junkf, ys_])
            op("dve", lambda e: e.tensor_tensor(out=ys_.ap[:, 2:3], in0=ys_.ap[:, 0:1], in1=ys_.ap[:, 1:2], op=ALU.add),
               reads=[ys_], writes=[ys_])
            op("act", lambda e: e.activation(out=ys_.ap[:, 3:4], in_=ys_.ap[:, 2:3], func=AF.Sqrt, scale=1.0 / D, bias=eps1.ap),
               reads=[ys_, eps1], writes=[ys_])
            op("dve", lambda e: e.reciprocal(out=ys_.ap[:, 3:4], in_=ys_.ap[:, 3:4]), reads=[ys_], writes=[ys_])
            for half in range(2):
                hs = slice(half * 512, (half + 1) * 512)
                op("dve", lambda e, hs=hs, p_=pys[half]: e.scalar_tensor_tensor(
                    out=tm_.ap[:, hs], in0=p_.ap, scalar=ys_.ap[:, 3:4], in1=gt1g.ap[:, hs], op0=ALU.mult, op1=ALU.mult),
                   reads=[pys[half], ys_, gt1g], writes=[tm_])

        def s2(sub):
            xx, x1_, tm_, r0 = st8[sub]
            op("pool", lambda e: e.tensor_tensor(out=x1_.ap, in0=tm_.ap, in1=xx.ap, op=ALU.add), reads=[tm_, xx], writes=[x1_])
            P.dma("pool", x1_s[r0:r0 + 128, :], x1_.ap, reads=[x1_])
            cnt["x"] += 1
            xsb = xs[cnt["x"] % 2]
            st = rot(stt, "st")
            op("act", lambda e: e.memzero(st.ap), writes=[st])
            op("act", lambda e, jk=junk: e.activation(out=jk.ap, in_=x1_.ap, func=AF.Square, accum_out=st.ap[:, 0:1]),
               reads=[x1_, st], writes=[junk, st])
            op("act", lambda e: e.activation(out=st.ap[:, 1:2], in_=st.ap[:, 0:1], func=AF.Sqrt, scale=1.0 / D, bias=eps1.ap),
               reads=[st, eps1], writes=[st])
            op("dve", lambda e: e.reciprocal(out=st.ap[:, 2:3], in_=st.ap[:, 1:2]), reads=[st], writes=[st])
            op("dve", lambda e: e.tensor_scalar(out=xsb.ap, in0=x1_.ap, scalar1=st.ap[:, 2:3], scalar2=None, op0=ALU.mult),
               reads=[x1_, st], writes=[xsb])
            st8[sub] = (xsb,)

        def s3(sub):
            (xsb,) = st8[sub]
            pb = npb()

            def tr(e):
                ins = None
                for kc in range(8):
                    ins = e.transpose(out=pb.ap[:, kc * 128:(kc + 1) * 128], in_=xsb.ap[:, kc * 128:(kc + 1) * 128],
                                      identity=identb.ap)
                return ins
            op("pe", tr, reads=[xsb, identb], writes=[pb])
            for kc in range(8):
                dst = hx.ap[:, kc, sub * 128:(sub + 1) * 128]
                srcp = pb.ap[:, kc * 128:(kc + 1) * 128]
                if kc % 2 == 0:
                    op("act", lambda e, dst=dst, srcp=srcp, kc=kc: e.activation(
                        out=dst, in_=srcp, func=AF.Identity, scale=A2.ap[:, kc:kc + 1], bias=B2.ap[:, kc:kc + 1]),
                       reads=[pb, A2, B2], writes=[hx])
                else:
                    op("dve", lambda e, dst=dst, srcp=srcp, kc=kc: e.tensor_scalar(
                        out=dst, in0=srcp, scalar1=A2.ap[:, kc:kc + 1], scalar2=B2.ap[:, kc:kc + 1], op0=ALU.mult, op1=ALU.add),
                       reads=[pb, A2, B2], writes=[hx])

        for slot in range(6):
            if slot < 4:
                s1(slot)
            if 1 <= slot < 5:
                s2(slot - 1)
            if slot >= 2:
                s3(slot - 2)
            yield
        P.dma("sp", h2v[:, :, tsl], hx.ap, reads=[hx])
        op("pool", lambda e: e.tensor_copy(out=halp.ap[:, t, :, 0:1], in_=hx.ap[:, :, 0:1]), reads=[hx], writes=[halp])
        op("pool", lambda e: e.tensor_copy(out=halp.ap[:, t, :, 1:2], in_=hx.ap[:, :, 511:512]), reads=[hx], writes=[halp])
        yield

    interleave([chainC(0)])
    for t in range(T // 512):
        gens = [chainS(t)]
        if t + 1 < T // 512:
            gens.append(chainC(t + 1))
        interleave(gens)
    if stop_after <= 6:
        return finish(P, nc, out, ar)

    P.barrier()
    ar.reset()
    w_up_bf = ar.alloc([8, 2 * DFF], BF16, "w_up_bf")
    wub = [Buf("wup%d" % i) for i in range(4)]
    for pi in (0, 1, 2, 3):
        c0 = pi * 1408
        P.dma("sp", w_up_bf.ap[:, :, c0:c0 + 1408], wup_s[:, :, c0:c0 + 1408], writes=[wub[pi]])
    hxf = [ar.alloc([8, 512], BF16, "hxf%d" % i) for i in range(2)]
    hal = [ar.alloc([8, 2], BF16, "hal%d" % i) for i in range(2)]
    hg = ar.alloc([NFC, 2], F32, "hg")
    fT = ar.alloc([NFC, 512], BF16, "fT")
    ut = [ar.alloc([512], BF16, "ut%d" % i) for i in range(3)]
    gstg = [ar.alloc([514], F32, "gstg%d" % i) for i in range(3)]
    facc = [ar.alloc([512], F32, "facc%d" % i) for i in range(3)]
    wdt = [ar.alloc([D], BF16, "wdt%d" % i) for i in range(3)]
    dacc = [ar.alloc([2, D], F32, "dacc%d" % i) for i in range(2)]
    x1t = [ar.alloc([D], F32, "x1f%d" % i) for i in range(2)]
    yst = [ar.alloc([4], F32, "ystf%d" % i) for i in range(4)]
    junkf = ar.alloc([512], BF16, "junkff")
    psacc = [psf[1], psf[2], psf[3], psf[4]]
    cnt3 = {"f": 0, "w": 0, "e": 0, "p": 0}

    def epilogue(t, pair, da):
        for s2 in range(2):
            sub = pair * 2 + s2
            r0 = t * 512 + sub * 128
            cnt3["e"] += 1
            x1_, ys_ = x1t[cnt3["e"] % 2], yst[cnt3["e"] % 4]
            dv_ = da.ap[:, s2, :]
            P.dma("pool", x1_.ap, x1_s[r0:r0 + 128, :], writes=[x1_])
            op("act", lambda e, ys_=ys_: e.memzero(ys_.ap), writes=[ys_])
            for half in range(2):
                hs = slice(half * 512, (half + 1) * 512)
                op("act", lambda e, half=half, hs=hs, ys_=ys_, dv_=dv_, jk=junkf: e.activation(
                    out=jk.ap, in_=dv_[:, hs], func=AF.Square, accum_out=ys_.ap[:, half:half + 1]),
                   reads=[da, ys_], writes=[junkf, ys_])
            yield
            op("pool", lambda e, ys_=ys_: e.tensor_tensor(out=ys_.ap[:, 2:3], in0=ys_.ap[:, 0:1], in1=ys_.ap[:, 1:2], op=ALU.add),
               reads=[ys_], writes=[ys_])
            op("act", lambda e, ys_=ys_: e.activation(out=ys_.ap[:, 3:4], in_=ys_.ap[:, 2:3], func=AF.Sqrt, scale=1.0 / D, bias=eps1.ap),
               reads=[ys_, eps1], writes=[ys_])
            op("dve", lambda e, ys_=ys_: e.reciprocal(out=ys_.ap[:, 3:4], in_=ys_.ap[:, 3:4]), reads=[ys_], writes=[ys_])
            yield
            op("dve", lambda e, ys_=ys_, dv_=dv_: e.scalar_tensor_tensor(
                out=dv_, in0=dv_, scalar=ys_.ap[:, 3:4], in1=gt2g.ap, op0=ALU.mult, op1=ALU.mult),
               reads=[da, ys_, gt2g], writes=[da])
            yield
            op("pool", lambda e, x1_=x1_, dv_=dv_: e.tensor_tensor(out=dv_, in0=dv_, in1=x1_.ap, op=ALU.add),
               reads=[da, x1_], writes=[da])
            P.dma("pool", out[r0:r0 + 128, :], dv_, reads=[da])
            yield

    fTb = [Buf("fT%d" % j) for j in range(NFC)]

    def down_chunk(pair, j):
        cnt3["w"] += 1
        wt = wdt[cnt3["w"] % 3]
        P.dma("sp", wt.ap, wd_s[j], writes=[wt])
        for a in range(4):
            sub, half = pair * 2 + a // 2, a % 2
            acc_ = psacc[a]

            def fn(e, acc_=acc_, wt=wt, j=j, sub=sub, half=half):
                return e.matmul(acc_.ap, lhsT=fT.ap[:, j, sub * 128:(sub + 1) * 128], rhs=wt.ap[:, half * 512:(half + 1) * 512],
                                start=(j == 0), stop=(j == NFC - 1))
            op("pe", fn, reads=[fTb[j], wt] + ([acc_] if j > 0 else []), writes=[acc_])

    def evac_pair(pair):
        cnt3["p"] += 1
        da = dacc[cnt3["p"] % 2]
        for a in range(4):
            s2, half = a // 2, a % 2
            op("act", lambda e, a=a, s2=s2, half=half, da=da: e.copy(out=da.ap[:, s2, half * 512:(half + 1) * 512], in_=psacc[a].ap),
               reads=[psacc[a]], writes=[da])
        pending.append(epilogue(cur_t[0], pair, da))

    cur_t = [0]
    pending = []

    def pump(n=1):
        for _ in range(n):
            if not pending:
                return
            try:
                next(pending[0])
            except StopIteration:
                pending.pop(0)

    P.dma("sp", hxf[0].ap, h2v[:, :, 0:512], writes=[hxf[0]])
    for t in range(T // 512):
        tsl = slice(t * 512, (t + 1) * 512)
        cur_t[0] = t
        hx = hxf[t % 2]
        hl = hal[t % 2]
        if t + 1 < T // 512:
            P.dma("sp", hxf[(t + 1) % 2].ap, h2v[:, :, (t + 1) * 512:(t + 2) * 512], writes=[hxf[(t + 1) % 2]])
        op("pool", lambda e, hl=hl: e.memset(hl.ap, 0.0), writes=[hl])
        if t > 0:
            op("pool", lambda e, hl=hl, t=t: e.tensor_copy(out=hl.ap[:, :, 0:1], in_=halp.ap[:, t - 1, :, 1:2]), reads=[halp], writes=[hl])
        if t < T // 512 - 1:
            op("pool", lambda e, hl=hl, t=t: e.tensor_copy(out=hl.ap[:, :, 1:2], in_=halp.ap[:, t + 1, :, 0:1]), reads=[halp], writes=[hl])
        ph = psf[0]
        mm(ph, None, [(ph.ap[:, j * 2:j * 2 + 2], [(w_up_bf.ap[:, kc, j * 128:(j + 1) * 128], hl.ap[:, kc, :]) for kc in range(8)])
                      for j in range(NFC)], [wub[0], wub[1], hl])
        op("dve", lambda e: e.tensor_copy(out=hg.ap, in_=ph.ap[:, 0:2 * NFC].rearrange("p (j c) -> p j c", c=2)),
           reads=[ph], writes=[hg])
        def ffn_tail(j, gs, fa, u_):
            op("dve", lambda e: e.tensor_scalar(out=fa.ap, in0=gs.ap[:, 0:512], scalar1=cwf.ap[:, j:j + 1],
                                                scalar2=None, op0=ALU.mult), reads=[gs, cwf], writes=[fa])
            for tap in (1, 2):
                op("dve", lambda e, tap=tap: e.scalar_tensor_tensor(
                    out=fa.ap, in0=gs.ap[:, tap:tap + 512], scalar=cwf.ap[:, tap * NFC + j:tap * NFC + j + 1], in1=fa.ap,
                    op0=ALU.mult, op1=ALU.add), reads=[gs, cwf, fa], writes=[fa])
            op("act", lambda e: e.activation(out=fa.ap, in_=fa.ap, func=AF.Silu), reads=[fa], writes=[fa])
            op("dve", lambda e: e.tensor_tensor(out=fT.ap[:, j, :], in0=fa.ap, in1=u_.ap, op=ALU.mult),
               reads=[fa, u_], writes=[fTb[j]])

        prevf = None
        dq_ = []
        for j in list(range(NFC)) + [None]:
            if j is not None:
                cnt3["f"] += 1
                fi = cnt3["f"]
                gs, fa, u_ = gstg[fi % 3], facc[fi % 3], ut[fi % 3]
                pg = nps_ffn()
                mm(pg, pg.ap, [(None, [(w_up_bf.ap[:, kc, j * 128:(j + 1) * 128], hx.ap[:, kc, :]) for kc in range(8)])],
                   [wub[(j * 128) // 1408], hx])
                op("act", lambda e, gs=gs, pg=pg: e.copy(out=gs.ap[:, 1:513], in_=pg.ap), reads=[pg], writes=[gs])
                pu = nps_ffn()
                mm(pu, pu.ap, [(None, [(w_up_bf.ap[:, kc, DFF + j * 128:DFF + (j + 1) * 128], hx.ap[:, kc, :]) for kc in range(8)])],
                   [wub[(DFF + j * 128) // 1408], hx])
                op("act", lambda e, u_=u_, pu=pu: e.copy(out=u_.ap, in_=pu.ap), reads=[pu], writes=[u_])
                op("pool", lambda e, gs=gs, j=j: e.tensor_copy(out=gs.ap[:, 0:1], in_=hg.ap[:, j, 0:1]), reads=[hg], writes=[gs])
                op("pool", lambda e, gs=gs, j=j: e.tensor_copy(out=gs.ap[:, 513:514], in_=hg.ap[:, j, 1:2]), reads=[hg], writes=[gs])
            if prevf is not None:
                ffn_tail(*prevf)
                dq_.append(prevf[0])
            if len(dq_) > 2:
                down_chunk(0, dq_.pop(0))
            prevf = (j, gs, fa, u_) if j is not None else None
            pump(1)
        while dq_:
            down_chunk(0, dq_.pop(0))
        evac_pair(0)
        while pending:
            pump(1)
        for j in range(NFC):
            down_chunk(1, j)
        evac_pair(1)
    while pending:
        pump(1)

    return finish(P, nc, out, ar)


def finish(P, nc, out, ar):
    P.barrier()
    if DEBUG:
        d2, Gm, Bt, Gc, Bc, S = ar.dbg
        P.dma("sp", d2[:, 0:256], Gm.ap.rearrange("p a b -> p (a b)"), reads=[Gm])
        P.dma("sp", d2[:, 256:512], Bt.ap.rearrange("p a b -> p (a b)"), reads=[Bt])
        P.dma("sp", d2[:, 512:528], Gc.ap.rearrange("p a b -> p (a b)"), reads=[Gc])
        P.dma("sp", d2[:, 528:544], Bc.ap.rearrange("p a b -> p (a b)"), reads=[Bc])
        P.dma("sp", d2[:, 1024:2048], S.ap.rearrange("p a b -> p (a b)"), reads=[S])
        P.barrier()
    P.emit()
    return nc


_CACHE = {}


def kernel(**inputs):
    n = 8
    if "nc" not in _CACHE:
        _CACHE["nc"] = build_program()
    nc = _CACHE["nc"]
    f32 = np.float32
    shared = {}
    for k in ("w_mod", "b_mod", "g_pre_mix", "g_post_mix", "g_pre_ffn", "g_post_ffn", "w_in", "conv_qkv",
              "a_log", "dt_bias", "o_norm", "pool_w", "pool_scale", "w_out", "w_up", "conv_ffn", "w_down"):
        a = np.ascontiguousarray(np.asarray(inputs[k], dtype=f32)[0])
        if k in ("a_log", "dt_bias"):
            a = a.reshape(8)
        shared[k] = a
    x = np.asarray(inputs["x"], dtype=f32)
    ctx = np.asarray(inputs["ctx"], dtype=f32)
    c = np.asarray(inputs["c"], dtype=f32)
    c_ctx = np.asarray(inputs["c_ctx"], dtype=f32)
    in_maps = []
    for b in range(n):
        m = dict(shared)
        m["x"] = np.ascontiguousarray(x[b])
        m["ctx"] = np.ascontiguousarray(ctx[b])
        m["cc"] = np.ascontiguousarray(np.stack([c[b], c_ctx], axis=0))
        in_maps.append(m)
    res = run_bass_kernel_spmd(nc, in_maps, core_ids=list(range(n)))
    _CACHE["res"] = res
    return np.stack([np.asarray(r["out"], dtype=f32) for r in res.results], axis=0)
```

```python
import contextlib
import numpy as np
import concourse.bass as bass
import concourse.mybir as mybir
from concourse.bass_utils import run_bass_kernel_spmd

F32 = mybir.dt.float32
BF16 = mybir.dt.bfloat16
AF = mybir.ActivationFunctionType
ALU = mybir.AluOpType

D = 1024
T = 4096
TC = 256
INC = 2576
DFF = 2816
NFC = 22
CH = 128
EPS = 1e-6
DEBUG = False
NCH_DEBUG = 0


class Buf:
    __slots__ = ("name", "w", "r")

    def __init__(self, name=""):
        self.name = name
        self.w = None
        self.r = []


class TL:
    __slots__ = ("ap", "b")

    def __init__(self, ap, name=""):
        self.ap = ap
        self.b = Buf(name)


class Prog:
    ENGS = ("pe", "dve", "act", "pool", "sp")

    def __init__(self, nc, n_dma_sems=16):
        self.nc = nc
        self.stack = contextlib.ExitStack()
        self.ops = {e: [] for e in self.ENGS}
        self.sems = {}
        self.n_dma = n_dma_sems
        self.gen = -1
        self.n_inst = 0
        self._new_gen()

    def _new_gen(self):
        self.gen += 1
        g = self.gen
        self.count = {e: 0 for e in self.ENGS}
        old_seen = getattr(self, "seen", None)
        self.seen = {e: {} for e in self.ENGS}
        for e in ("pe", "dve", "act", "pool"):
            self.sems[(e, g)] = self.stack.enter_context(self.nc.semaphore("s_%s_%d" % (e, g)))
        if g == 0:
            self.dma_cnt = {}
            self.dma_rr = {}
            self.dma_n = {"sp": self.n_dma, "act": 8, "pool": 8}
            for q, nq in self.dma_n.items():
                self.dma_rr[q] = 0
                for k in range(nq):
                    key = ("d_%s%d" % (q, k), -1)
                    self.sems[key] = self.stack.enter_context(self.nc.semaphore("d_%s%d" % (q, k)))
                    self.dma_cnt[key] = 0
        else:
            for e in self.ENGS:
                for key, v in old_seen[e].items():
                    if key[1] == -1:
                        self.seen[e][key] = v

    def sbuf(self, name, shape, dtype):
        return self.stack.enter_context(self.nc.sbuf_tensor(name, list(shape), dtype))

    def psum(self, name, shape, dtype):
        return self.stack.enter_context(self.nc.psum_tensor(name, list(shape), dtype))

    def _deps(self, eng, reads, writes):
        deps = {}

        def add(tok):
            if tok is None:
                return
            k, v = tok
            if k[1] != self.gen and k[1] != -1:
                return
            if deps.get(k, 0) < v:
                deps[k] = v
        for b in reads:
            add(b.w)
        for b in writes:
            add(b.w)
            for t in b.r:
                add(t)
        waits = []
        seen = self.seen[eng]
        for k, v in deps.items():
            if seen.get(k, 0) < v:
                seen[k] = v
                waits.append((k, v))
        return waits

    def _commit(self, tok, reads, writes):
        for b in reads:
            b.r.append(tok)
            if len(b.r) > 64:
                best = {}
                for k, v in b.r:
                    if best.get(k, 0) < v:
                        best[k] = v
                b.r = list(best.items())
        for b in writes:
            b.w = tok
            b.r = []

    def op(self, eng, fn, reads=(), writes=()):
        reads = [t.b if isinstance(t, TL) else t for t in reads]
        writes = [t.b if isinstance(t, TL) else t for t in writes]
        waits = self._deps(eng, reads, writes)
        self.count[eng] += 1
        tok = ((eng, self.gen), self.count[eng])
        self.ops[eng].append((waits, fn, ((eng, self.gen), 1)))
        self._commit(tok, reads, writes)
        self.n_inst += 1
        return tok

    def dma(self, q, out, in_, reads=(), writes=(), **kw):
        reads = [t.b if isinstance(t, TL) else t for t in reads]
        writes = [t.b if isinstance(t, TL) else t for t in writes]
        k = self.dma_rr[q]
        self.dma_rr[q] = (k + 1) % self.dma_n[q]
        key = ("d_%s%d" % (q, k), -1)
        waits = self._deps(q, reads, writes)
        prev = self.dma_cnt[key]
        if prev > 0 and self.seen[q].get(key, 0) < prev:
            self.seen[q][key] = prev
            waits.append((key, prev))
        self.dma_cnt[key] = prev + 16
        tok = (key, prev + 16)

        def fn(e, out=out, in_=in_, kw=kw):
            return e.dma_start(out=out, in_=in_, **kw)
        self.ops[q].append((waits, fn, (key, 16)))
        self._commit(tok, reads, writes)
        self.n_inst += 1
        return tok

    def barrier(self):
        for e in self.ENGS:
            waits = []
            for k in ("pe", "dve", "act", "pool"):
                v = self.count[k]
                key = (k, self.gen)
                if k != e and v > 0 and self.seen[e].get(key, 0) < v:
                    self.seen[e][key] = v
                    waits.append((key, v))
            for key, v in self.dma_cnt.items():
                if v > 0 and self.seen[e].get(key, 0) < v:
                    self.seen[e][key] = v
                    waits.append((key, v))
            if waits:
                self.ops[e].append((waits, None, None))
        self._new_gen()

    def emit(self):
        nc = self.nc
        sems = self.sems

        def replay(eng_name, e):
            for waits, fn, inc in self.ops[eng_name]:
                for k, v in waits:
                    e.wait_ge(sems[k], v)
                if fn is not None:
                    ins = fn(e)
                    ins.then_inc(sems[inc[0]], inc[1])

        with nc.Block() as block:
            @block.tensor
            def _(e):
                replay("pe", e)

            @block.vector
            def _(e):
                replay("dve", e)

            @block.scalar
            def _(e):
                replay("act", e)

            @block.gpsimd
            def _(e):
                replay("pool", e)

            @block.sync
            def _(e):
                replay("sp", e)
        self.stack.close()


class Arena:
    def __init__(self, P, nbytes):
        self.cap = nbytes // 2
        self.t = P.sbuf("arena", [128, self.cap], BF16)
        self.off = 0

    def reset(self):
        self.off = 0

    def alloc(self, shape, dtype, name=""):
        n = 1
        for s in shape:
            n *= s
        w = n * (2 if dtype == F32 else 1)
        off = (self.off + 31) // 32 * 32
        assert off + w <= self.cap, ("arena overflow", name, off + w, self.cap)
        ap = self.t[:, off:off + w]
        self.off = off + w
        if dtype == F32:
            ap = ap.bitcast(F32)
        if len(shape) == 2:
            ap = ap.rearrange("p (a b) -> p a b", a=shape[0])
        elif len(shape) == 3:
            ap = ap.rearrange("p (a b c) -> p a b c", a=shape[0], b=shape[1])
        return TL(ap, name)


def build_program(stop_after=99):
    nc = bass.Bass("TRN2", target_bir_lowering=False)

    def din(name, shape):
        return nc.dram_tensor(name, list(shape), F32, kind="ExternalInput").ap()

    x = din("x", [T, D])
    ctx = din("ctx", [TC, D])
    cc = din("cc", [2, D])
    w_mod = din("w_mod", [D, 6 * D])
    b_mod = din("b_mod", [6 * D])
    g_pre_mix = din("g_pre_mix", [D])
    g_post_mix = din("g_post_mix", [D])
    g_pre_ffn = din("g_pre_ffn", [D])
    g_post_ffn = din("g_post_ffn", [D])
    w_in = din("w_in", [D, INC])
    conv_qkv = din("conv_qkv", [5, 1536])
    a_log = din("a_log", [8])
    dt_bias = din("dt_bias", [8])
    o_norm = din("o_norm", [128])
    pool_w = din("pool_w", [4, 128, 128])
    pool_scale = din("pool_scale", [512])
    w_out = din("w_out", [D, D])
    w_up = din("w_up", [D, 2 * DFF])
    conv_ffn = din("conv_ffn", [3, DFF])
    w_down = din("w_down", [DFF, D])
    out = nc.dram_tensor("out", [T, D], F32, kind="ExternalOutput").ap()

    skind = "ExternalOutput" if DEBUG else "Internal"

    def scr(name, shape, dt):
        return nc.dram_tensor(name, list(shape), dt, kind=skind).ap()

    qT_s = scr("qT_s", [4, 128, T], BF16)
    kT_s = scr("kT_s", [4, 128, T], BF16)
    vT_s = scr("vT_s", [4, 128, T], BF16)
    kT_c = scr("kT_c", [4, 128, TC], BF16)
    vT_c = scr("vT_c", [4, 128, TC], BF16)
    zT_s = scr("zT_s", [4, 128, T], F32)
    uT_s = scr("uT_s", [4, 128, T], F32)
    poolT_s = scr("poolT_s", [4, 128, T], BF16)
    of_s = scr("of_s", [T // CH, 128, 512], F32)
    ob_s = scr("ob_s", [T // CH, 128, 512], F32)
    x1_s = scr("x1_s", [T, D], F32)
    hx2T_s = scr("hx2T_s", [8, 128, T], BF16)
    wd_s = scr("wd_s", [NFC, 128, D], BF16)
    wout_s = scr("wout_s", [128, 8, D], BF16)
    wup_s = scr("wup_s", [128, 8, 2 * DFF], BF16)
    dbg_s = scr("dbg_s", [128, 4096], F32)
    dbg2_s = scr("dbg2_s", [128, 4096], F32)

    P = Prog(nc)
    op = P.op

    def pers(name, shape, dt):
        return TL(P.sbuf(name, [128] + list(shape), dt)[:], name)

    identf = pers("identf", [128], F32)
    identb = pers("identb", [128], BF16)
    onesf = pers("onesf", [128], F32)
    onesb = pers("onesb", [128], BF16)
    Lmask = pers("Lmask", [128], F32)
    Umask = pers("Umask", [128], F32)
    mbF = pers("mbF", [128], F32)
    mbB = pers("mbB", [128], F32)
    stF4 = pers("stF4", [4, 128], F32)
    stB4 = pers("stB4", [4, 128], F32)
    I4 = pers("I4", [4, 128], F32)
    eps1 = pers("eps1", [1], F32)
    epsq = pers("epsq", [1], F32)
    one1 = pers("one1", [1], F32)
    modfm = pers("modfm", [32, 2], F32)
    bmodfm = pers("bmodfm", [48], F32)
    gfm = pers("gfm", [16], F32)
    A1 = pers("A1", [8], F32)
    B1 = pers("B1", [8], F32)
    A1c = pers("A1c", [8], F32)
    B1c = pers("B1c", [8], F32)
    A2 = pers("A2", [8], F32)
    B2 = pers("B2", [8], F32)
    gt1g = pers("gt1g", [D], F32)
    gt2g = pers("gt2g", [D], F32)
    cw = pers("cw", [60], F32)
    cwf = pers("cwf", [66], F32)
    onorm = pers("onorm", [1], F32)
    pscale = pers("pscale", [4], F32)
    negA = pers("negA", [8], F32)
    dtb = pers("dtb", [8], F32)
    Gm = pers("Gm", [T // CH, 8], F32)
    Bt = pers("Bt", [T // CH, 8], F32)
    Gc = pers("Gc", [TC // CH, 8], F32)
    Bc = pers("Bc", [TC // CH, 8], F32)
    S = pers("S", [8, 128], F32)
    Sb = pers("Sb", [8, 128], BF16)
    halp = pers("halp", [T // 512, 8, 2], BF16)

    psf = [TL(P.psum("psf%d" % i, [128, 512], F32)[:], "psf%d" % i) for i in range(6)]
    psb = [TL(P.psum("psb%d" % i, [128, 1024], BF16)[:], "psb%d" % i) for i in range(2)]
    rr = {"f": 0, "b": 0}

    def nps():
        rr["f"] = rr["f"] % 5 + 1
        return psf[rr["f"]]

    def npb():
        rr["b"] = (rr["b"] + 1) % 2
        return psb[rr["b"]]

    ffn_banks = [psf[5], TL(psb[0].ap.bitcast(F32), "psb0f"), TL(psb[1].ap.bitcast(F32), "psb1f")]
    ffn_banks[1].b = psb[0].b
    ffn_banks[2].b = psb[1].b

    def nps_ffn():
        rr["b"] = (rr["b"] + 1) % 3
        return ffn_banks[rr["b"]]

    ar = Arena(P, 177152)
    if DEBUG:
        print("sbuf bytes remaining after arena:", nc.sbuf_bytes_remaining)
    ar.dbg = (dbg2_s, Gm, Bt, Gc, Bc, S)

    def mm(outT, out_ap, groups, reads):
        def fn(e, groups=groups, out_ap=out_ap):
            ins = None
            for oa, pairs in groups:
                n = len(pairs)
                for i, (l, r) in enumerate(pairs):
                    ins = e.matmul(oa if oa is not None else out_ap, lhsT=l, rhs=r, start=(i == 0), stop=(i == n - 1))
            return ins
        return op("pe", fn, reads=reads, writes=[outT])

    def sel(t, pattern, cm, cmp_op, fill=0.0):
        op("pool", lambda e: e.affine_select(out=t.ap, in_=t.ap, pattern=pattern, compare_op=cmp_op, fill=fill,
                                             base=0, channel_multiplier=cm), reads=[t], writes=[t])

    for t_ in (identf, onesf, Lmask, Umask):
        op("pool", lambda e, t_=t_: e.memset(t_.ap, 1.0), writes=[t_])
    sel(identf, [[-1, 128]], 1, ALU.is_equal)
    sel(Lmask, [[1, 128]], -1, ALU.is_ge)
    sel(Umask, [[-1, 128]], 1, ALU.is_ge)
    op("dve", lambda e: e.tensor_copy(out=identb.ap, in_=identf.ap), reads=[identf], writes=[identb])
    op("dve", lambda e: e.tensor_copy(out=onesb.ap, in_=onesf.ap), reads=[onesf], writes=[onesb])
    op("dve", lambda e: e.tensor_scalar(out=mbF.ap, in0=Lmask.ap, scalar1=-1.0, scalar2=1e5, op0=ALU.add, op1=ALU.mult),
       reads=[Lmask], writes=[mbF])
    op("dve", lambda e: e.tensor_scalar(out=mbB.ap, in0=Umask.ap, scalar1=-1.0, scalar2=1e5, op0=ALU.add, op1=ALU.mult),
       reads=[Umask], writes=[mbB])
    for h in range(4):
        op("dve", lambda e, h=h: e.tensor_tensor(out=stF4.ap[:, h, :], in0=Lmask.ap, in1=identf.ap, op=ALU.subtract),
           reads=[Lmask, identf], writes=[stF4])
        op("dve", lambda e, h=h: e.tensor_tensor(out=stB4.ap[:, h, :], in0=Umask.ap, in1=identf.ap, op=ALU.subtract),
           reads=[Umask, identf], writes=[stB4])
        op("dve", lambda e, h=h: e.tensor_copy(out=I4.ap[:, h, :], in_=identf.ap), reads=[identf], writes=[I4])
    blkm = {}
    for bsz in (16, 32, 64):
        E_ = ar.alloc([128], F32, "Eblk%d" % bsz)
        op("pool", lambda e, E_=E_: e.memset(E_.ap, 1.0), writes=[E_])
        op("pool", lambda e, E_=E_, bsz=bsz: e.affine_select(out=E_.ap, in_=E_.ap, pattern=[[1, 128]], compare_op=ALU.is_ge,
                                                             fill=0.0, base=0, channel_multiplier=-bsz), reads=[E_], writes=[E_])
        op("pool", lambda e, E_=E_, bsz=bsz: e.affine_select(out=E_.ap, in_=E_.ap, pattern=[[-1, 128]], compare_op=ALU.is_ge,
                                                             fill=0.0, base=bsz - 1, channel_multiplier=bsz), reads=[E_], writes=[E_])
        ps = nps()
        mm(ps, ps.ap[:, 0:128], [(None, [(E_.ap, E_.ap)])], [E_])
        blkm[bsz] = ar.alloc([128], F32, "blk%d" % bsz)
        op("dve", lambda e, ps=ps, bsz=bsz: e.tensor_copy(out=blkm[bsz].ap, in_=ps.ap[:, 0:128]), reads=[ps], writes=[blkm[bsz]])
    off16 = ar.alloc([128], F32, "off16")
    off32 = ar.alloc([128], F32, "off32")
    off64 = ar.alloc([128], F32, "off64")
    op("dve", lambda e: e.tensor_tensor(out=off16.ap, in0=blkm[32].ap, in1=blkm[16].ap, op=ALU.subtract), reads=[blkm[32], blkm[16]], writes=[off16])
    op("dve", lambda e: e.tensor_tensor(out=off32.ap, in0=blkm[64].ap, in1=blkm[32].ap, op=ALU.subtract), reads=[blkm[64], blkm[32]], writes=[off32])
    op("dve", lambda e: e.tensor_tensor(out=off64.ap, in0=onesf.ap, in1=blkm[64].ap, op=ALU.subtract), reads=[onesf, blkm[64]], writes=[off64])
    mk4 = {}
    for nm_, src_m in (("bm16", blkm[16]), ("off16", off16), ("off32", off32), ("off64", off64)):
        mk4[nm_] = pers(nm_ + "_4b", [4, 128], BF16)
        for h in range(4):
            op("dve", lambda e, nm_=nm_, src_m=src_m, h=h: e.tensor_copy(out=mk4[nm_].ap[:, h, :], in_=src_m.ap),
               reads=[src_m], writes=[mk4[nm_]])
    op("dve", lambda e: e.memset(eps1.ap, EPS), writes=[eps1])
    op("dve", lambda e: e.memset(epsq.ap, EPS * 128.0), writes=[epsq])
    op("dve", lambda e: e.memset(one1.ap, 1.0), writes=[one1])

    ldst = [ar.alloc([128], F32, "ldst%d" % i) for i in range(2)]
    ldi = [0]

    def load_fm(src2d, n, dst_ap, dstT):
        st = ldst[ldi[0] % 2]
        ldi[0] += 1
        P.dma("sp", st.ap[0:n, :], src2d, writes=[st])
        ps = nps()
        mm(ps, ps.ap[:, 0:n], [(None, [(st.ap[0:n, :], identf.ap[0:n, 0:n])])], [st, identf])
        op("dve", lambda e: e.tensor_copy(out=dst_ap, in_=ps.ap[:, 0:n]), reads=[ps], writes=[dstT])

    load_fm(b_mod.rearrange("(c p) -> c p", p=128), 48, bmodfm.ap, bmodfm)
    load_fm(g_pre_mix.rearrange("(c p) -> c p", p=128), 8, gfm.ap[:, 0:8], gfm)
    load_fm(g_pre_ffn.rearrange("(c p) -> c p", p=128), 8, gfm.ap[:, 8:16], gfm)
    load_fm(conv_qkv.rearrange("t (c p) -> (t c) p", p=128), 60, cw.ap, cw)
    load_fm(conv_ffn.rearrange("t (c p) -> (t c) p", p=128), 66, cwf.ap, cwf)
    load_fm(pool_scale.rearrange("(c p) -> c p", p=128), 4, pscale.ap, pscale)
    load_fm(o_norm.rearrange("(c p) -> c p", p=128), 1, onorm.ap, onorm)
    P.dma("sp", negA.ap, a_log.partition_broadcast(128), writes=[negA])
    P.dma("sp", dtb.ap, dt_bias.partition_broadcast(128), writes=[dtb])
    op("act", lambda e: e.activation(out=negA.ap, in_=negA.ap, func=AF.Exp), reads=[negA], writes=[negA])
    op("dve", lambda e: e.tensor_scalar(out=negA.ap, in0=negA.ap, scalar1=-1.0, scalar2=None, op0=ALU.mult),
       reads=[negA], writes=[negA])

    ccst = ar.alloc([128], F32, "ccst")
    sT = ar.alloc([8, 2], F32, "sT")
    sRep = ar.alloc([8, 128], F32, "sRep")
    wst = [ar.alloc([8, 512], F32, "wst%d" % i) for i in range(5)]
    browst = [ar.alloc([512], F32, "brow%d" % i) for i in range(2)]
    grow = [ar.alloc([512], F32, "grow%d" % i) for i in range(2)]
    P.dma("sp", ccst.ap[0:16, :], cc.rearrange("r (k p) -> (r k) p", p=128), writes=[ccst])
    ps = nps()
    mm(ps, ps.ap[:, 0:16], [(None, [(ccst.ap[0:16, :], identf.ap[0:16, 0:16])])], [ccst, identf])
    for r in range(2):
        op("act", lambda e, r=r, ps=ps: e.activation(out=sT.ap[:, :, r], in_=ps.ap[:, r * 8:(r + 1) * 8], func=AF.Silu),
           reads=[ps], writes=[sT])
    for kc in range(8):
        op("dve", lambda e, kc=kc: e.tensor_scalar(out=sRep.ap[:, kc, :], in0=onesf.ap, scalar1=sT.ap[:, kc, 0:1],
                                                   scalar2=None, op0=ALU.mult), reads=[onesf, sT], writes=[sRep])
    psm = psf[0]
    wm_v = w_mod.rearrange("(k p) n -> p k n", p=128)
    sTb = ar.alloc([8, 2], BF16, "sTb")
    sRepb = ar.alloc([8, 128], BF16, "sRepb")
    wbf = [ar.alloc([8, 512], BF16, "wbf%d" % i) for i in range(3)]
    op("dve", lambda e: e.tensor_copy(out=sTb.ap, in_=sT.ap), reads=[sT], writes=[sTb])
    op("dve", lambda e: e.tensor_copy(out=sRepb.ap, in_=sRep.ap), reads=[sRep], writes=[sRepb])
    sT_f, sRep_f = sT, sRep
    sT, sRep = sTb, sRepb
    fmi = 0
    for hb in range(12):
        blk = hb // 2
        wf = wst[hb % 5]
        P.dma("sp", wf.ap, wm_v[:, :, hb * 512:(hb + 1) * 512], writes=[wf])
        w = wbf[hb % 3]
        if hb % 2 == 0:
            op("dve", lambda e, w=w, wf=wf: e.tensor_copy(out=w.ap, in_=wf.ap), reads=[wf], writes=[w])
        else:
            op("act", lambda e, w=w, wf=wf: e.copy(out=w.ap, in_=wf.ap), reads=[wf], writes=[w])
        if blk in (2, 5):
            ps = nps()
            mm(ps, ps.ap, [(None, [(sRep.ap[:, kc, :], w.ap[:, kc, :]) for kc in range(8)])], [sRep, w])
            br = browst[hb % 2]
            gr = grow[hb % 2]
            dst = gt1g if blk == 2 else gt2g
            gsrc = g_post_mix if blk == 2 else g_post_ffn
            half = hb % 2
            P.dma("sp", br.ap, b_mod[hb * 512:(hb + 1) * 512].partition_broadcast(128), writes=[br])
            P.dma("sp", gr.ap, gsrc[half * 512:(half + 1) * 512].partition_broadcast(128), writes=[gr])
            op("dve", lambda e, ps=ps, br=br: e.tensor_tensor(out=br.ap, in0=ps.ap, in1=br.ap, op=ALU.add),
               reads=[ps, br], writes=[br])
            op("dve", lambda e, br=br, gr=gr, dst=dst, half=half: e.tensor_tensor(
                out=dst.ap[:, half * 512:(half + 1) * 512], in0=br.ap, in1=gr.ap, op=ALU.mult),
               reads=[br, gr], writes=[dst])
        else:
            groups = []
            for fc in range(4):
                groups.append((psm.ap[:, fmi * 2:fmi * 2 + 2],
                               [(w.ap[:, kc, fc * 128:(fc + 1) * 128], sT.ap[:, kc, :]) for kc in range(8)]))
                fmi += 1
            mm(psm, None, groups, [w, sT])
    op("dve", lambda e: e.tensor_copy(out=modfm.ap, in_=psm.ap[:, 0:64].rearrange("p (i r) -> p i r", r=2)),
       reads=[psm], writes=[modfm])
    bsel = {0: 0, 1: 8, 2: 24, 3: 32}

    def mk_ab(Adst, Bdst, r, sh_i, sc_i, g_off, bsh, bsc):
        op("dve", lambda e: e.tensor_tensor(out=Bdst.ap, in0=modfm.ap[:, sh_i:sh_i + 8, r], in1=bmodfm.ap[:, bsh:bsh + 8],
                                            op=ALU.add), reads=[modfm, bmodfm], writes=[Bdst])
        op("dve", lambda e: e.scalar_tensor_tensor(out=Adst.ap, in0=modfm.ap[:, sc_i:sc_i + 8, r], scalar=1.0,
                                                   in1=bmodfm.ap[:, bsc:bsc + 8], op0=ALU.add, op1=ALU.add),
           reads=[modfm, bmodfm], writes=[Adst])
        op("dve", lambda e: e.tensor_tensor(out=Adst.ap, in0=Adst.ap, in1=gfm.ap[:, g_off:g_off + 8], op=ALU.mult),
           reads=[Adst, gfm], writes=[Adst])

    mk_ab(A1, B1, 0, 0, 8, 0, 0, 8)
    mk_ab(A1c, B1c, 1, 0, 8, 0, 0, 8)
    mk_ab(A2, B2, 0, 16, 24, 8, 24, 32)

    if DEBUG:
        dbgt = ar.alloc([4096], F32, "dbgt")
        op("dve", lambda e: e.memset(dbgt.ap, 0.0), writes=[dbgt])
        for i, (t_, n) in enumerate(((A1, 8), (B1, 8), (A1c, 8), (B1c, 8), (A2, 8), (B2, 8), (cw, 60), (cwf, 66),
                                     (negA, 8), (dtb, 8), (pscale, 4), (onorm, 1))):
            op("dve", lambda e, t_=t_, n=n, i=i: e.tensor_copy(out=dbgt.ap[:, i * 128:i * 128 + n], in_=t_.ap),
               reads=[t_], writes=[dbgt])
        for i, t_ in enumerate((blkm[16], off16, off32, off64)):
            src_ap = t_.ap
            op("dve", lambda e, src_ap=src_ap, i=i: e.tensor_copy(out=dbgt.ap[:, 1536 + i * 128:1664 + i * 128], in_=src_ap),
               reads=[t_], writes=[dbgt])
        op("dve", lambda e: e.tensor_copy(out=dbgt.ap[:, 2048:3072], in_=gt1g.ap), reads=[gt1g], writes=[dbgt])
        op("dve", lambda e: e.tensor_copy(out=dbgt.ap[:, 3072:4096], in_=gt2g.ap), reads=[gt2g], writes=[dbgt])
        P.dma("sp", dbg_s, dbgt.ap, reads=[dbgt])

    if stop_after <= 0:
        return finish(P, nc, out, ar)

    P.barrier()
    ar.reset()
    w_in_bf = ar.alloc([8, INC], BF16, "w_in_bf")
    mark1 = ar.off
    wst2 = [ar.alloc([8, 512], F32, "wstA%d" % i) for i in range(2)]
    cast_i = [0]

    def load_cast(dstT, src_view, ncols, piece=512):
        for c0 in range(0, ncols, piece):
            n = min(piece, ncols - c0)
            i = cast_i[0]
            cast_i[0] += 1
            st = wst2[i % 2]
            P.dma("sp", st.ap[:, :, 0:n], src_view[:, :, c0:c0 + n], writes=[st])
            eng = ("dve", "act", "pool")[i % 3]
            if eng == "act":
                op("act", lambda e, st=st, c0=c0, n=n: e.copy(out=dstT.ap[:, :, c0:c0 + n], in_=st.ap[:, :, 0:n]),
                   reads=[st], writes=[dstT])
            else:
                op(eng, lambda e, st=st, c0=c0, n=n: e.tensor_copy(out=dstT.ap[:, :, c0:c0 + n], in_=st.ap[:, :, 0:n]),
                   reads=[st], writes=[dstT])

    load_cast(w_in_bf, w_in.rearrange("(k p) n -> p k n", p=128), INC)
    P.barrier()
    ar.off = mark1

    stg = [ar.alloc([516], F32, "stg%d" % j) for j in range(12)]
    xt = [ar.alloc([D], F32, "xt%d" % i) for i in range(2)]
    xs = [ar.alloc([D], BF16, "xs%d" % i) for i in range(2)]
    junk = ar.alloc([D], BF16, "junk")
    hxT = [ar.alloc([8, 512], BF16, "hxT%d" % i) for i in range(2)]
    stt = [ar.alloc([4], F32, "stt%d" % i) for i in range(4)]
    acc = [ar.alloc([512], F32, "acc%d" % i) for i in range(2)]
    qs = [[ar.alloc([512], F32, "qs%d_%d" % (p_, i)) for i in range(8)] for p_ in range(2)]
    sq = [ar.alloc([512], BF16, "sq%d" % i) for i in range(3)]
    rt = [ar.alloc([512], F32, "rt%d" % i) for i in range(3)]
    ot = [ar.alloc([512], BF16, "ot%d" % i) for i in range(4)]
    zo = [ar.alloc([512], F32, "zo%d" % i) for i in range(3)]
    gt1_ = ar.alloc([4, 8], F32, "gtmp1")
    gt2_ = ar.alloc([4, 8], F32, "gtmp2")
    cnt = {"x": 0, "acc": 0, "sq": 0, "ot": 0, "zo": 0, "st": 0}

    def rot(lst, key):
        cnt[key] += 1
        return lst[cnt[key] % len(lst)]

    def make_hxT(hx, src, r0, sub, A_, B_, xtile_in=None):
        cnt["x"] += 1
        xtile = xt[cnt["x"] % 2] if xtile_in is None else xtile_in
        xsb = xs[cnt["x"] % 2]
        st = rot(stt, "st")
        if xtile_in is None:
            P.dma("sp", xtile.ap, src[r0:r0 + 128, :], writes=[xtile])
        op("act", lambda e: e.memzero(st.ap), writes=[st])
        op("act", lambda e, jk=junk: e.activation(out=jk.ap, in_=xtile.ap, func=AF.Square, accum_out=st.ap[:, 0:1]),
           reads=[xtile, st], writes=[junk, st])
        op("act", lambda e: e.activation(out=st.ap[:, 1:2], in_=st.ap[:, 0:1], func=AF.Sqrt, scale=1.0 / D, bias=eps1.ap),
           reads=[st, eps1], writes=[st])
        op("dve", lambda e: e.reciprocal(out=st.ap[:, 2:3], in_=st.ap[:, 1:2]), reads=[st], writes=[st])
        op("dve", lambda e: e.tensor_scalar(out=xsb.ap, in0=xtile.ap, scalar1=st.ap[:, 2:3], scalar2=None, op0=ALU.mult),
           reads=[xtile, st], writes=[xsb])
        pb = npb()

        def tr(e):
            ins = None
            for kc in range(8):
                ins = e.transpose(out=pb.ap[:, kc * 128:(kc + 1) * 128], in_=xsb.ap[:, kc * 128:(kc + 1) * 128],
                                  identity=identb.ap)
            return ins
        op("pe", tr, reads=[xsb, identb], writes=[pb])
        for kc in range(8):
            dst = hx.ap[:, kc, sub * 128:(sub + 1) * 128]
            srcp = pb.ap[:, kc * 128:(kc + 1) * 128]
            if kc % 2 == 0:
                op("act", lambda e, dst=dst, srcp=srcp, kc=kc: e.activation(
                    out=dst, in_=srcp, func=AF.Identity, scale=A_.ap[:, kc:kc + 1], bias=B_.ap[:, kc:kc + 1]),
                   reads=[pb, A_, B_], writes=[hx])
            else:
                op("dve", lambda e, dst=dst, srcp=srcp, kc=kc: e.tensor_scalar(
                    out=dst, in0=srcp, scalar1=A_.ap[:, kc:kc + 1], scalar2=B_.ap[:, kc:kc + 1], op0=ALU.mult, op1=ALU.add),
                   reads=[pb, A_, B_], writes=[hx])

    def interleave(gens):
        gens = list(gens)
        while gens:
            for g_ in list(gens):
                try:
                    next(g_)
                except StopIteration:
                    gens.remove(g_)

    def project(src, Ttok, W, A_, B_, chunks, dq, dk, dv, Gd, Bd):
        ntile = Ttok // W
        nsub = W // 128
        for j in range(12):
            op("pool", lambda e, j=j: e.memset(stg[j].ap[:, 0:4], 0.0), writes=[stg[j]])

        def conv_silu(j, m0, n, tok0, qp):
            a = rot(acc, "acc")
            sj = stg[j]
            op("dve", lambda e, a=a, sj=sj, j=j: e.tensor_scalar(
                out=a.ap[:, 0:n], in0=sj.ap[:, m0:m0 + n], scalar1=cw.ap[:, j:j + 1], scalar2=None, op0=ALU.mult),
               reads=[sj, cw], writes=[a])
            for tap in range(1, 5):
                op("dve", lambda e, a=a, sj=sj, j=j, tap=tap: e.scalar_tensor_tensor(
                    out=a.ap[:, 0:n], in0=sj.ap[:, m0 + tap:m0 + tap + n], scalar=cw.ap[:, tap * 12 + j:tap * 12 + j + 1],
                    in1=a.ap[:, 0:n], op0=ALU.mult, op1=ALU.add), reads=[sj, cw, a], writes=[a])
            if j < 8:
                op("act", lambda e, a=a, j=j: e.activation(out=qs[qp][j].ap[:, 0:n], in_=a.ap[:, 0:n], func=AF.Silu),
                   reads=[a], writes=[qs[qp][j]])
            else:
                o = rot(ot, "ot")
                op("act", lambda e, a=a, o=o: e.activation(out=o.ap[:, 0:n], in_=a.ap[:, 0:n], func=AF.Silu),
                   reads=[a], writes=[o])
                P.dma("act", dv[j - 8][:, tok0:tok0 + n], o.ap[:, 0:n], reads=[o])

        def l2norm_front(j, n, qp):
            s_ = rot(sq, "sq")
            r_ = rt[cnt["sq"] % 3]
            q_ = qs[qp][j]
            isq = j < 4
            op("pool", lambda e: e.tensor_tensor(out=s_.ap[:, 0:n], in0=q_.ap[:, 0:n], in1=q_.ap[:, 0:n], op=ALU.mult),
               reads=[q_], writes=[s_])
            pn = nps()
            mm(pn, pn.ap[:, 0:n], [(None, [(onesb.ap, s_.ap[:, 0:n])])], [onesb, s_])
            op("act", lambda e: e.activation(out=r_.ap[:, 0:n], in_=pn.ap[:, 0:n], func=AF.Ln, scale=(128.0 if isq else 1.0),
                                             bias=(epsq.ap if isq else eps1.ap)), reads=[pn, epsq, eps1], writes=[r_])
            op("act", lambda e: e.activation(out=r_.ap[:, 0:n], in_=r_.ap[:, 0:n], func=AF.Exp, scale=-0.5), reads=[r_], writes=[r_])
            return (j, r_, qp)

        def l2norm_back(jr, n, tok0):
            j, r_, qp = jr
            q_ = qs[qp][j]
            o = rot(ot, "ot")
            op("pool", lambda e: e.tensor_tensor(out=o.ap[:, 0:n], in0=q_.ap[:, 0:n], in1=r_.ap[:, 0:n], op=ALU.mult),
               reads=[q_, r_], writes=[o])
            dst = dq[j] if j < 4 else dk[j - 4]
            P.dma("pool", dst[:, tok0:tok0 + n], o.ap[:, 0:n], reads=[o])

        def stageA(t):
            hx = hxT[t % 2]
            gp = psf[0]
            pend = None
            for sub in range(nsub + 1):
                if sub < nsub:
                    cnt["x"] += 1
                    xtile, xsb = xt[cnt["x"] % 2], xs[cnt["x"] % 2]
                    st = rot(stt, "st")
                    r0 = t * W + sub * 128
                    P.dma("sp", xtile.ap, src[r0:r0 + 128, :], writes=[xtile])
                    op("act", lambda e, st=st: e.memzero(st.ap), writes=[st])
                    op("act", lambda e, jk=junk, xtile=xtile, st=st: e.activation(out=jk.ap, in_=xtile.ap, func=AF.Square,
                                                                                 accum_out=st.ap[:, 0:1]),
                       reads=[xtile, st], writes=[junk, st])
                    op("act", lambda e, st=st: e.activation(out=st.ap[:, 1:2], in_=st.ap[:, 0:1], func=AF.Sqrt, scale=1.0 / D,
                                                            bias=eps1.ap), reads=[st, eps1], writes=[st])
                    op("dve", lambda e, st=st: e.reciprocal(out=st.ap[:, 2:3], in_=st.ap[:, 1:2]), reads=[st], writes=[st])
                    op("dve", lambda e, xsb=xsb, xtile=xtile, st=st: e.tensor_scalar(
                        out=xsb.ap, in0=xtile.ap, scalar1=st.ap[:, 2:3], scalar2=None, op0=ALU.mult),
                       reads=[xtile, st], writes=[xsb])
                if pend is not None:
                    psub, pxs = pend
                    pb = npb()

                    def tr(e, pb=pb, pxs=pxs):
                        ins = None
                        for kc in range(8):
                            ins = e.transpose(out=pb.ap[:, kc * 128:(kc + 1) * 128], in_=pxs.ap[:, kc * 128:(kc + 1) * 128],
                                              identity=identb.ap)
                        return ins
                    op("pe", tr, reads=[pxs, identb], writes=[pb])
                    for kc in range(8):
                        dst = hx.ap[:, kc, psub * 128:(psub + 1) * 128]
                        srcp = pb.ap[:, kc * 128:(kc + 1) * 128]
                        if kc % 2 == 0:
                            op("act", lambda e, dst=dst, srcp=srcp, kc=kc: e.activation(
                                out=dst, in_=srcp, func=AF.Identity, scale=A_.ap[:, kc:kc + 1], bias=B_.ap[:, kc:kc + 1]),
                               reads=[pb, A_, B_], writes=[hx])
                        else:
                            op("dve", lambda e, dst=dst, srcp=srcp, kc=kc: e.tensor_scalar(
                                out=dst, in0=srcp, scalar1=A_.ap[:, kc:kc + 1], scalar2=B_.ap[:, kc:kc + 1], op0=ALU.mult, op1=ALU.add),
                               reads=[pb, A_, B_], writes=[hx])
                    mm(gp, gp.ap[:, psub * 16:(psub + 1) * 16],
                       [(None, [(hx.ap[:, kc, psub * 128:(psub + 1) * 128], w_in_bf.ap[:, kc, 2048:2064]) for kc in range(8)])],
                       [hx, w_in_bf])
                pend = (sub, xsb) if sub < nsub else None
                yield
            gv = gp.ap[:, 0:nsub * 16].rearrange("p (s c) -> p s c", c=16)
            t1 = gt1_.ap[:, 0:nsub, :]
            t2 = gt2_.ap[:, 0:nsub, :]
            op("act", lambda e: e.activation(out=t1, in_=gv[:, :, 0:8], func=AF.Exp, scale=-1.0), reads=[gp], writes=[gt1_])
            for sub in range(nsub):
                op("dve", lambda e, sub=sub: e.tensor_tensor(out=gt2_.ap[:, sub, :], in0=gv[:, sub, 8:16], in1=dtb.ap,
                                                            op=ALU.add), reads=[gp, dtb], writes=[gt2_])
            op("dve", lambda e: e.tensor_scalar(out=t1, in0=t1, scalar1=1.0, scalar2=None, op0=ALU.add), reads=[gt1_], writes=[gt1_])
            op("dve", lambda e: e.reciprocal(out=Bd.ap[:, t * nsub:(t + 1) * nsub, :], in_=t1), reads=[gt1_], writes=[Bd])
            op("act", lambda e: e.activation(out=t2, in_=t2, func=AF.Exp), reads=[gt2_], writes=[gt2_])
            op("act", lambda e: e.activation(out=t2, in_=t2, func=AF.Ln, bias=one1.ap), reads=[gt2_, one1], writes=[gt2_])
            for sub in range(nsub):
                op("dve", lambda e, sub=sub: e.tensor_tensor(out=Gd.ap[:, t * nsub + sub, :], in0=gt2_.ap[:, sub, :],
                                                            in1=negA.ap, op=ALU.mult), reads=[gt2_, negA], writes=[Gd])
            yield

        def stageBC(t):
            hx = hxT[t % 2]
            m0 = 2 if t == 0 else 0
            n = W - m0
            tok0 = t * W - 2 + m0
            order = [j for j in chunks if j < 12] + [j for j in chunks if j >= 12]
            prev = None
            for j in order + [None]:
                if j is not None:
                    c0 = j * 128 if j < 16 else 2064 + (j - 17) * 128
                    ps = nps()
                    mm(ps, ps.ap[:, 0:W], [(None, [(w_in_bf.ap[:, kc, c0:c0 + 128], hx.ap[:, kc, 0:W]) for kc in range(8)])],
                       [w_in_bf, hx])
                    if j < 12:
                        op("act", lambda e, ps=ps, j=j: e.copy(out=stg[j].ap[:, 4:4 + W], in_=ps.ap[:, 0:W]),
                           reads=[ps], writes=[stg[j]])
                    else:
                        z_ = rot(zo, "zo")
                        op("act", lambda e, ps=ps, z_=z_: e.copy(out=z_.ap[:, 0:W], in_=ps.ap[:, 0:W]), reads=[ps], writes=[z_])
                        dst = zT_s[j - 12] if j < 16 else uT_s[j - 17]
                        P.dma("act", dst[:, t * W:(t + 1) * W], z_.ap[:, 0:W], reads=[z_])
                if prev is not None:
                    conv_silu(prev, m0, n, tok0, t % 2)
                prev = j if (j is not None and j < 12) else None
                yield
            for j in chunks:
                if j < 12:
                    op("pool", lambda e, j=j: e.tensor_copy(out=stg[j].ap[:, 0:4], in_=stg[j].ap[:, W:W + 4]),
                       reads=[stg[j]], writes=[stg[j]])
            yield

        def stageL(t):
            m0 = 2 if t == 0 else 0
            n = W - m0
            tok0 = t * W - 2 + m0
            prevn = None
            for j in [j for j in chunks if j < 8] + [None]:
                cur = l2norm_front(j, n, t % 2) if j is not None else None
                if prevn is not None:
                    l2norm_back(prevn, n, tok0)
                prevn = cur
                yield
            if t == ntile - 1:
                for j in chunks:
                    if j < 12:
                        op("pool", lambda e, j=j: e.memset(stg[j].ap[:, 4:8], 0.0), writes=[stg[j]])
                        conv_silu(j, 0, 2, Ttok - 2, t % 2)
                yield
                for j in chunks:
                    if j < 8:
                        l2norm_back(l2norm_front(j, 2, t % 2), 2, Ttok - 2)
                yield

        interleave([stageA(0)])
        for t in range(ntile):
            gens = [stageBC(t)]
            if t + 1 < ntile:
                gens.append(stageA(t + 1))
            if t >= 1:
                gens.append(stageL(t - 1))
            interleave(gens)
        interleave([stageL(ntile - 1)])

    project(ctx, TC, 256, A1c, B1c, list(range(4, 12)), None, [kT_c[h] for h in range(4)], [vT_c[h] for h in range(4)], Gc, Bc)
    if stop_after <= 1:
        return finish(P, nc, out, ar)
    project(x, T, 512, A1, B1, list(range(0, 16)) + list(range(17, 21)), [qT_s[h] for h in range(4)],
            [kT_s[h] for h in range(4)], [vT_s[h] for h in range(4)], Gm, Bt)
    if stop_after <= 2:
        return finish(P, nc, out, ar)

    P.barrier()
    ar.reset()
    U = ar.alloc([64, 64], F32, "poolU")
    PA = ar.alloc([64, 80], F32, "poolA")
    PB = ar.alloc([64, 80], F32, "poolB")
    PC = ar.alloc([80, 64], F32, "poolC")
    PD = ar.alloc([80, 64], F32, "poolD")
    PM = ar.alloc([64, 64], F32, "poolM")
    dTb = ar.alloc([T], BF16, "pooldT")
    pwst = ar.alloc([4, 128], F32, "pwst")
    pw_bf = ar.alloc([4, 128], BF16, "pw_bf")
    po = [ar.alloc([512], BF16, "po%d" % i) for i in range(2)]
    ca = ar.alloc([80], F32, "cnta")
    cb = ar.alloc([80], F32, "cntb")
    rcs = [ar.alloc([64], F32, "rc%d" % g) for g in range(4)]
    cst = [ar.alloc([8, 512], F32, "cst%d" % i) for i in range(2)]
    cbf = [ar.alloc([8, 512], BF16, "cbf%d" % i) for i in range(1)]

    def cast_bg():
        pieces = []
        wo_v = w_out.rearrange("(k p) n -> p k n", p=128)
        wu_v = w_up.rearrange("(k p) n -> p k n", p=128)
        wd_v = w_down.rearrange("(j p) n -> p j n", p=128)
        wds_v = wd_s.rearrange("j p n -> p j n")
        for c0 in range(0, D, 512):
            pieces.append((wo_v[:, :, c0:c0 + 512], wout_s[:, :, c0:c0 + 512], 8, 512))
        for c0 in range(0, 2 * DFF, 512):
            pieces.append((wu_v[:, :, c0:c0 + 512], wup_s[:, :, c0:c0 + 512], 8, 512))
        for j0 in range(0, NFC, 4):
            nj = min(4, NFC - j0)
            pieces.append((wd_v[:, j0:j0 + nj, :], wds_v[:, j0:j0 + nj, :], nj, D))
        for i, (src_, dst_, a_, b_) in enumerate(pieces):
            st_, bf_ = cst[i % 2], cbf[0]
            if b_ == 512:
                sv, bv = st_.ap[:, 0:a_, :], bf_.ap[:, 0:a_, :]
            else:
                sv = st_.ap.rearrange("p a b -> p (a b)")[:, 0:a_ * b_].rearrange("p (a b) -> p a b", a=a_)
                bv = bf_.ap.rearrange("p a b -> p (a b)")[:, 0:a_ * b_].rearrange("p (a b) -> p a b", a=a_)
            P.dma("sp", sv, src_, writes=[st_])
            op("act", lambda e, sv=sv, bv=bv: e.copy(out=bv, in_=sv), reads=[st_], writes=[bf_])
            P.dma("act", dst_, bv, reads=[bf_])
            yield

    bg_ = cast_bg()

    def bgstep(k=1):
        for _ in range(k):
            try:
                next(bg_)
            except StopIteration:
                return

    P.dma("sp", pwst.ap, pool_w.rearrange("g c d -> c g d"), writes=[pwst])
    op("dve", lambda e: e.tensor_copy(out=pw_bf.ap, in_=pwst.ap), reads=[pwst], writes=[pw_bf])
    for g in range(4):
        L = g + 1
        wv = 2 ** L
        left = wv // 2
        lo = 8 - left
        op("pool", lambda e: e.memset(ca.ap, 0.0), writes=[ca])
        op("pool", lambda e: e.memset(ca.ap[:, 8:72], 1.0), writes=[ca])
        src_, dst_ = ca, cb
        for l in range(L):
            sft = 2 ** l
            op("pool", lambda e, src_=src_, dst_=dst_, sft=sft: e.tensor_tensor(
                out=dst_.ap[:, 0:80 - sft], in0=src_.ap[:, 0:80 - sft], in1=src_.ap[:, sft:80], op=ALU.add),
               reads=[src_], writes=[dst_])
            src_, dst_ = dst_, src_
        rc = rcs[g]
        op("dve", lambda e, src_=src_, rc=rc, lo=lo: e.reciprocal(out=rc.ap, in_=src_.ap[:, lo:lo + 64]), reads=[src_], writes=[rc])
        P.dma("sp", U.ap, uT_s[g].rearrange("p (r c) -> p r c", c=64), writes=[U])
        op("pool", lambda e: e.memset(PA.ap, 0.0), writes=[PA])
        op("pool", lambda e: e.tensor_copy(out=PA.ap[:, :, 8:72], in_=U.ap), reads=[U], writes=[PA])
        src_, dst_ = PA, PB
        for l in range(L):
            sft = 2 ** l
            op("dve", lambda e, src_=src_, dst_=dst_, sft=sft: e.tensor_tensor(
                out=dst_.ap[:, :, 0:80 - sft], in0=src_.ap[:, :, 0:80 - sft], in1=src_.ap[:, :, sft:80], op=ALU.add),
               reads=[src_], writes=[dst_])
            src_, dst_ = dst_, src_
            bgstep(1)
        op("pool", lambda e: e.memset(PC.ap, 0.0), writes=[PC])
        op("pool", lambda e, src_=src_, rc=rc, lo=lo: e.tensor_tensor(
            out=PC.ap[:, 8:72, :], in0=src_.ap[:, :, lo:lo + 64], in1=rc.ap.unsqueeze(1).to_broadcast([128, 64, 64]),
            op=ALU.mult), reads=[src_, rc], writes=[PC])
        src_, dst_ = PC, PD
        for l in range(L):
            sft = 2 ** l
            op("dve", lambda e, src_=src_, dst_=dst_, sft=sft: e.tensor_tensor(
                out=dst_.ap[:, 0:80 - sft, :], in0=src_.ap[:, 0:80 - sft, :], in1=src_.ap[:, sft:80, :], op=ALU.add),
               reads=[src_], writes=[dst_])
            src_, dst_ = dst_, src_
        op("pool", lambda e, src_=src_, rc=rc, lo=lo: e.tensor_tensor(
            out=PM.ap, in0=src_.ap[:, lo:lo + 64, :], in1=rc.ap.unsqueeze(2).to_broadcast([128, 64, 64]),
            op=ALU.mult), reads=[src_, rc], writes=[PM])
        op("dve", lambda e: e.tensor_tensor(out=dTb.ap.rearrange("p (r c) -> p r c", c=64), in0=PM.ap, in1=U.ap,
                                            op=ALU.subtract), reads=[PM, U], writes=[dTb])
        bgstep(3)
        for tt in range(8):
            ps = nps()
            mm(ps, ps.ap, [(None, [(pw_bf.ap[:, g, :], dTb.ap[:, tt * 512:(tt + 1) * 512])])], [pw_bf, dTb])
            o = po[tt % 2]
            op("act", lambda e, ps=ps, o=o, g=g: e.activation(out=o.ap, in_=ps.ap, func=AF.Copy, scale=pscale.ap[:, g:g + 1]),
               reads=[ps, pscale], writes=[o])
            P.dma("act", poolT_s[g][:, tt * 512:(tt + 1) * 512], o.ap, reads=[o])
    for _ in bg_:
        pass
    if stop_after <= 3:
        return finish(P, nc, out, ar)

    P.barrier()
    ar.reset()
    op("pool", lambda e: e.memset(S.ap, 0.0), writes=[S])
    op("pool", lambda e: e.memset(Sb.ap, 0.0), writes=[Sb])
    Sd = [Buf("S0"), Buf("S1")]
    Sbd = [Buf("Sb0"), Buf("Sb1")]
    P.barrier()

    def wset(tag):
        W_ = {}
        for nm in ("Kf", "Vf", "Qf", "Vt", "KD", "NTb", "AT", "Mb", "Yb", "Pa", "PTa", "Pb", "PTb", "N0", "M0", "W2b", "YTb"):
            W_[nm] = ar.alloc([4, 128], BF16, nm + tag)
        for nm in ("GM", "DT", "DTs"):
            W_[nm] = ar.alloc([4, 128], F32, nm + tag)
        W_["cg"] = ar.alloc([16], F32, "cg" + tag)
        W_["E"] = ar.alloc([12], F32, "E" + tag)
        W_["nec"] = ar.alloc([4], F32, "nec" + tag)
        return W_

    WS = [[wset("_%d%d" % (d, p_)) for p_ in range(3)] for d in range(2)]
    RS = []
    for d in range(2):
        R_ = {}
        for nm in ("R3", "VN"):
            R_[nm] = ar.alloc([4, 128], BF16, "%s_%d" % (nm, d))
        for nm in ("OI", "O"):
            R_[nm] = ar.alloc([4, 128], F32, "%s_%d" % (nm, d))
        RS.append(R_)

    def v4(ps):
        return ps.ap.rearrange("p (h t) -> p h t", h=4)

    def mm4(ps, pairs_h, reads):
        groups = [(ps.ap[:, h * 128:(h + 1) * 128], pairs_h(h)) for h in range(4)]
        return mm(ps, None, groups, reads)

    def tr4(src_):
        pb = npb()

        def tr(e, src_=src_, pb=pb):
            ins = None
            for h in range(4):
                ins = e.transpose(out=pb.ap[:, h * 128:(h + 1) * 128], in_=src_.ap[:, h, :], identity=identb.ap)
            return ins
        op("pe", tr, reads=[src_, identb], writes=[pb])
        return pb

    def pb4(pb):
        return pb.ap[:, 0:512].rearrange("p (h t) -> p h t", h=4)

    def bfree(ap4):
        return ap4.unsqueeze(2).to_broadcast([128, 4, 128])

    def bhead(ap128):
        return ap128.unsqueeze(1).to_broadcast([128, 4, 128])

    def scan_pre(d, n, par, kd_, vd_, qd_, Gd, Bd):
        W_ = WS[d][par]
        with_out = qd_ is not None
        Kf, Vf, Qf, Vt, KD = W_["Kf"], W_["Vf"], W_["Qf"], W_["Vt"], W_["KD"]
        sl = slice(n * 128, (n + 1) * 128)
        P.dma("sp", Kf.ap, kd_.rearrange("h p t -> p h t")[:, :, sl], writes=[Kf])
        P.dma("sp", Vf.ap, vd_.rearrange("h p t -> p h t")[:, :, sl], writes=[Vf])
        if with_out:
            P.dma("sp", Qf.ap, qd_.rearrange("h p t -> p h t")[:, :, sl], writes=[Qf])
        yield
        g_ap = Gd.ap[:, n, d * 4:(d + 1) * 4]
        b_ap = Bd.ap[:, n, d * 4:(d + 1) * 4]
        mask = Lmask if d == 0 else Umask
        mb = mbF if d == 0 else mbB
        st4 = stF4 if d == 0 else stB4
        cg, E, nec = W_["cg"], W_["E"], W_["nec"]
        pc = nps()
        mm(pc, None, [(pc.ap[:, 0:4], [(mask.ap, g_ap)]), (pc.ap[:, 4:8], [(onesf.ap, g_ap)])], [mask, onesf, Gd])
        op("dve", lambda e: e.tensor_copy(out=cg.ap[:, 0:4], in_=pc.ap[:, 0:4]), reads=[pc], writes=[cg])
        op("dve", lambda e: e.tensor_tensor(out=cg.ap[:, 4:8], in0=pc.ap[:, 4:8], in1=cg.ap[:, 0:4], op=ALU.subtract),
           reads=[pc, cg], writes=[cg])
        op("dve", lambda e: e.tensor_copy(out=cg.ap[:, 8:12], in_=pc.ap[:, 4:8]), reads=[pc], writes=[cg])
        op("dve", lambda e: e.tensor_scalar(out=cg.ap[:, 12:16], in0=cg.ap[:, 0:4], scalar1=-1.0, scalar2=None, op0=ALU.mult),
           reads=[cg], writes=[cg])
        yield
        op("act", lambda e: e.activation(out=E.ap, in_=cg.ap[:, 0:12], func=AF.Exp), reads=[cg], writes=[E])
        op("dve", lambda e: e.tensor_scalar(out=nec.ap, in0=E.ap[:, 0:4], scalar1=-1.0, scalar2=None, op0=ALU.mult),
           reads=[E], writes=[nec])
        yield
        pbv = tr4(Vf)
        op("act", lambda e: e.copy(out=Vt.ap, in_=pb4(pbv)), reads=[pbv], writes=[Vt])
        yield
        pbk = tr4(Kf)
        op("dve", lambda e: e.tensor_tensor(out=KD.ap, in0=pb4(pbk), in1=bfree(E.ap[:, 4:8]), op=ALU.mult),
           reads=[pbk, E], writes=[KD])
        yield
        GM, DT, DTs = W_["GM"], W_["DT"], W_["DTs"]
        op("dve", lambda e: e.tensor_tensor(out=GM.ap, in0=bhead(mask.ap), in1=bfree(g_ap), op=ALU.mult), reads=[mask, Gd], writes=[GM])
        yield
        pd = nps()
        mm4(pd, lambda h: [(onesf.ap, GM.ap[:, h, :]), (identf.ap, mb.ap)], [onesf, GM, identf, mb])
        for h in range(4):
            op("act", lambda e, h=h: e.activation(out=DT.ap[:, h, :], in_=pd.ap[:, h * 128:(h + 1) * 128], func=AF.Exp,
                                                  bias=cg.ap[:, 12 + h:13 + h]), reads=[pd, cg], writes=[DT])
        yield
        op("pool", lambda e: e.tensor_tensor(out=DTs.ap, in0=DT.ap, in1=st4.ap, op=ALU.mult), reads=[DT, st4], writes=[DTs])
        yield
        NTb, AT, Mb, Yb = W_["NTb"], W_["AT"], W_["Mb"], W_["Yb"]
        pk = nps()
        mm4(pk, lambda h: [(Kf.ap[:, h, :], Kf.ap[:, h, :])], [Kf])
        for h in range(4):
            op("dve", lambda e, h=h: e.scalar_tensor_tensor(out=NTb.ap[:, h, :], in0=pk.ap[:, h * 128:(h + 1) * 128],
                                                            scalar=b_ap[:, h:h + 1], in1=DTs.ap[:, h, :], op0=ALU.mult, op1=ALU.mult),
               reads=[pk, Bd, DTs], writes=[NTb])
        yield
        if with_out:
            pq = nps()
            mm4(pq, lambda h: [(Kf.ap[:, h, :], Qf.ap[:, h, :])], [Kf, Qf])
            op("dve", lambda e: e.tensor_tensor(out=AT.ap, in0=v4(pq), in1=DT.ap, op=ALU.mult), reads=[pq, DT], writes=[AT])
            yield
        pbn = tr4(NTb)
        op("act", lambda e: e.copy(out=Mb.ap, in_=pb4(pbn)), reads=[pbn], writes=[Mb])
        yield
        N0, M0, W2b, YTb = W_["N0"], W_["M0"], W_["W2b"], W_["YTb"]
        op("pool", lambda e: e.tensor_tensor(out=N0.ap, in0=NTb.ap, in1=mk4["bm16"].ap, op=ALU.mult), reads=[NTb, mk4["bm16"]], writes=[N0])
        op("pool", lambda e: e.tensor_tensor(out=M0.ap, in0=Mb.ap, in1=mk4["bm16"].ap, op=ALU.mult), reads=[Mb, mk4["bm16"]], writes=[M0])
        op("pool", lambda e: e.tensor_tensor(out=Yb.ap, in0=I4.ap, in1=N0.ap, op=ALU.subtract), reads=[I4, N0], writes=[Yb])
        yield
        Pc, PTc = N0, M0
        nxt = [(W_["Pa"], W_["PTa"]), (W_["Pb"], W_["PTb"])]
        for lvl in range(3):
            Pn, PTn = nxt[lvl % 2]
            if lvl < 2:
                p1 = nps()
                mm4(p1, lambda h, Pc=Pc, PTc=PTc: [(PTc.ap[:, h, :], Pc.ap[:, h, :])], [Pc, PTc])
                op("act", lambda e, p1=p1, Pn=Pn: e.copy(out=Pn.ap, in_=v4(p1)), reads=[p1], writes=[Pn])
            p2 = nps()
            mm4(p2, lambda h, Pc=Pc, PTc=PTc: [(Pc.ap[:, h, :], PTc.ap[:, h, :])], [Pc, PTc])
            op("act", lambda e, p2=p2, PTn=PTn: e.copy(out=PTn.ap, in_=v4(p2)), reads=[p2], writes=[PTn])
            yield
            p3 = nps()
            mm4(p3, lambda h, PTn=PTn: [(PTn.ap[:, h, :], Yb.ap[:, h, :])], [PTn, Yb])
            op("dve", lambda e, p3=p3: e.tensor_tensor(out=Yb.ap, in0=v4(p3), in1=Yb.ap, op=ALU.add), reads=[p3, Yb], writes=[Yb])
            yield
            Pc, PTc = Pn, PTn
        for offk in ("off16", "off32", "off64"):
            pw = nps()
            mm4(pw, lambda h: [(Mb.ap[:, h, :], Yb.ap[:, h, :])], [Mb, Yb])
            pby = tr4(Yb)
            op("dve", lambda e, pw=pw, offk=offk: e.tensor_tensor(out=W2b.ap, in0=v4(pw), in1=mk4[offk].ap, op=ALU.mult),
               reads=[pw, mk4[offk]], writes=[W2b])
            op("act", lambda e, pby=pby: e.copy(out=YTb.ap, in_=pb4(pby)), reads=[pby], writes=[YTb])
            yield
            py = nps()
            mm4(py, lambda h: [(YTb.ap[:, h, :], W2b.ap[:, h, :])], [YTb, W2b])
            op("dve", lambda e, py=py: e.tensor_tensor(out=Yb.ap, in0=Yb.ap, in1=v4(py), op=ALU.subtract), reads=[py, Yb], writes=[Yb])
            yield

    def scan_rec(d, n, par, with_out, Bd, odst):
        W_ = WS[d][par]
        R_ = RS[d]
        Kf, Qf, Vt, KD, Yb, AT, E, nec = W_["Kf"], W_["Qf"], W_["Vt"], W_["KD"], W_["Yb"], W_["AT"], W_["E"], W_["nec"]
        R3, VN, OI, O = R_["R3"], R_["VN"], R_["OI"], R_["O"]
        b_ap = Bd.ap[:, n, d * 4:(d + 1) * 4]
        Sv = S.ap[:, d * 4:(d + 1) * 4, :]
        Sbv = Sb.ap[:, d * 4:(d + 1) * 4, :]
        pks = nps()
        mm4(pks, lambda h: [(Kf.ap[:, h, :], Sbv[:, h, :])], [Kf, Sbd[d]])
        if with_out:
            pqs = nps()
            mm4(pqs, lambda h: [(Qf.ap[:, h, :], Sbv[:, h, :])], [Qf, Sbd[d]])
            for h in range(4):
                op("act", lambda e, h=h: e.activation(out=OI.ap[:, h, :], in_=pqs.ap[:, h * 128:(h + 1) * 128], func=AF.Copy,
                                                      scale=E.ap[:, h:h + 1]), reads=[pqs, E], writes=[OI])
        for h in range(4):
            op("dve", lambda e, h=h: e.scalar_tensor_tensor(out=R3.ap[:, h, :], in0=pks.ap[:, h * 128:(h + 1) * 128],
                                                            scalar=nec.ap[:, h:h + 1], in1=Vt.ap[:, h, :], op0=ALU.mult, op1=ALU.add),
               reads=[pks, nec, Vt], writes=[R3])
        yield
        pv = nps()
        mm4(pv, lambda h: [(Yb.ap[:, h, :], R3.ap[:, h, :])], [Yb, R3])
        for h in range(4):
            op("act", lambda e, h=h: e.activation(out=VN.ap[:, h, :], in_=pv.ap[:, h * 128:(h + 1) * 128], func=AF.Copy,
                                                  scale=b_ap[:, h:h + 1]), reads=[pv, Bd], writes=[VN])
        yield
        pds = nps()
        mm4(pds, lambda h: [(KD.ap[:, h, :], VN.ap[:, h, :])], [KD, VN])
        for h in range(4):
            op("dve", lambda e, h=h: e.scalar_tensor_tensor(out=Sv[:, h, :], in0=Sv[:, h, :], scalar=E.ap[:, 8 + h:9 + h],
                                                            in1=pds.ap[:, h * 128:(h + 1) * 128], op0=ALU.mult, op1=ALU.add),
               reads=[Sd[d], E, pds], writes=[Sd[d]])
        op("act", lambda e: e.copy(out=Sbv, in_=Sv), reads=[Sd[d]], writes=[Sbd[d]])
        yield
        if with_out:
            pav = nps()
            mm4(pav, lambda h: [(AT.ap[:, h, :], VN.ap[:, h, :])], [AT, VN])
            op("dve", lambda e: e.tensor_tensor(out=O.ap, in0=v4(pav), in1=OI.ap, op=ALU.add), reads=[pav, OI], writes=[O])
            P.dma("sp", odst[n], O.ap.rearrange("p h t -> p (h t)"), reads=[O])
            yield

    NPAR = 3

    def scan_pass(nch, kd_, vd_, qd_, Gd, Bd):
        with_out = qd_ is not None
        chunk = [lambda r: r, lambda r: nch - 1 - r]
        odst = [of_s, ob_s]
        pre_done = [set(), set()]
        rec_done = [set(), set()]
        pre_started = [0, 0]
        rec_started = [0, 0]
        active = []
        while True:
            for d in range(2):
                r = pre_started[d]
                if r < nch and (r < NPAR or (r - NPAR) in rec_done[d]):
                    active.append((scan_pre(d, chunk[d](r), r % NPAR, kd_, vd_, qd_, Gd, Bd), "pre", d, r))
                    pre_started[d] += 1
                r = rec_started[d]
                if r < nch and r in pre_done[d] and (r == 0 or (r - 1) in rec_done[d]):
                    active.append((scan_rec(d, chunk[d](r), r % NPAR, with_out, Bd, odst[d]), "rec", d, r))
                    rec_started[d] += 1
            if not active:
                break
            for item in list(active):
                g_, kind, d, r = item
                try:
                    next(g_)
                except StopIteration:
                    active.remove(item)
                    (pre_done if kind == "pre" else rec_done)[d].add(r)

    scan_pass(TC // CH, kT_c, vT_c, None, Gc, Bc)
    if stop_after <= 4:
        return finish(P, nc, out, ar)
    scan_pass(NCH_DEBUG or T // CH, kT_s, vT_s, qT_s, Gm, Bt)
    if stop_after <= 5:
        return finish(P, nc, out, ar)

    P.barrier()
    ar.reset()
    w_out_bf = ar.alloc([8, D], BF16, "w_out_bf")
    P.dma("sp", w_out_bf.ap, wout_s, writes=[w_out_bf])
    xt = [ar.alloc([D], F32, "xtB%d" % i) for i in range(2)]
    xs = [ar.alloc([D], BF16, "xsB%d" % i) for i in range(2)]
    junk = ar.alloc([D], BF16, "junkB")
    stt = [ar.alloc([4], F32, "sttB%d" % i) for i in range(4)]
    hxT = [ar.alloc([8, 512], BF16, "hx2T%d" % i) for i in range(2)]
    zt = [ar.alloc([4, 512], F32, "zt%d" % i) for i in range(2)]
    mixT = [ar.alloc([8, 512], BF16, "mixT%d" % i) for i in range(2)]
    oft = [ar.alloc([4, 128], F32, "oft%d" % i) for i in range(2)]
    obt = [ar.alloc([4, 128], F32, "obt%d" % i) for i in range(2)]
    onb = [ar.alloc([4, 128], BF16, "onb%d" % i) for i in range(2)]
    ost = [ar.alloc([12], F32, "ost%d" % i) for i in range(2)]
    x1t = [ar.alloc([D], F32, "x1t%d" % i) for i in range(2)]
    tmpt = [ar.alloc([D], F32, "tmpt%d" % i) for i in range(2)]
    yst = [ar.alloc([4], F32, "yst%d" % i) for i in range(2)]
    junkf = ar.alloc([512], F32, "junkf")
    zv = zT_s.rearrange("h p t -> p h t")
    pv_ = poolT_s.rearrange("h p t -> p h t")
    h2v = hx2T_s.rearrange("k p t -> p k t")
    c3 = {"c": 0, "s": 0}

    def chainC(t):
        tsl = slice(t * 512, (t + 1) * 512)
        z_ = zt[t % 2]
        mx = mixT[t % 2]
        P.dma("sp", z_.ap, zv[:, :, tsl], writes=[z_])
        P.dma("sp", mx.ap[:, 4:8, :], pv_[:, :, tsl], writes=[mx])
        op("act", lambda e: e.activation(out=z_.ap, in_=z_.ap, func=AF.Silu), reads=[z_], writes=[z_])
        yield
        for c_ in range(4):
            n = t * 4 + c_
            c3["c"] += 1
            ci = c3["c"]
            of_, ob_, on_, os_ = oft[ci % 2], obt[ci % 2], onb[ci % 2], ost[ci % 2]
            P.dma("sp", of_.ap.rearrange("p h t -> p (h t)"), of_s[n], writes=[of_])
            P.dma("sp", ob_.ap.rearrange("p h t -> p (h t)"), ob_s[n], writes=[ob_])
            op("dve", lambda e, of_=of_, ob_=ob_: e.tensor_tensor(out=of_.ap, in0=of_.ap, in1=ob_.ap, op=ALU.add),
               reads=[of_, ob_], writes=[of_])
            op("act", lambda e, os_=os_: e.memzero(os_.ap), writes=[os_])
            for h in range(4):
                op("act", lambda e, of_=of_, os_=os_, h=h, jk=junkf: e.activation(out=jk.ap[:, 0:128], in_=of_.ap[:, h, :], func=AF.Square,
                                                                                 accum_out=os_.ap[:, h:h + 1]),
                   reads=[of_, os_], writes=[junkf, os_])
            op("act", lambda e, os_=os_: e.activation(out=os_.ap[:, 4:8], in_=os_.ap[:, 0:4], func=AF.Sqrt, scale=1.0 / 128, bias=eps1.ap),
               reads=[os_, eps1], writes=[os_])
            op("dve", lambda e, os_=os_: e.reciprocal(out=os_.ap[:, 8:12], in_=os_.ap[:, 4:8]), reads=[os_], writes=[os_])
            yield
            for h in range(4):
                op("dve", lambda e, of_=of_, os_=os_, on_=on_, h=h: e.tensor_scalar(
                    out=on_.ap[:, h, :], in0=of_.ap[:, h, :], scalar1=os_.ap[:, 8 + h:9 + h], scalar2=None, op0=ALU.mult),
                   reads=[of_, os_], writes=[on_])
            pb = npb()

            def tro(e, pb=pb, on_=on_):
                ins = None
                for h in range(4):
                    ins = e.transpose(out=pb.ap[:, h * 128:(h + 1) * 128], in_=on_.ap[:, h, :], identity=identb.ap)
                return ins
            op("pe", tro, reads=[on_, identb], writes=[pb])
            op("dve", lambda e, pb=pb, c_=c_: e.scalar_tensor_tensor(
                out=mx.ap[:, 0:4, c_ * 128:(c_ + 1) * 128], in0=pb.ap[:, 0:512].rearrange("p (h t) -> p h t", h=4),
                scalar=onorm.ap[:, 0:1], in1=z_.ap[:, :, c_ * 128:(c_ + 1) * 128], op0=ALU.mult, op1=ALU.mult),
               reads=[pb, onorm, z_], writes=[mx])
            yield

    def chainS(t):
        tsl = slice(t * 512, (t + 1) * 512)
        mx = mixT[t % 2]
        hx = hxT[t % 2]
        st8 = {}

        def s1(sub):
            r0 = t * 512 + sub * 128
            c3["s"] += 1
            ci = c3["s"]
            xx, x1_, tm_, ys_ = xt[ci % 2], x1t[ci % 2], tmpt[ci % 2], yst[ci % 2]
            st8[sub] = (xx, x1_, tm_, r0)
            P.dma("sp", xx.ap, x[r0:r0 + 128, :], writes=[xx])
            pys = [nps(), nps()]
            for half in range(2):
                mm(pys[half], pys[half].ap, [(None, [(mx.ap[:, k, sub * 128:(sub + 1) * 128],
                                                       w_out_bf.ap[:, k, half * 512:(half + 1) * 512]) for k in range(8)])],
                   [mx, w_out_bf])
            op("act", lambda e: e.memzero(ys_.ap), writes=[ys_])
            for half in range(2):
                op("act", lambda e, half=half, p_=pys[half], jk=junkf: e.activation(
                    out=jk.ap, in_=p_.ap, func=AF.Square, accum_out=ys_.ap[:, half:half + 1]),
                   reads=[pys[half], ys_], writes=[junkf, ys_])
            op("dve", lambda e: e.tensor_tensor(out=ys_.ap[:, 2:3], in0=ys_.ap[:, 0:1], in1=ys_.ap[:, 1:2], op=ALU.add),
               reads=[ys_], writes=[ys_])
            op("act", lambda e: e.activation(out=ys_.ap[:, 3:4], in_=ys_.ap[:, 2:3], func=AF.Sqrt, scale=1.0 / D, bias=eps1.ap),
               reads=[ys_, eps1], writes=[ys_])
            op("dve", lambda e: e.reciprocal(out=ys_.ap[:, 3:4], in_=ys_.ap[:, 3:4]), reads=[ys_], writes=[ys_])
            for half in range(2):
                hs = slice(half * 512, (half + 1) * 512)
                op("dve", lambda e, hs=hs, p_=pys[half]: e.scalar_tensor_tensor(
                    out=tm_.ap[:, hs], in0=p_.ap, scalar=ys_.ap[:, 3:4], in1=gt1g.ap[:, hs], op0=ALU.mult, op1=ALU.mult),
                   reads=[pys[half], ys_, gt1g], writes=[tm_])

        def s2(sub):
            xx, x1_, tm_, r0 = st8[sub]
            op("pool", lambda e: e.tensor_tensor(out=x1_.ap, in0=tm_.ap, in1=xx.ap, op=ALU.add), reads=[tm_, xx], writes=[x1_])
            P.dma("pool", x1_s[r0:r0 + 128, :], x1_.ap, reads=[x1_])
            cnt["x"] += 1
            xsb = xs[cnt["x"] % 2]
            st = rot(stt, "st")
            op("act", lambda e: e.memzero(st.ap), writes=[st])
            op("act", lambda e, jk=junk: e.activation(out=jk.ap, in_=x1_.ap, func=AF.Square, accum_out=st.ap[:, 0:1]),
               reads=[x1_, st], writes=[junk, st])
            op("act", lambda e: e.activation(out=st.ap[:, 1:2], in_=st.ap[:, 0:1], func=AF.Sqrt, scale=1.0 / D, bias=eps1.ap),
               reads=[st, eps1], writes=[st])
            op("dve", lambda e: e.reciprocal(out=st.ap[:, 2:3], in_=st.ap[:, 1:2]), reads=[st], writes=[st])
            op("dve", lambda e: e.tensor_scalar(out=xsb.ap, in0=x1_.ap, scalar1=st.ap[:, 2:3], scalar2=None, op0=ALU.mult),
               reads=[x1_, st], writes=[xsb])
            st8[sub] = (xsb,)

        def s3(sub):
            (xsb,) = st8[sub]
            pb = npb()

            def tr(e):
                ins = None
                for kc in range(8):
                    ins = e.transpose(out=pb.ap[:, kc * 128:(kc + 1) * 128], in_=xsb.ap[:, kc * 128:(kc + 1) * 128],
                                      identity=identb.ap)
                return ins
            op("pe", tr, reads=[xsb, identb], writes=[pb])
            for kc in range(8):
                dst = hx.ap[:, kc, sub * 128:(sub + 1) * 128]
                srcp = pb.ap[:, kc * 128:(kc + 1) * 128]
                if kc % 2 == 0:
                    op("act", lambda e, dst=dst, srcp=srcp, kc=kc: e.activation(
                        out=dst, in_=srcp, func=AF.Identity, scale=A2.ap[:, kc:kc + 1], bias=B2.ap[:, kc:kc + 1]),
                       reads=[pb, A2, B2], writes=[hx])
                else:
                    op("dve", lambda e, dst=dst, srcp=srcp, kc=kc: e.tensor_scalar(
                        out=dst, in0=srcp, scalar1=A2.ap[:, kc:kc + 1], scalar2=B2.ap[:, kc:kc + 1], op0=ALU.mult, op1=ALU.add),
                       reads=[pb, A2, B2], writes=[hx])

        for slot in range(6):
            if slot < 4:
                s1(slot)
            if 1 <= slot < 5:
                s2(slot - 1)
            if slot >= 2:
                s3(slot - 2)
            yield
        P.dma("sp", h2v[:, :, tsl], hx.ap, reads=[hx])
        op("pool", lambda e: e.tensor_copy(out=halp.ap[:, t, :, 0:1], in_=hx.ap[:, :, 0:1]), reads=[hx], writes=[halp])
        op("pool", lambda e: e.tensor_copy(out=halp.ap[:, t, :, 1:2], in_=hx.ap[:, :, 511:512]), reads=[hx], writes=[halp])
        yield

    interleave([chainC(0)])
    for t in range(T // 512):
        gens = [chainS(t)]
        if t + 1 < T // 512:
            gens.append(chainC(t + 1))
        interleave(gens)
    if stop_after <= 6:
        return finish(P, nc, out, ar)

    P.barrier()
    ar.reset()
    w_up_bf = ar.alloc([8, 2 * DFF], BF16, "w_up_bf")
    wub = [Buf("wup%d" % i) for i in range(4)]
    for pi in (0, 1, 2, 3):
        c0 = pi * 1408
        P.dma("sp", w_up_bf.ap[:, :, c0:c0 + 1408], wup_s[:, :, c0:c0 + 1408], writes=[wub[pi]])
    hxf = [ar.alloc([8, 512], BF16, "hxf%d" % i) for i in range(2)]
    hal = [ar.alloc([8, 2], BF16, "hal%d" % i) for i in range(2)]
    hg = ar.alloc([NFC, 2], F32, "hg")
    fT = ar.alloc([NFC, 512], BF16, "fT")
    ut = [ar.alloc([512], BF16, "ut%d" % i) for i in range(3)]
    gstg = [ar.alloc([514], F32, "gstg%d" % i) for i in range(3)]
    facc = [ar.alloc([512], F32, "facc%d" % i) for i in range(3)]
    wdt = [ar.alloc([D], BF16, "wdt%d" % i) for i in range(3)]
    dacc = [ar.alloc([2, D], F32, "dacc%d" % i) for i in range(2)]
    x1t = [ar.alloc([D], F32, "x1f%d" % i) for i in range(2)]
    yst = [ar.alloc([4], F32, "ystf%d" % i) for i in range(4)]
    junkf = ar.alloc([512], BF16, "junkff")
    psacc = [psf[1], psf[2], psf[3], psf[4]]
    cnt3 = {"f": 0, "w": 0, "e": 0, "p": 0}

    def epilogue(t, pair, da):
        for s2 in range(2):
            sub = pair * 2 + s2
            r0 = t * 512 + sub * 128
            cnt3["e"] += 1
            x1_, ys_ = x1t[cnt3["e"] % 2], yst[cnt3["e"] % 4]
            dv_ = da.ap[:, s2, :]
            P.dma("pool", x1_.ap, x1_s[r0:r0 + 128, :], writes=[x1_])
            op("act", lambda e, ys_=ys_: e.memzero(ys_.ap), writes=[ys_])
            for half in range(2):
                hs = slice(half * 512, (half + 1) * 512)
                op("act", lambda e, half=half, hs=hs, ys_=ys_, dv_=dv_, jk=junkf: e.activation(
                    out=jk.ap, in_=dv_[:, hs], func=AF.Square, accum_out=ys_.ap[:, half:half + 1]),
                   reads=[da, ys_], writes=[junkf, ys_])
            yield
            op("pool", lambda e, ys_=ys_: e.tensor_tensor(out=ys_.ap[:, 2:3], in0=ys_.ap[:, 0:1], in1=ys_.ap[:, 1:2], op=ALU.add),
               reads=[ys_], writes=[ys_])
            op("act", lambda e, ys_=ys_: e.activation(out=ys_.ap[:, 3:4], in_=ys_.ap[:, 2:3], func=AF.Sqrt, scale=1.0 / D, bias=eps1.ap),
               reads=[ys_, eps1], writes=[ys_])
            op("dve", lambda e, ys_=ys_: e.reciprocal(out=ys_.ap[:, 3:4], in_=ys_.ap[:, 3:4]), reads=[ys_], writes=[ys_])
            yield
            op("dve", lambda e, ys_=ys_, dv_=dv_: e.scalar_tensor_tensor(
                out=dv_, in0=dv_, scalar=ys_.ap[:, 3:4], in1=gt2g.ap, op0=ALU.mult, op1=ALU.mult),
               reads=[da, ys_, gt2g], writes=[da])
            yield
            op("pool", lambda e, x1_=x1_, dv_=dv_: e.tensor_tensor(out=dv_, in0=dv_, in1=x1_.ap, op=ALU.add),
               reads=[da, x1_], writes=[da])
            P.dma("pool", out[r0:r0 + 128, :], dv_, reads=[da])
            yield

    fTb = [Buf("fT%d" % j) for j in range(NFC)]

    def down_chunk(pair, j):
        cnt3["w"] += 1
        wt = wdt[cnt3["w"] % 3]
        P.dma("sp", wt.ap, wd_s[j], writes=[wt])
        for a in range(4):
            sub, half = pair * 2 + a // 2, a % 2
            acc_ = psacc[a]

            def fn(e, acc_=acc_, wt=wt, j=j, sub=sub, half=half):
                return e.matmul(acc_.ap, lhsT=fT.ap[:, j, sub * 128:(sub + 1) * 128], rhs=wt.ap[:, half * 512:(half + 1) * 512],
                                start=(j == 0), stop=(j == NFC - 1))
            op("pe", fn, reads=[fTb[j], wt] + ([acc_] if j > 0 else []), writes=[acc_])

    def evac_pair(pair):
        cnt3["p"] += 1
        da = dacc[cnt3["p"] % 2]
        for a in range(4):
            s2, half = a // 2, a % 2
            op("act", lambda e, a=a, s2=s2, half=half, da=da: e.copy(out=da.ap[:, s2, half * 512:(half + 1) * 512], in_=psacc[a].ap),
               reads=[psacc[a]], writes=[da])
        pending.append(epilogue(cur_t[0], pair, da))

    cur_t = [0]
    pending = []

    def pump(n=1):
        for _ in range(n):
            if not pending:
                return
            try:
                next(pending[0])
            except StopIteration:
                pending.pop(0)

    P.dma("sp", hxf[0].ap, h2v[:, :, 0:512], writes=[hxf[0]])
    for t in range(T // 512):
        tsl = slice(t * 512, (t + 1) * 512)
        cur_t[0] = t
        hx = hxf[t % 2]
        hl = hal[t % 2]
        if t + 1 < T // 512:
            P.dma("sp", hxf[(t + 1) % 2].ap, h2v[:, :, (t + 1) * 512:(t + 2) * 512], writes=[hxf[(t + 1) % 2]])
        op("pool", lambda e, hl=hl: e.memset(hl.ap, 0.0), writes=[hl])
        if t > 0:
            op("pool", lambda e, hl=hl, t=t: e.tensor_copy(out=hl.ap[:, :, 0:1], in_=halp.ap[:, t - 1, :, 1:2]), reads=[halp], writes=[hl])
        if t < T // 512 - 1:
            op("pool", lambda e, hl=hl, t=t: e.tensor_copy(out=hl.ap[:, :, 1:2], in_=halp.ap[:, t + 1, :, 0:1]), reads=[halp], writes=[hl])
        ph = psf[0]
        mm(ph, None, [(ph.ap[:, j * 2:j * 2 + 2], [(w_up_bf.ap[:, kc, j * 128:(j + 1) * 128], hl.ap[:, kc, :]) for kc in range(8)])
                      for j in range(NFC)], [wub[0], wub[1], hl])
        op("dve", lambda e: e.tensor_copy(out=hg.ap, in_=ph.ap[:, 0:2 * NFC].rearrange("p (j c) -> p j c", c=2)),
           reads=[ph], writes=[hg])
        def ffn_tail(j, gs, fa, u_):
            op("dve", lambda e: e.tensor_scalar(out=fa.ap, in0=gs.ap[:, 0:512], scalar1=cwf.ap[:, j:j + 1],
                                                scalar2=None, op0=ALU.mult), reads=[gs, cwf], writes=[fa])
            for tap in (1, 2):
                op("dve", lambda e, tap=tap: e.scalar_tensor_tensor(
                    out=fa.ap, in0=gs.ap[:, tap:tap + 512], scalar=cwf.ap[:, tap * NFC + j:tap * NFC + j + 1], in1=fa.ap,
                    op0=ALU.mult, op1=ALU.add), reads=[gs, cwf, fa], writes=[fa])
            op("act", lambda e: e.activation(out=fa.ap, in_=fa.ap, func=AF.Silu), reads=[fa], writes=[fa])
            op("dve", lambda e: e.tensor_tensor(out=fT.ap[:, j, :], in0=fa.ap, in1=u_.ap, op=ALU.mult),
               reads=[fa, u_], writes=[fTb[j]])

        prevf = None
        dq_ = []
        for j in list(range(NFC)) + [None]:
            if j is not None:
                cnt3["f"] += 1
                fi = cnt3["f"]
                gs, fa, u_ = gstg[fi % 3], facc[fi % 3], ut[fi % 3]
                pg = nps_ffn()
                mm(pg, pg.ap, [(None, [(w_up_bf.ap[:, kc, j * 128:(j + 1) * 128], hx.ap[:, kc, :]) for kc in range(8)])],
                   [wub[(j * 128) // 1408], hx])
                op("act", lambda e, gs=gs, pg=pg: e.copy(out=gs.ap[:, 1:513], in_=pg.ap), reads=[pg], writes=[gs])
                pu = nps_ffn()
                mm(pu, pu.ap, [(None, [(w_up_bf.ap[:, kc, DFF + j * 128:DFF + (j + 1) * 128], hx.ap[:, kc, :]) for kc in range(8)])],
                   [wub[(DFF + j * 128) // 1408], hx])
                op("act", lambda e, u_=u_, pu=pu: e.copy(out=u_.ap, in_=pu.ap), reads=[pu], writes=[u_])
                op("pool", lambda e, gs=gs, j=j: e.tensor_copy(out=gs.ap[:, 0:1], in_=hg.ap[:, j, 0:1]), reads=[hg], writes=[gs])
                op("pool", lambda e, gs=gs, j=j: e.tensor_copy(out=gs.ap[:, 513:514], in_=hg.ap[:, j, 1:2]), reads=[hg], writes=[gs])
            if prevf is not None:
                ffn_tail(*prevf)
                dq_.append(prevf[0])
            if len(dq_) > 2:
                down_chunk(0, dq_.pop(0))
            prevf = (j, gs, fa, u_) if j is not None else None
            pump(1)
        while dq_:
            down_chunk(0, dq_.pop(0))
        evac_pair(0)
        while pending:
            pump(1)
        for j in range(NFC):
            down_chunk(1, j)
        evac_pair(1)
    while pending:
        pump(1)

    return finish(P, nc, out, ar)


def finish(P, nc, out, ar):
    P.barrier()
    if DEBUG:
        d2, Gm, Bt, Gc, Bc, S = ar.dbg
        P.dma("sp", d2[:, 0:256], Gm.ap.rearrange("p a b -> p (a b)"), reads=[Gm])
        P.dma("sp", d2[:, 256:512], Bt.ap.rearrange("p a b -> p (a b)"), reads=[Bt])
        P.dma("sp", d2[:, 512:528], Gc.ap.rearrange("p a b -> p (a b)"), reads=[Gc])
        P.dma("sp", d2[:, 528:544], Bc.ap.rearrange("p a b -> p (a b)"), reads=[Bc])
        P.dma("sp", d2[:, 1024:2048], S.ap.rearrange("p a b -> p (a b)"), reads=[S])
        P.barrier()
    P.emit()
    return nc


_CACHE = {}


def kernel(**inputs):
    n = 8
    if "nc" not in _CACHE:
        _CACHE["nc"] = build_program()
    nc = _CACHE["nc"]
    f32 = np.float32
    shared = {}
    for k in ("w_mod", "b_mod", "g_pre_mix", "g_post_mix", "g_pre_ffn", "g_post_ffn", "w_in", "conv_qkv",
              "a_log", "dt_bias", "o_norm", "pool_w", "pool_scale", "w_out", "w_up", "conv_ffn", "w_down"):
        a = np.ascontiguousarray(np.asarray(inputs[k], dtype=f32)[0])
        if k in ("a_log", "dt_bias"):
            a = a.reshape(8)
        shared[k] = a
    x = np.asarray(inputs["x"], dtype=f32)
    ctx = np.asarray(inputs["ctx"], dtype=f32)
    c = np.asarray(inputs["c"], dtype=f32)
    c_ctx = np.asarray(inputs["c_ctx"], dtype=f32)
    in_maps = []
    for b in range(n):
        m = dict(shared)
        m["x"] = np.ascontiguousarray(x[b])
        m["ctx"] = np.ascontiguousarray(ctx[b])
        m["cc"] = np.ascontiguousarray(np.stack([c[b], c_ctx], axis=0))
        in_maps.append(m)
    res = run_bass_kernel_spmd(nc, in_maps, core_ids=list(range(n)))
    _CACHE["res"] = res
    return np.stack([np.asarray(r["out"], dtype=f32) for r in res.results], axis=0)
```
